# Optimizing a Trainium2 kernel written in Bass

```python
import math
import jax, jax.numpy as jnp
from jax import lax
import numpy as np

D_MODEL = 1024
BATCH = 1
SEQ = 16384
DEPTH = 1
DEC_BATCH = 2
DEC_SEQ = 8192
PAST_LEN = 128

N_META = 16
CHUNK = 128
LEAD_PAD = CHUNK - N_META
EPS = 1e-6
MLA_HEADS = 8
Q_LORA = 256
KV_LORA = 128
QK_NOPE = 64
QK_ROPE = 32
V_HEAD = 64
QK_DIM = QK_NOPE + QK_ROPE
MLA_WIDTH = MLA_HEADS * V_HEAD
ROPE_THETA = 10000.0
M_HEADS = 4
M_HEAD_DIM = 128
M_WIDTH = M_HEADS * M_HEAD_DIM
CONV_W = 3
D_FF = -((-8 * D_MODEL) // (3 * 256)) * 256
SEG = (Q_LORA, KV_LORA, QK_ROPE, M_WIDTH, M_WIDTH, M_WIDTH, M_WIDTH, 4 * M_HEADS, D_MODEL, D_MODEL)
SPLIT_IDX = tuple(sum(SEG[:i + 1]) for i in range(len(SEG) - 1))
D_IN = sum(SEG)
OFF_MGATES = sum(SEG[:7])

kernel_name = "hybrid_mla_mlstm_encoder"


def rmsnorm(x, g):
    xf = x.astype(jnp.float32)
    y = xf * lax.rsqrt(jnp.mean(xf * xf, axis=-1, keepdims=True) + EPS)
    return (y * g.astype(jnp.float32)).astype(x.dtype)


def rope(x):
    T = x.shape[1]
    half = QK_ROPE // 2
    freqs = ROPE_THETA ** (-jnp.arange(half, dtype=jnp.float32) / half)
    ang = jnp.arange(T, dtype=jnp.float32)[:, None] * freqs[None, :]
    shape = (1, T) + (1,) * (x.ndim - 3) + (half,)
    cos = jnp.cos(ang).reshape(shape)
    sin = jnp.sin(ang).reshape(shape)
    xf = x.astype(jnp.float32)
    x1, x2 = xf[..., :half], xf[..., half:]
    return jnp.concatenate([x1 * cos - x2 * sin, x2 * cos + x1 * sin], axis=-1).astype(x.dtype)


def mla_attention(c_q, c_kv, k_r, q_norm_g, kv_norm_g, w_uq, w_ukv):
    B, T, _ = c_q.shape
    q = (rmsnorm(c_q, q_norm_g) @ w_uq).reshape(B, T, MLA_HEADS, QK_DIM)
    q = jnp.concatenate([q[..., :QK_NOPE], rope(q[..., QK_NOPE:])], axis=-1) * (QK_DIM ** -0.5)
    kv = (rmsnorm(c_kv, kv_norm_g) @ w_ukv).reshape(B, T, MLA_HEADS, QK_NOPE + V_HEAD)
    k_nope, v = kv[..., :QK_NOPE], kv[..., QK_NOPE:]
    k_rope = jnp.broadcast_to(rope(k_r)[:, :, None, :], (B, T, MLA_HEADS, QK_ROPE))
    k = jnp.concatenate([k_nope, k_rope], axis=-1)
    q = jnp.pad(q, ((0, 0), (LEAD_PAD, 0), (0, 0), (0, 0)))
    nb = (T + LEAD_PAD) // CHUNK
    qb = q.reshape(B, nb, CHUNK, MLA_HEADS, QK_DIM).transpose(1, 0, 2, 3, 4)

    def block(qi):
        s = jnp.einsum('bqhd,bkhd->bhqk', qi, k, preferred_element_type=jnp.float32)
        p = jax.nn.softmax(s, axis=-1).astype(v.dtype)
        return jnp.einsum('bhqk,bkhd->bqhd', p, v)

    o = lax.map(block, qb)
    o = o.transpose(1, 0, 2, 3, 4).reshape(B, T + LEAD_PAD, MLA_WIDTH)
    return o[:, LEAD_PAD:]


def mlstm_chunkwise(q, k, v, a, b):
    F = jnp.cumsum(a, axis=-1)
    FL = F[..., -1]
    g = FL[..., None] - F + b

    def step(carry, xs):
        C, n, m = carry
        k_c, v_c, g_c, FL_c = xs
        m_new = jnp.maximum(FL_c + m, jnp.max(g_c, axis=-1))
        decay = jnp.exp(FL_c + m - m_new)
        w = jnp.exp(g_c - m_new[..., None])
        C_new = decay[..., None, None] * C + jnp.einsum('bhs,bhsd,bhse->bhde', w, v_c, k_c)
        n_new = decay[..., None] * n + jnp.einsum('bhs,bhse->bhe', w, k_c)
        return (C_new, n_new, m_new), (C, n, m)

    B, NH, NC, L, dh = q.shape
    init = (jnp.zeros((B, NH, dh, dh), jnp.float32), jnp.zeros((B, NH, dh), jnp.float32),
            jnp.zeros((B, NH), jnp.float32))
    xs = (k.transpose(2, 0, 1, 3, 4), v.transpose(2, 0, 1, 3, 4), g.transpose(2, 0, 1, 3), FL.transpose(2, 0, 1))
    _, (Cs, ns, ms) = lax.scan(step, init, xs)
    Cs = Cs.transpose(1, 2, 0, 3, 4)
    ns = ns.transpose(1, 2, 0, 3)
    ms = ms.transpose(1, 2, 0)
    causal = jnp.tril(jnp.ones((L, L), dtype=bool))
    logD = jnp.where(causal, F[..., :, None] - F[..., None, :] + b[..., None, :], -jnp.inf)
    inter = F + ms[..., None]
    m = jnp.maximum(inter, jnp.max(logD, axis=-1))
    S = jnp.einsum('bhcjd,bhcsd->bhcjs', q, k) * jnp.exp(logD - m[..., None])
    wi = jnp.exp(inter - m)
    num = wi[..., None] * jnp.einsum('bhcde,bhcje->bhcjd', Cs, q) + jnp.einsum('bhcjs,bhcsd->bhcjd', S, v)
    den = wi * jnp.einsum('bhce,bhcje->bhcj', ns, q) + jnp.sum(S, axis=-1)
    return num / jnp.maximum(jnp.abs(den), jnp.exp(-m))[..., None]


def mlstm_branch(mq, mk, mv, mo, mgates, conv_w, conv_b, m_norm_g):
    B, T, _ = mq.shape
    dtype = mq.dtype
    qk = jnp.concatenate([mq, mk], axis=-1)
    qk = lax.conv_general_dilated(qk, conv_w[:, None, :].astype(qk.dtype), window_strides=(1,),
                                  padding=[(CONV_W // 2, CONV_W // 2)],
                                  dimension_numbers=('NWC', 'WIO', 'NWC'),
                                  feature_group_count=2 * M_WIDTH) + conv_b
    qk = jax.nn.silu(qk).astype(jnp.float32)
    q = qk[..., :M_WIDTH]
    k = qk[..., M_WIDTH:] * (M_HEAD_DIM ** -0.5)
    v = mv.astype(jnp.float32)
    gts = mgates.astype(jnp.float32)
    i_f, f_f, i_b, f_b = (gts[..., j * M_HEADS:(j + 1) * M_HEADS] for j in range(4))
    Tp = T + LEAD_PAD
    NC = Tp // CHUNK
    padt = lambda x: jnp.pad(x, ((0, 0), (LEAD_PAD, 0)) + ((0, 0),) * (x.ndim - 2))
    q, k, v = (padt(t).reshape(B, Tp, M_HEADS, M_HEAD_DIM) for t in (q, k, v))
    i_f, f_f, i_b, f_b = (padt(t) for t in (i_f, f_f, i_b, f_b))
    valid = (jnp.arange(Tp) >= LEAD_PAD)[None, :, None]

    def run(q, k, v, ig, fg, valid):
        a = jnp.where(valid, jax.nn.log_sigmoid(fg), 0.0)
        b = jnp.where(valid, ig, -jnp.inf)
        ch = lambda x: x.reshape(B, NC, CHUNK, M_HEADS, M_HEAD_DIM).transpose(0, 3, 1, 2, 4)
        cg = lambda x: x.reshape(B, NC, CHUNK, M_HEADS).transpose(0, 3, 1, 2)
        h = mlstm_chunkwise(ch(q), ch(k), ch(v), cg(a), cg(b))
        return h.transpose(0, 2, 3, 1, 4).reshape(B, Tp, M_HEADS, M_HEAD_DIM)

    h_f = run(q, k, v, i_f, f_f, valid)
    fl = lambda x: jnp.flip(x, axis=1)
    h_b = fl(run(fl(q), fl(k), fl(v), fl(i_b), fl(f_b), fl(valid)))
    h = (h_f + h_b)[:, LEAD_PAD:]
    h = h * jax.nn.sigmoid(mo.astype(jnp.float32)).reshape(B, T, M_HEADS, M_HEAD_DIM)
    h = rmsnorm(h, m_norm_g.reshape(M_HEADS, M_HEAD_DIM))
    return h.reshape(B, T, M_WIDTH).astype(dtype)


def encoder_layer(x, norm1_g, w_in, b_in, conv_w, conv_b, q_norm_g, kv_norm_g, w_uq, w_ukv,
                  m_norm_g, w_pa, w_pb, w_o, norm2_g, w_ffn_gate, w_ffn_up, w_ffn_down):
    xn = rmsnorm(x, norm1_g)
    proj = xn @ w_in + b_in
    c_q, c_kv, k_r, mq, mk, mv, mo, mgates, g_a, g_b = jnp.split(proj, SPLIT_IDX, axis=-1)
    a_out = mla_attention(c_q, c_kv, k_r, q_norm_g, kv_norm_g, w_uq, w_ukv)
    m_out = mlstm_branch(mq, mk, mv, mo, mgates, conv_w, conv_b, m_norm_g)
    merged = jax.nn.sigmoid(g_a) * (a_out @ w_pa) + jax.nn.sigmoid(g_b) * (m_out @ w_pb)
    x = x + merged @ w_o
    xn2 = rmsnorm(x, norm2_g)
    return x + (jax.nn.silu(xn2 @ w_ffn_gate) * (xn2 @ w_ffn_up)) @ w_ffn_down


def trunk(x, meta_tokens, norm1_g, w_in, b_in, conv_w, conv_b, q_norm_g, kv_norm_g, w_uq, w_ukv,
          m_norm_g, w_pa, w_pb, w_o, norm2_g, w_ffn_gate, w_ffn_up, w_ffn_down, final_norm_g):
    B = x.shape[0]
    meta = jnp.broadcast_to(meta_tokens[None].astype(x.dtype), (B, N_META, D_MODEL))
    h = jnp.concatenate([meta, x], axis=1)
    for l in range(DEPTH):
        h = encoder_layer(h, norm1_g[l], w_in[l], b_in[l], conv_w[l], conv_b[l], q_norm_g[l], kv_norm_g[l],
                          w_uq[l], w_ukv[l], m_norm_g[l], w_pa[l], w_pb[l], w_o[l], norm2_g[l],
                          w_ffn_gate[l], w_ffn_up[l], w_ffn_down[l])
    h = rmsnorm(h, final_norm_g)
    return h[:, N_META:]


def setup_inputs(seed: int = 0) -> dict:
    key = jax.random.key(seed)
    ks = jax.random.split(key, 24)
    nrm = lambda k, shape, fan: jax.random.normal(k, shape, jnp.float32) * (fan ** -0.5)
    gain = lambda k, shape: 1.0 + 0.02 * jax.random.normal(k, shape, jnp.float32)
    b_in = 0.02 * jax.random.normal(ks[4], (DEPTH, D_IN), jnp.float32)
    f_bias = jax.random.uniform(ks[5], (DEPTH, 2, M_HEADS), jnp.float32, 3.0, 6.0)
    b_in = b_in.at[:, OFF_MGATES + M_HEADS:OFF_MGATES + 2 * M_HEADS].add(f_bias[:, 0])
    b_in = b_in.at[:, OFF_MGATES + 3 * M_HEADS:OFF_MGATES + 4 * M_HEADS].add(f_bias[:, 1])
    return {
        "x_prompt": jax.random.normal(ks[0], (BATCH, SEQ, D_MODEL), jnp.float32),
        "x_sample": jax.random.normal(ks[1], (DEC_BATCH, DEC_SEQ, D_MODEL), jnp.float32),
        "meta_tokens": jax.random.normal(ks[2], (N_META, D_MODEL), jnp.float32),
        "norm1_g": gain(ks[3], (DEPTH, D_MODEL)),
        "w_in": nrm(ks[6], (DEPTH, D_MODEL, D_IN), D_MODEL),
        "b_in": b_in,
        "conv_w": nrm(ks[7], (DEPTH, CONV_W, 2 * M_WIDTH), CONV_W),
        "conv_b": 0.02 * jax.random.normal(ks[8], (DEPTH, 2 * M_WIDTH), jnp.float32),
        "q_norm_g": gain(ks[9], (DEPTH, Q_LORA)),
        "kv_norm_g": gain(ks[10], (DEPTH, KV_LORA)),
        "w_uq": nrm(ks[11], (DEPTH, Q_LORA, MLA_HEADS * QK_DIM), Q_LORA),
        "w_ukv": nrm(ks[12], (DEPTH, KV_LORA, MLA_HEADS * (QK_NOPE + V_HEAD)), KV_LORA),
        "m_norm_g": gain(ks[13], (DEPTH, M_WIDTH)),
        "w_pa": nrm(ks[14], (DEPTH, MLA_WIDTH, D_MODEL), MLA_WIDTH),
        "w_pb": nrm(ks[15], (DEPTH, M_WIDTH, D_MODEL), M_WIDTH),
        "w_o": nrm(ks[16], (DEPTH, D_MODEL, D_MODEL), D_MODEL),
        "norm2_g": gain(ks[17], (DEPTH, D_MODEL)),
        "w_ffn_gate": nrm(ks[18], (DEPTH, D_MODEL, D_FF), D_MODEL),
        "w_ffn_up": nrm(ks[19], (DEPTH, D_MODEL, D_FF), D_MODEL),
        "w_ffn_down": nrm(ks[20], (DEPTH, D_FF, D_MODEL), D_FF),
        "final_norm_g": gain(ks[21], (D_MODEL,)),
    }


def reference(x_prompt, x_sample, meta_tokens, norm1_g, w_in, b_in, conv_w, conv_b, q_norm_g, kv_norm_g,
              w_uq, w_ukv, m_norm_g, w_pa, w_pb, w_o, norm2_g, w_ffn_gate, w_ffn_up, w_ffn_down, final_norm_g):
    y_prompt = trunk(x_prompt, meta_tokens, norm1_g, w_in, b_in, conv_w, conv_b, q_norm_g, kv_norm_g, w_uq,
                     w_ukv, m_norm_g, w_pa, w_pb, w_o, norm2_g, w_ffn_gate, w_ffn_up, w_ffn_down, final_norm_g)
    y_sample = trunk(x_sample, meta_tokens, norm1_g, w_in, b_in, conv_w, conv_b, q_norm_g, kv_norm_g, w_uq,
                     w_ukv, m_norm_g, w_pa, w_pb, w_o, norm2_g, w_ffn_gate, w_ffn_up, w_ffn_down, final_norm_g)
    return (y_prompt, y_sample)
```

```python
import numpy as np
from contextlib import ExitStack
import concourse.bass as bass
import concourse.mybir as mybir
from concourse.bass_utils import run_bass_kernel_spmd

F32 = mybir.dt.float32
BF = mybir.dt.bfloat16
AF = mybir.ActivationFunctionType
ALU = mybir.AluOpType
AX = mybir.AxisListType

NCORE = 8
D = 1024
P = 128
NSUP = (32, 16, 16)
NLOC = (4, 2, 2)
NSEQ = 3
QK_NOPE, QK_ROPE, V_HEAD, NH = 64, 32, 64, 8
QK_DIM = 96
MH, MD = 4, 128
D_FF = 2816
EPS = 1e-6
SM_SCALE = QK_DIM ** -0.5
K_SCALE = MD ** -0.5
META_LO, META_HI = 111, 127
NA = 128 + 64 + 512 + 512 + 16
NL_ = 256 + 512 + 512
DEBUG = False


def TK(s):
    return (1 + 4 * NSUP[s]) * 128


class Sched:
    LIMIT = 20000
    R = 24
    DMAQ = ('sp',)

    def __init__(self):
        self.ops = []
        self.lw = {}
        self.rd = {}
        self.bar = None

    def _last_ops(self):
        last = {}
        dl = {}
        for i, o in enumerate(self.ops):
            if o[0] in self.DMAQ:
                dl.setdefault(o[0], []).append(i)
            else:
                last[o[0]] = i
        s = set(last.values())
        for e, l in dl.items():
            s.update(l[-self.R:])
        return s

    def barrier(self):
        self.bar = (self._last_ops(), set())
        self.lw = {}
        self.rd = {}

    def op(self, eng, fn, r=(), w=()):
        i = len(self.ops)
        hard, soft = set(), set()
        if self.bar is not None and eng not in self.bar[1]:
            hard.update(self.bar[0])
            self.bar[1].add(eng)
        isd = eng in self.DMAQ
        for k in r:
            hard.update(self.lw.get(k, ()))
        for k in w:
            hard.update(self.lw.get(k, ()))
            for kk, v in self.rd.get(k, {}).items():
                if isinstance(v, list):
                    soft.update(v)
                else:
                    soft.add(v)
        self.ops.append([eng, fn, hard, soft])
        for k in r:
            d = self.rd.setdefault(k, {})
            if isd:
                d.setdefault(('dma', eng), []).append(i)
            else:
                d[eng] = i
        for k in w:
            if isd and not self.rd.get(k) and k in self.lw and all(self.ops[x][0] in self.DMAQ for x in self.lw[k]):
                self.lw[k] = set(self.lw[k]) | {i}
            else:
                self.lw[k] = {i}
            self.rd[k] = {}
        return i

    def emit(self, nc, stack):
        ops = self.ops
        n = len(ops)
        dmaq = self.DMAQ
        need = [False] * n
        deps = [None] * n
        for i, (eng, fn, hard, soft) in enumerate(ops):
            dl = {}
            dd = set()
            for d, is_hard in [(x, True) for x in hard] + [(x, False) for x in soft]:
                if d >= i:
                    continue
                e2 = ops[d][0]
                if e2 in dmaq:
                    dd.add(d)
                    continue
                if e2 == eng:
                    if eng == 'pe' or not is_hard:
                        continue
                if e2 not in dl or dl[e2] < d:
                    dl[e2] = d
            deps[i] = (dl, sorted(dd))
            for d in dl.values():
                need[d] = True
        sig = [None] * n
        cnt, ep = {}, {}
        dcount = {}
        for i, o in enumerate(ops):
            e = o[0]
            if e in dmaq:
                j = dcount.get(e, 0)
                dcount[e] = j + 1
                sig[i] = (e, 'd%d' % (j % self.R), j // self.R + 1)
                continue
            if not need[i]:
                continue
            c = cnt.get(e, 0) + 1
            if c > self.LIMIT:
                ep[e] = ep.get(e, 0) + 1
                c = 1
            cnt[e] = c
            sig[i] = (e, ep.get(e, 0), c)
        sems = {}
        for s in sig:
            if s is not None and (s[0], s[1]) not in sems:
                sems[(s[0], s[1])] = stack.enter_context(nc.semaphore("s_%s_%s" % (s[0], s[1])))
        per_eng = {}
        for i, o in enumerate(ops):
            per_eng.setdefault(o[0], []).append(i)
        final = self._last_ops()

        def run(eng_name, handle):
            waited = {}

            def wait_for(d):
                s = sig[d]
                key = (s[0], s[1])
                if waited.get(key, 0) >= s[2]:
                    return
                handle.wait_ge(sems[key], s[2] * (16 if s[0] in dmaq else 1))
                waited[key] = s[2]

            for i in per_eng.get(eng_name, []):
                dl, dd = deps[i]
                for e2, d in dl.items():
                    wait_for(d)
                for d in dd:
                    wait_for(d)
                if eng_name in dmaq:
                    s = sig[i]
                    if s[2] > 1:
                        key = (s[0], s[1])
                        if waited.get(key, 0) < s[2] - 1:
                            handle.wait_ge(sems[key], (s[2] - 1) * 16)
                            waited[key] = s[2] - 1
                ins = ops[i][1](handle)
                if sig[i] is not None:
                    s = sig[i]
                    ins.then_inc(sems[(s[0], s[1])], 16 if eng_name in dmaq else 1)
            if eng_name in dmaq:
                for d in sorted(final):
                    if ops[d][0] in dmaq:
                        wait_for(d)

        with nc.Block() as block:
            @block.tensor
            def _(t):
                run('pe', t)

            @block.scalar
            def _(t):
                run('act', t)

            @block.vector
            def _(t):
                run('dve', t)

            @block.gpsimd
            def _(t):
                run('pool', t)

            @block.sync
            def _(t):
                run('sp', t)


def dap(t, off, pat):
    return bass.AP(t, off, [list(p) for p in pat])


def build(stage=99):
    nc = bass.Bass("TRN2", target_bir_lowering=False)
    S = Sched()
    st = ExitStack()
    di = {}

    def din(name, shape, dt=F32):
        di[name] = nc.dram_tensor(name, list(shape), dt, kind="ExternalInput")
        return di[name]

    def dscr(name, shape, dt=BF):
        return nc.dram_tensor(name, list(shape), dt, kind="Internal")

    def dout(name, shape, dt=F32):
        return nc.dram_tensor(name, list(shape), dt, kind="ExternalOutput")

    def sb(name, shape, dt=F32):
        return st.enter_context(nc.sbuf_tensor("sb_" + name, list(shape), dt))

    xin = [din("xin%d" % s, [128 + NSUP[s] * 512, D]) for s in range(NSEQ)]
    xloc = [din("xloc%d" % s, [NLOC[s] * 512, D]) for s in range(NSEQ)]
    cosd = [din("cos%d" % s, [32, TK(s)]) for s in range(NSEQ)]
    sind = [din("sin%d" % s, [32, TK(s)]) for s in range(NSEQ)]
    mskd = [din("msk%d" % s, [128, (1 + NSUP[s]) * 4]) for s in range(NSEQ)]
    cst = din("cst", [128, 6 * 128])
    vmeta_d = din("vmeta", [128, 1])
    w_a_d = din("w_a", [D, NA]); b_a_d = din("b_a", [1, NA])
    w_l_d = din("w_l", [D, NL_]); b_l_d = din("b_l", [1, NL_])
    w_g_d = din("w_g", [D, 2048]); b_g_d = din("b_g", [1, 2048])
    w_uq_d = din("w_uq", [256, 1024])
    w_ukv_d = din("w_ukv", [128, 1024])
    w_pa_d = din("w_pa", [512, D]); w_pb_d = din("w_pb", [512, D]); w_o_d = din("w_o", [D, D])
    w_gu_d = din("w_gu", [D, 2 * D_FF]); w_dn_d = din("w_dn", [D_FF, D])
    g1_d = din("g1", [128, 8]); g2_d = din("g2", [128, 8])
    gq_d = din("gq", [128, 2]); gkv_d = din("gkv", [128, 1]); gm_d = din("gm", [128, 4])
    gfin_d = din("gfin", [1, D])
    cw_d = din("cw", [128, 8 * 3]); cb_d = din("cb", [128, 8])

    y = [dout("y%d" % s, [NLOC[s] * 512, D]) for s in range(NSEQ)]

    KN = [dscr("kn%d" % s, [NH * 64, TK(s)]) for s in range(NSEQ)]
    KR = [dscr("kr%d" % s, [33, TK(s)]) for s in range(NSEQ)]
    VS = [dscr("vs%d" % s, [NH, 128, TK(s) // 128, 65]) for s in range(NSEQ)]
    QT = [dscr("qt%d" % s, [NH, 97, NLOC[s] * 512]) for s in range(NSEQ)]
    LQ = [dscr("lq%d" % s, [MH * 128, NLOC[s] * 512]) for s in range(NSEQ)]
    LK = [dscr("lk%d" % s, [MH * 128, NLOC[s] * 512]) for s in range(NSEQ)]
    LKT = [dscr("lkt%d" % s, [128, NLOC[s] * 4, 512]) for s in range(NSEQ)]
    LV = [dscr("lv%d" % s, [128, NLOC[s] * 4, MH * 129]) for s in range(NSEQ)]
    LG = [dscr("lg%d" % s, [128, NLOC[s] * 4, 16], F32) for s in range(NSEQ)]
    LMO = [dscr("lmo%d" % s, [128, NLOC[s] * 4, 512]) for s in range(NSEQ)]
    AOT = [dscr("aot%d" % s, [512, NLOC[s] * 512]) for s in range(NSEQ)]
    RIV = [dscr("riv%d" % s, [NH, NLOC[s] * 512], F32) for s in range(NSEQ)]
    MOT = [dscr("mot%d" % s, [512, NLOC[s] * 512]) for s in range(NSEQ)]
    H1 = [dscr("h1_%d" % s, [NLOC[s] * 512, D], F32) for s in range(NSEQ)]
    X2T = [dscr("x2t%d" % s, [D, NLOC[s] * 512]) for s in range(NSEQ)]
    HT = [dscr("ht%d" % s, [D_FF, NLOC[s] * 512]) for s in range(NSEQ)]
    dbg = {}
    if DEBUG:
        dbg['cf'] = dout("dbg_cf", [NSEQ, 128, MH * 129])
        dbg['bb'] = dout("dbg_bb", [NSEQ, 128, MH * 129])
        dbg['kn'] = dout("dbg_kn", [NH * 64, TK(1)], BF)
        dbg['kr'] = dout("dbg_kr", [33, TK(1)], BF)
        dbg['vs'] = dout("dbg_vs", [NH, 128, TK(1) // 128, 65], BF)
        dbg['qt'] = dout("dbg_qt", [NH, 97, NLOC[1] * 512], BF)
        dbg['lq'] = dout("dbg_lq", [MH * 128, NLOC[1] * 512], BF)
        dbg['lk'] = dout("dbg_lk", [MH * 128, NLOC[1] * 512], BF)
        dbg['aot'] = dout("dbg_aot", [512, NLOC[1] * 512], BF)
        dbg['mot'] = dout("dbg_mot", [512, NLOC[1] * 512], BF)

    C = sb("cst", [128, 6 * 128])
    Cb = sb("cstb", [128, 6 * 128], BF)
    ident_b = Cb[:, 0:128]
    ones_b = Cb[:, 3 * 128:4 * 128]
    triU_b = Cb[:, 128:256]; triL_b = Cb[:, 256:384]
    triU = C[:, 128:256]; triL = C[:, 256:384]; onesF = C[:, 384:512]
    maskF = C[:, 512:640]; maskB = C[:, 640:768]
    vmeta = sb("vmeta", [128, 1])
    epst = sb("epst", [128, 1]); onet = sb("onet", [128, 1]); lnks = sb("lnks", [128, 1])
    g1 = sb("g1", [128, 8]); g2 = sb("g2", [128, 8]); gq = sb("gq", [128, 2]); gkv = sb("gkv", [128, 1])
    gm = sb("gm", [128, 4])
    cw = sb("cw", [128, 24]); cb = sb("cb", [128, 8])
    WA = sb("WA", [128, 8, NA], BF)
    WL = sb("WL", [128, 8, NL_], BF)
    WUQ = sb("WUQ", [128, 2, 1024], BF)
    WUKV = sb("WUKV", [128, 1024], BF)
    bA = sb("bA", [128, 2])
    bAm = sb("bAm", [128, 4])
    bLq = sb("bLq", [128, 2 + 4])
    bbc = sb("bbc", [128, 512 + 16 + 512])
    stage_t = sb("stage", [128, 1024])
    bG_t = sb("bG", [128, 16])
    S4_t = sb("S4", [128, 16])
    BIG = sb("BIG", [128, 34 * 1024 + 3072])

    def carve(off_words, shape, dt):
        n = int(np.prod(shape[1:]))
        words = n if dt == F32 else (n + 1) // 2
        v = BIG[0:shape[0], off_words:off_words + words]
        if dt == BF:
            v = v.bitcast(BF)[:, 0:n]
        if len(shape) == 3:
            v = v.rearrange("p (a b) -> p a b", a=shape[1])
        elif len(shape) == 4:
            v = v.rearrange("p (a b c) -> p a b c", a=shape[1], b=shape[2])
        return v, off_words + words

    PSA = st.enter_context(nc.psum_tensor("psa", [128, 8 * 512], F32))
    PSAb = PSA.bitcast(BF)

    class _Bank:
        def __init__(self, i):
            self.i = i

        def __getitem__(self, idx):
            p, c = idx
            c0 = 0 if c.start is None else c.start
            c1 = 512 if c.stop is None else c.stop
            return PSA[p, self.i * 512 + c0:self.i * 512 + c1]

        def bitcast(self, dt):
            b = self

            class _B:
                def __getitem__(self, idx):
                    p, c = idx
                    c0 = 0 if c.start is None else c.start
                    c1 = 1024 if c.stop is None else c.stop
                    return PSAb[p, b.i * 1024 + c0:b.i * 1024 + c1]
            return _B()
    PS = [_Bank(i) for i in range(8)]
    psn = [0]

    def psum():
        i = psn[0] % 8
        psn[0] += 1
        return PS[i], ('ps', i)

    def dma(out, in_, r, w, q='sp'):
        S.op(q, lambda e: e.dma_start(out=out, in_=in_, allow_slow_non_contiguous=True), r, w)

    def act(out, in_, func, r, w, bias=None, scale=None, accum=None):
        kw = {}
        if bias is not None:
            kw['bias'] = bias
        if scale is not None:
            kw['scale'] = scale
        if accum is not None:
            kw['accum_out'] = accum
        S.op('act', lambda e: e.activation(out, in_, func, **kw), r, w)

    def ts(eng, out, in0, s1, s2, op0, op1, r, w):
        if op1 is None:
            S.op(eng, lambda e: e.tensor_scalar(out, in0, s1, None, op0), r, w)
        else:
            S.op(eng, lambda e: e.tensor_scalar(out, in0, s1, s2, op0, op1), r, w)

    def tt(eng, out, in0, in1, op, r, w):
        S.op(eng, lambda e: e.tensor_tensor(out, in0, in1, op), r, w)

    def stt(out, in0, sc, in1, op0, op1, r, w):
        S.op('dve', lambda e: e.scalar_tensor_tensor(out, in0, sc, in1, op0, op1), r, w)

    def cp(eng, out, in_, r, w):
        if eng == 'act':
            S.op('act', lambda e: e.copy(out, in_), r, w)
        else:
            S.op(eng, lambda e: e.tensor_copy(out, in_), r, w)

    def mm(out, lhsT, rhs, start, stop, r, w):
        S.op('pe', lambda e: e.matmul(out, lhsT, rhs, start=start, stop=stop), r, w)

    def tr(out, in_, r, w):
        S.op('pe', lambda e: e.transpose(out, in_, ident_b), r, w)

    S.op('pool', lambda e: e.memset(BIG[:, :], 0.0), [], ['BIGZ'])
    S.barrier()
    dma(C[:], cst.ap(), [], ['C'])
    cp('dve', Cb[:], C[:], ['C'], ['Cb'])
    dma(vmeta[:], vmeta_d.ap(), [], ['vmeta'])
    S.op('dve', lambda e: e.memset(epst[:], EPS), [], ['epst'])
    S.op('dve', lambda e: e.memset(onet[:], 1.0), [], ['onet'])
    S.op('dve', lambda e: e.memset(lnks[:], float(np.log(K_SCALE))), [], ['lnks'])
    for t_, d_, k_ in ((g1, g1_d, 'g1'), (g2, g2_d, 'g2'), (gq, gq_d, 'gq'), (gkv, gkv_d, 'gkv'), (gm, gm_d, 'gm'),
                       (cw, cw_d, 'cw'), (cb, cb_d, 'cb')):
        dma(t_[:], d_.ap(), [], [k_])
    dma(bA[:, 0:1], dap(b_a_d, 0, [[1, 128], [1, 1]]), [], ['bA'])
    dma(bA[0:64, 1:2], dap(b_a_d, 128, [[1, 64], [1, 1]]), [], ['bA'])
    dma(bAm[:], dap(b_a_d, 192, [[1, 128], [128, 4]]), [], ['bAm'])
    dma(bLq[:, 0:2], dap(b_l_d, 0, [[1, 128], [128, 2]]), [], ['bLq'])
    dma(bLq[:, 2:6], dap(b_l_d, 256, [[1, 128], [128, 4]]), [], ['bLq'])
    dma(bbc[:, 0:528], dap(b_a_d, 704, [[0, 128], [1, 528]]), [], ['bbc'])
    dma(bbc[:, 528:1040], dap(b_l_d, 768, [[0, 128], [1, 512]]), [], ['bbc'])

    def load_weight(dst, src, rows, cols, gain, gkey, wkey, engines=('dve', 'act'), jobs=None):
        nk = rows // 128

        def one_chunk(kt, c0, ci):
            cn = min(1024, cols - c0)
            dma(stage_t[:, 0:cn], dap(src, kt * 128 * cols + c0, [[cols, 128], [1, cn]]), [], ['stage'])
            o = dst[:, kt, c0:c0 + cn] if nk > 1 or len(dst.shape) == 3 else dst[:, c0:c0 + cn]
            eng = engines[ci % len(engines)]
            if gain is None:
                cp(eng, o, stage_t[:, 0:cn], ['stage'], [wkey])
            elif eng == 'dve':
                ts(eng, o, stage_t[:, 0:cn], gain[:, kt:kt + 1], None, ALU.mult, None,
                   ['stage', gkey], [wkey])
            else:
                act(o, stage_t[:, 0:cn], AF.Copy, ['stage', gkey], [wkey], scale=gain[:, kt:kt + 1])

        ci = 0
        for kt in range(nk):
            for c0 in range(0, cols, 1024):
                if jobs is not None:
                    jobs.append(lambda kt=kt, c0=c0, ci=ci: one_chunk(kt, c0, ci))
                else:
                    one_chunk(kt, c0, ci)
                ci += 1

    load_weight(WA, w_a_d, D, NA, g1, 'g1', 'WA')
    load_weight(WL, w_l_d, D, NL_, g1, 'g1', 'WL')
    load_weight(WUQ, w_uq_d, 256, 1024, gq, 'gq', 'WUQ')
    load_weight(WUKV, w_ukv_d, 128, 1024, gkv, 'gkv', 'WUKV')

    o = 0
    ST_, o = carve(o, [128, 2, 4 * 129], F32)
    O_PERSIST = o
    XT, o = carve(o, [128, 4, 1024], F32)
    XN, o = carve(o, [128, 2, 1024], BF)
    XNT, o = carve(o, [128, 2, 8, 512], BF)
    MKr, o = carve(o, [128, 3, 8, 514], BF)
    VR, o = carve(o, [128, 3, 4, 512], BF)
    GR, o = carve(o, [128, 3, 4, 16], F32)
    CKV, o = carve(o, [128, 2, 512], BF)
    SQ, o = carve(o, [128, 2, 512], BF)
    LNT, o = carve(o, [128, 512], F32)
    RBC, o = carve(o, [128, 512], F32)
    KNS, o = carve(o, [128, 4, 512], BF)
    KRS, o = carve(o, [33, 512], BF)
    KT1, o = carve(o, [32, 512], F32)
    KT2, o = carve(o, [32, 512], F32)
    VST, o = carve(o, [128, 4, 8, 65], BF)
    CSK, o = carve(o, [64, 2, 512], F32)
    SSQ, o = carve(o, [128, 8], F32)
    RST, o = carve(o, [128, 8], F32)
    CV, o = carve(o, [128, 2, 512], F32)
    KTb, o = carve(o, [128, 8, 512], BF)
    KTK, o = carve(o, [128, 4, 512], BF)
    WV, o = carve(o, [128, 4, 2, 4 * 129], BF)
    GT, o = carve(o, [128, 16, 16], F32)
    AHL, o = carve(o, [128, 2, 4, 8], BF)
    CQT, o = carve(o, [128, 2, 512], BF)
    QTS, o = carve(o, [97, 2, 512], BF)
    CSQ, o = carve(o, [128, 2, 512], F32)
    MOS, o = carve(o, [128, 2, 512], BF)
    VAL, o = carve(o, [128, 4, 4 * 129], BF)
    SM, o = carve(o, [128, 64], F32)
    MSK, o = carve(o, [128, 33 * 4], F32)
    MW, o = carve(o, [128, 8], F32)
    assert o <= 34 * 1024 + 3072, o


    def mlstm_local(s):
        nl = NLOC[s]
        nch = nl * 4
        nq = nl * 512
        o2 = O_PERSIST
        QTl, o2 = carve(o2, [128, 4, 512], BF)
        KTl, o2 = carve(o2, [128, 4, 512], BF)
        KKl, o2 = carve(o2, [128, 4, 512], BF)
        VAl, o2 = carve(o2, [128, 4, 516], BF)
        GRl, o2 = carve(o2, [128, 4, 16], F32)
        MOl, o2 = carve(o2, [128, 4, 512], BF)
        HF, o2 = carve(o2, [128, nch, 512], F32)
        G2, o2 = carve(o2, [128, 16, 16], F32)
        AH2, o2 = carve(o2, [128, 2, 4, 4], BF)
        SST, o2 = carve(o2, [128, 2, 128], BF)
        WVl, o2 = carve(o2, [128, 2, 130], BF)
        CBF, o2 = carve(o2, [128, 2, 4 * 130], BF)
        RC, o2 = carve(o2, [128, 8], F32)
        HG, o2 = carve(o2, [128, 512], F32)
        HSQ, o2 = carve(o2, [128, 128], F32)
        MOb, o2 = carve(o2, [128, 512], BF)
        MTs, o2 = carve(o2, [128, 4, 128], BF)
        assert o2 <= 34 * 1024 + 3072, o2

        def load_sup(i, with_mo=False):
            for h in range(4):
                dma(QTl[:, h, :], dap(LQ[s], h * 128 * nq + i * 512, [[nq, 128], [1, 512]]), [('LQ', s)], ['QTl'])
                dma(KTl[:, h, :], dap(LK[s], h * 128 * nq + i * 512, [[nq, 128], [1, 512]]), [('LK', s)], ['KTl'])
            dma(KKl[:, :, :].rearrange("p t c -> p (t c)"), dap(LKT[s], i * 4 * 512, [[nch * 512, 128], [1, 2048]]), [('LKT', s)], ['KKl'])
            dma(VAl[:, :, :].rearrange("p t c -> p (t c)"), dap(LV[s], i * 4 * 516, [[nch * 516, 128], [1, 4 * 516]]), [('LV', s)], ['VAl'])
            dma(GRl[:, :, :].rearrange("p t c -> p (t c)"), dap(LG[s], i * 64, [[nch * 16, 128], [1, 64]]), [('LG', s)], ['GRl'])

        for d_ in range(2):
            skey = 'Cf' if d_ == 0 else 'Bb'
            for h in range(4):
                cp('pool', CBF[:, d_, h * 130:h * 130 + 129], ST_[:, d_, h * 129:(h + 1) * 129], [skey], [('CBF', d_)])
            sups = range(nl) if d_ == 0 else range(nl - 1, -1, -1)
            for i in sups:
                load_sup(i)
                G4 = GRl[:, :, :].rearrange("p t (d j h) -> p t d j h", d=2, j=2)
                fpre = G4[:, :, d_, 1, :]; ipre = G4[:, :, d_, 0, :]
                T_ = G2[:, 0:4, 0:4]; A_ = G2[:, 0:4, 4:8]; CS = G2[:, 4:8, 0:8]
                ARG = G2[:, 0:4, 8:12]; DD = G2[:, 8:12, 0:4]; WW = G2[:, 8:12, 4:8]; EF = G2[:, 8:12, 8:12]
                FLO = G2[:, 8:12, 12:16]
                act(T_, fpre, AF.Exp, ['GRl'], ['T2'], scale=-1.0)
                act(T_, T_, AF.Ln, ['T2', 'onet'], ['T2'], bias=onet[:, 0:1])
                ts('dve', A_, T_, -1.0, None, ALU.mult, None, ['T2'], ['A2'])
                cp('dve', AH2[:, 0, :, :], A_, ['A2'], ['AH2'])
                tt('dve', AH2[:, 1, :, :], A_, AH2[:, 0, :, :], ALU.subtract, ['A2', 'AH2'], ['AL2'])
                ps, pk = psum()
                tri = triU_b if d_ == 0 else triL_b
                for t in range(4):
                    for j_, kk in ((0, 'AH2'), (1, 'AL2')):
                        mm(ps[:, t * 8:t * 8 + 4], tri, AH2[:, j_, t, :], j_ == 0, j_ == 1, ['Cb', kk], [pk])
                    for j_, kk in ((0, 'AH2'), (1, 'AL2')):
                        mm(ps[:, t * 8 + 4:t * 8 + 8], ones_b, AH2[:, j_, t, :], j_ == 0, j_ == 1, ['Cb', kk], [pk])
                cp('dve', CS, ps[:, 0:32].rearrange("p (t c) -> p t c", t=4), [pk], ['CS2'])
                tt('dve', ARG, ipre, CS[:, :, 0:4], ALU.subtract, ['GRl', 'CS2'], ['ARG2'])
                act(DD, ARG, AF.Exp, ['ARG2', 'lnks'], ['DD'], bias=lnks[:, 0:1])
                tt('dve', ARG, ARG, CS[:, :, 4:8], ALU.add, ['ARG2', 'CS2'], ['ARG2'])
                act(WW, ARG, AF.Exp, ['ARG2', 'lnks'], ['WW'], bias=lnks[:, 0:1])
                act(EF, CS[:, :, 4:8], AF.Exp, ['CS2'], ['EF'])
                act(FLO, CS[:, :, 0:4], AF.Exp, ['CS2'], ['FLO'], scale=-1.0)
                chunks = range(4) if d_ == 0 else range(3, -1, -1)
                msk = maskF if d_ == 0 else maskB
                for t in chunks:
                    c = i * 4 + t
                    for h in range(4):
                        sl = h % 2
                        ps1, pk1 = psum()
                        mm(ps1[:, 0:128], KTl[:, h, t * 128:(t + 1) * 128], QTl[:, h, t * 128:(t + 1) * 128], True, True,
                           ['KTl', 'QTl'], [pk1])
                        stt(SST[:, sl, :], ps1[:, 0:128], DD[:, t, h:h + 1], msk, ALU.mult, ALU.mult, [pk1, 'DD', 'C'], [('SST', sl)])
                        ps2, pk2 = psum()
                        mm(ps2[:, 0:129], QTl[:, h, t * 128:(t + 1) * 128], CBF[:, d_, h * 130:h * 130 + 129], True, False,
                           ['QTl', ('CBF', d_)], [pk2])
                        mm(ps2[:, 0:129], SST[:, sl, :], VAl[:, t, h * 129:(h + 1) * 129], False, True, [('SST', sl), 'VAl'], [pk2])
                        ts('dve', RC[:, sl:sl + 1], ps2[:, 128:129], FLO[:, t, h:h + 1], None, ALU.max, None, [pk2, 'FLO'], [('RC', sl)])
                        stt(RC[:, sl:sl + 1], ps2[:, 128:129], -1.0, RC[:, sl:sl + 1], ALU.mult, ALU.max, [pk2, ('RC', sl)], [('RC', sl)])
                        S.op('dve', lambda e, sl=sl: e.reciprocal(RC[:, sl:sl + 1], RC[:, sl:sl + 1]), [('RC', sl)], [('RC', sl)])
                        hdst = HF[:, c, h * 128:(h + 1) * 128]
                        if d_ == 0:
                            ts('dve', hdst, ps2[:, 0:128], RC[:, sl:sl + 1], None, ALU.mult, None, [pk2, ('RC', sl)], [('HF', c)])
                        else:
                            stt(hdst, ps2[:, 0:128], RC[:, sl:sl + 1], hdst, ALU.mult, ALU.add, [pk2, ('RC', sl), ('HF', c)], [('HF', c)])
                        ts('dve', WVl[:, sl, 0:129], VAl[:, t, h * 129:(h + 1) * 129], WW[:, t, h:h + 1], None, ALU.mult, None,
                           ['VAl', 'WW'], [('WVl', sl)])
                        ps3, pk3 = psum()
                        mm(ps3[:, 0:129], KKl[:, t, h * 128:(h + 1) * 128], WVl[:, sl, 0:129], True, True, ['KKl', ('WVl', sl)], [pk3])
                        stt(ST_[:, d_, h * 129:(h + 1) * 129], ST_[:, d_, h * 129:(h + 1) * 129], EF[:, t, h:h + 1], ps3[:, 0:129],
                            ALU.mult, ALU.add, [skey, 'EF', pk3], [skey])
                        cp('pool', CBF[:, d_, h * 130:h * 130 + 129], ST_[:, d_, h * 129:(h + 1) * 129], [skey], [('CBF', d_)])
        for i in range(nl):
            dma(MOl[:, :, :].rearrange("p t c -> p (t c)"), dap(LMO[s], i * 4 * 512, [[nch * 512, 128], [1, 2048]]), [('LMO', s)], ['MOl'])
            for t in range(4):
                c = i * 4 + t
                tt('dve', HG[:, :], HF[:, c, :], MOl[:, t, :], ALU.mult, [('HF', c), 'MOl'], ['HG'])
                for h in range(4):
                    act(HSQ[:, :], HG[:, h * 128:(h + 1) * 128], AF.Square, ['HG'], ['HSQ', 'RC2'], accum=RC[:, 4 + h:5 + h])
                act(RC[:, 4:8], RC[:, 4:8], AF.Sqrt, ['RC2', 'epst'], ['RC2'], bias=epst[:, 0:1], scale=1.0 / 128)
                S.op('dve', lambda e: e.reciprocal(RC[:, 4:8], RC[:, 4:8]), ['RC2'], ['RC2'])
                tt('dve', MOb[:, :].rearrange("p (h d) -> p h d", h=4), HG[:, :].rearrange("p (h d) -> p h d", h=4),
                   RC[:, 4:8].unsqueeze(2).to_broadcast([128, 4, 128]), ALU.mult, ['HG', 'RC2'], ['MOb'])
                ps, pk = psum()
                psb = ps.bitcast(BF)
                for h in range(4):
                    tr(psb[:, h * 128:(h + 1) * 128], MOb[:, h * 128:(h + 1) * 128], ['MOb', 'Cb'], [pk])
                cp('act', MTs[:, :, :], psb[:, 0:512].rearrange("p (h t) -> p h t", h=4), [pk], ['MTs'])
                dma(dap(MOT[s], c * 128, [[nq, 128], [128 * nq, 4], [1, 128]]), MTs[:, :, :], ['MTs'], [('MOT', s)])
        if DEBUG and s == 1:
            dma(dbg['mot'].ap(), MOT[s].ap(), [('MOT', s)], ['dbg8'])

    ARENA_END = 34 * 1024 + 3072
    O_B4W = ARENA_END - 16384

    def b4a_weights():
        o3 = O_B4W
        WG_, o3 = carve(o3, [128, 8, 2048], BF)
        WPA_, o3 = carve(o3, [128, 4, 1024], BF)
        WPB_, o3 = carve(o3, [128, 4, 1024], BF)
        WO_, o3 = carve(o3, [128, 8, 1024], BF)
        assert o3 <= ARENA_END
        return WG_, WPA_, WPB_, WO_

    b4a_jobs = []

    def prefetch_b4a_weights():
        WG_, WPA_, WPB_, WO_ = b4a_weights()
        load_weight(WG_, w_g_d, D, 2048, g1, 'g1', 'WG', engines=('dve',), jobs=b4a_jobs)
        load_weight(WPA_, w_pa_d, 512, D, None, None, 'WPA', engines=('dve',), jobs=b4a_jobs)
        load_weight(WPB_, w_pb_d, 512, D, gm, 'gm', 'WPB', engines=('dve',), jobs=b4a_jobs)
        load_weight(WO_, w_o_d, D, D, None, None, 'WO', engines=('dve',), jobs=b4a_jobs)
        b4a_jobs.append(lambda: dma(bG_t[:, :], dap(b_g_d, 0, [[1, 128], [128, 16]]), [], ['bG']))

    def attention(s):
        nq = NLOC[s] * 512
        nb = TK(s) // 128
        nqb = nq // 512
        o2 = O_PERSIST
        KTt, o2 = carve(o2, [97, TK(s)], BF)
        Vh, o2 = carve(o2, [128, nb, 65], BF)
        if o2 % 2:
            o2 += 1
        QTh, o2 = carve(o2, [97, nq], BF)
        PT, o2 = carve(o2, [128, 2, 1536], BF)
        RR, o2 = carve(o2, [128, 2, 512], F32)
        AO, o2 = carve(o2, [64, 2, 512], BF)
        assert o2 <= O_B4W, o2
        NSEG = 8
        segb = [(nb * g) // NSEG for g in range(NSEG + 1)]
        seg_of = {}
        for g in range(NSEG):
            for kb in range(segb[g], segb[g + 1]):
                seg_of[kb] = g
        for g in range(NSEG):
            c0, c1 = segb[g] * 128, segb[g + 1] * 128
            dma(KTt[64:97, c0:c1], dap(KR[s], c0, [[TK(s), 33], [1, c1 - c0]]), [('KR', s)], [('KTr', g)])
        for h in range(NH):
            for g in range(NSEG):
                c0, c1 = segb[g] * 128, segb[g + 1] * 128
                dma(KTt[0:64, c0:c1], dap(KN[s], h * 64 * TK(s) + c0, [[TK(s), 64], [1, c1 - c0]]), [('KN', s)], [('KTn', g)])
                b0, b1 = segb[g], segb[g + 1]
                dma(Vh[:, b0:b1, :], dap(VS[s], (h * 128 * nb + b0) * 65, [[nb * 65, 128], [65, b1 - b0], [1, 65]]),
                    [('VS', s)], [('Vh', g)])
            dma(QTh[:, :], dap(QT[s], h * 97 * nq, [[nq, 97], [1, nq]]), [('QT', s)], ['QTh'])
            if s == 0 and h == 1:
                prefetch_b4a_weights()
            for qb in range(nqb):
                pob = 6 + (qb % 2)
                po = PS[pob]; pok = ('ps', pob)
                groups = [(kb0, min(3, nb - kb0)) for kb0 in range(0, nb, 3)]

                def emit_S(gi):
                    kb0, n2 = groups[gi]
                    sb0 = (gi % 2) * 3
                    for j in range(n2):
                        kb = kb0 + j
                        g = seg_of[kb]
                        mm(PS[sb0 + j][:, 0:512], KTt[:, kb * 128:(kb + 1) * 128], QTh[:, qb * 512:(qb + 1) * 512], True, True,
                           [('KTr', g), ('KTn', g), 'QTh'], [('ps', sb0 + j)])
                    act(PT[:, gi % 2, 0:n2 * 512], PSA[:, sb0 * 512:(sb0 + n2) * 512], AF.Exp,
                        [('ps', sb0 + j) for j in range(n2)], [('PT', gi % 2)], scale=SM_SCALE)

                def emit_PV(gi):
                    kb0, n2 = groups[gi]
                    for j in range(n2):
                        kb = kb0 + j
                        g = seg_of[kb]
                        mm(po[0:65, 0:512], Vh[:, kb, :], PT[:, gi % 2, j * 512:(j + 1) * 512], kb == 0, kb == nb - 1,
                           [('Vh', g), ('PT', gi % 2)], [pok])

                emit_S(0)
                for gi in range(len(groups)):
                    if gi + 1 < len(groups):
                        emit_S(gi + 1)
                    emit_PV(gi)
                asl = qb % 2
                S.op('dve', lambda e, po=po, asl=asl: e.reciprocal(RR[64:65, asl, :], po[64:65, 0:512]), [pok], [('RR', asl)])
                cp('dve', AO[:, asl, :], po[0:64, 0:512], [pok], [('AO', asl)])
                dma(dap(AOT[s], h * 64 * nq + qb * 512, [[nq, 64], [1, 512]]), AO[:, asl, :], [('AO', asl)], [('AOT', s)])
                dma(dap(RIV[s], h * nq + qb * 512, [[nq, 1], [1, 512]]), RR[64:65, asl, :], [('RR', asl)], [('RIV', s)])
                for _ in range(4):
                    if b4a_jobs:
                        b4a_jobs.pop(0)()
        if DEBUG and s == 1:
            dma(dbg['aot'].ap(), AOT[s].ap(), [('AOT', s)], ['dbg7'])


    def ffn_phases():
        o2 = O_PERSIST
        WG, WPA, WPB, WO = b4a_weights()
        XT4, o2 = carve(o2, [128, 4, 1024], F32)
        XN4, o2 = carve(o2, [128, 2, 1024], BF)
        XNT4, o2 = carve(o2, [128, 8, 512], BF)
        AOl, o2 = carve(o2, [128, 4, 512], BF)
        RBl, o2 = carve(o2, [128, 4, 512], F32)
        MOl4, o2 = carve(o2, [128, 4, 512], BF)
        MRG, o2 = carve(o2, [128, 8, 512], BF)
        H1t, o2 = carve(o2, [128, 1, 1024], F32)
        X2N, o2 = carve(o2, [128, 2, 1024], BF)
        X2Ts, o2 = carve(o2, [128, 1, 8, 128], BF)
        SGA, o2 = carve(o2, [128, 512], F32)
        SGB, o2 = carve(o2, [128, 512], F32)
        T1, o2 = carve(o2, [128, 512], F32)
        assert o2 <= O_B4W, o2
        assert not b4a_jobs
        bG = bG_t
        S4 = S4_t
        assert o2 <= 34 * 1024 + 3072, o2
        for s in range(NSEQ):
            nq = NLOC[s] * 512
            for i in range(NLOC[s]):
                for t in range(4):
                    dma(XT4[:, t, :], dap(xloc[s], (i * 512 + t * 128) * D, [[D, 128], [1, D]]), [], [('XT4', t)])
                    act(XN4[:, t % 2, :], XT4[:, t, :], AF.Square, [('XT4', t)], [('XN4', t % 2), ('S4', t)], accum=S4[:, t:t + 1])
                    act(S4[:, 4 + t:5 + t], S4[:, t:t + 1], AF.Sqrt, [('S4', t), 'epst'], [('S4b', t)], bias=epst[:, 0:1], scale=1.0 / D)
                    S.op('dve', lambda e, t=t: e.reciprocal(S4[:, 4 + t:5 + t], S4[:, 4 + t:5 + t]), [('S4b', t)], [('S4b', t)])
                    ts('dve', XN4[:, t % 2, :], XT4[:, t, :], S4[:, 4 + t:5 + t], None, ALU.mult, None,
                       [('XT4', t), ('S4b', t)], [('XN4', t % 2)])
                    ps, pk = psum()
                    psb = ps.bitcast(BF)
                    for kt in range(8):
                        tr(psb[:, kt * 128:(kt + 1) * 128], XN4[:, t % 2, kt * 128:(kt + 1) * 128], [('XN4', t % 2), 'Cb'], [pk])
                    cp('act', XNT4[:, :, t * 128:(t + 1) * 128], psb[:, 0:1024].rearrange("p (k t) -> p k t", k=8), [pk], ['XNT4'])
                for j in range(4):
                    dma(AOl[:, j, :], dap(AOT[s], j * 128 * nq + i * 512, [[nq, 128], [1, 512]]), [('AOT', s)], ['AOl'])
                    for hh in range(2):
                        dma(RBl[hh * 64:(hh + 1) * 64, j, :], dap(RIV[s], (2 * j + hh) * nq + i * 512, [[0, 64], [1, 512]]),
                            [('RIV', s)], ['RBl'])
                    dma(MOl4[:, j, :], dap(MOT[s], j * 128 * nq + i * 512, [[nq, 128], [1, 512]]), [('MOT', s)], ['MOl4'])
                tt('dve', AOl[:, :, :], AOl[:, :, :], RBl[:, :, :], ALU.mult, ['AOl', 'RBl'], ['AOl'])
                for c in range(8):
                    pa, pka = psum()
                    for k in range(4):
                        mm(pa[:, 0:512], WPA[:, k, c * 128:(c + 1) * 128], AOl[:, k, :], k == 0, k == 3, ['WPA', 'AOl'], [pka])
                    pb, pkb = psum()
                    for k in range(4):
                        mm(pb[:, 0:512], WPB[:, k, c * 128:(c + 1) * 128], MOl4[:, k, :], k == 0, k == 3, ['WPB', 'MOl4'], [pkb])
                    ga, pkga = psum()
                    for k in range(8):
                        mm(ga[:, 0:512], WG[:, k, c * 128:(c + 1) * 128], XNT4[:, k, :], k == 0, k == 7, ['WG', 'XNT4'], [pkga])
                    gb, pkgb = psum()
                    for k in range(8):
                        mm(gb[:, 0:512], WG[:, k, 1024 + c * 128:1024 + (c + 1) * 128], XNT4[:, k, :], k == 0, k == 7, ['WG', 'XNT4'], [pkgb])
                    act(SGA[:, :], ga[:, 0:512], AF.Sigmoid, [pkga, 'bG'], ['SGA'], bias=bG[:, c:c + 1])
                    act(SGB[:, :], gb[:, 0:512], AF.Sigmoid, [pkgb, 'bG'], ['SGB'], bias=bG[:, 8 + c:9 + c])
                    tt('dve', T1[:, :], pa[:, 0:512], SGA[:, :], ALU.mult, [pka, 'SGA'], ['T1'])
                    tt('dve', SGB[:, :], pb[:, 0:512], SGB[:, :], ALU.mult, [pkb, 'SGB'], ['SGB'])
                    tt('pool', MRG[:, c, :], T1[:, :], SGB[:, :], ALU.add, ['T1', 'SGB'], ['MRG'])
                for t in range(4):
                    hs = t % 2
                    for half in range(2):
                        ps, pk = psum()
                        for k in range(8):
                            mm(ps[:, 0:512], MRG[:, k, t * 128:(t + 1) * 128], WO[:, k, half * 512:(half + 1) * 512], k == 0, k == 7,
                               ['MRG', 'WO'], [pk])
                        tt('dve', H1t[:, 0, half * 512:(half + 1) * 512], ps[:, 0:512], XT4[:, t, half * 512:(half + 1) * 512], ALU.add,
                           [pk, ('XT4', t)], ['H1t'])
                    dma(dap(H1[s], (i * 512 + t * 128) * D, [[D, 128], [1, D]]), H1t[:, 0, :], ['H1t'], [('H1', s)])
                    act(X2N[:, hs, :], H1t[:, 0, :], AF.Square, ['H1t'], [('X2N', hs), ('S4c', hs)], accum=S4[:, 8 + hs:9 + hs])
                    act(S4[:, 10 + hs:11 + hs], S4[:, 8 + hs:9 + hs], AF.Sqrt, [('S4c', hs), 'epst'], [('S4d', hs)], bias=epst[:, 0:1], scale=1.0 / D)
                    S.op('dve', lambda e, hs=hs: e.reciprocal(S4[:, 10 + hs:11 + hs], S4[:, 10 + hs:11 + hs]), [('S4d', hs)], [('S4d', hs)])
                    ts('dve', X2N[:, hs, :], H1t[:, 0, :], S4[:, 10 + hs:11 + hs], None, ALU.mult, None, ['H1t', ('S4d', hs)], [('X2N', hs)])
                    ps, pk = psum()
                    psb = ps.bitcast(BF)
                    for kt in range(8):
                        tr(psb[:, kt * 128:(kt + 1) * 128], X2N[:, hs, kt * 128:(kt + 1) * 128], [('X2N', hs), 'Cb'], [pk])
                    cp('act', X2Ts[:, 0, :, :], psb[:, 0:1024].rearrange("p (k t) -> p k t", k=8), [pk], ['X2Ts'])
                    dma(dap(X2T[s], i * 512 + t * 128, [[nq, 128], [128 * nq, 8], [1, 128]]), X2Ts[:, 0, :, :], ['X2Ts'], [('X2T', s)])
        S.barrier()
        o2 = O_PERSIST
        WGU, o2 = carve(o2, [128, 8, 2 * D_FF], BF)
        X2l, o2 = carve(o2, [128, 2, 8, 512], BF)
        SIL, o2 = carve(o2, [128, 2, 512], F32)
        HTc, o2 = carve(o2, [128, 2, 512], BF)
        assert o2 <= 34 * 1024 + 3072, o2
        load_weight(WGU, w_gu_d, D, 2 * D_FF, g2, 'g2', 'WGU')
        it = 0
        for s in range(NSEQ):
            nq = NLOC[s] * 512
            for i in range(NLOC[s]):
                xsl = it % 2
                it += 1
                dma(X2l[:, xsl, :, :], dap(X2T[s], i * 512, [[nq, 128], [128 * nq, 8], [1, 512]]), [('X2T', s)], [('X2l', xsl)])
                for c in range(D_FF // 128):
                    sl = c % 2
                    pg, pkg = psum()
                    for k in range(8):
                        mm(pg[:, 0:512], WGU[:, k, c * 128:(c + 1) * 128], X2l[:, xsl, k, :], k == 0, k == 7, ['WGU', ('X2l', xsl)], [pkg])
                    pu, pku = psum()
                    for k in range(8):
                        mm(pu[:, 0:512], WGU[:, k, D_FF + c * 128:D_FF + (c + 1) * 128], X2l[:, xsl, k, :], k == 0, k == 7,
                           ['WGU', ('X2l', xsl)], [pku])
                    act(SIL[:, sl, :], pg[:, 0:512], AF.Silu, [pkg], [('SIL', sl)])
                    tt('dve', HTc[:, sl, :], pu[:, 0:512], SIL[:, sl, :], ALU.mult, [pku, ('SIL', sl)], [('HTc', sl)])
                    dma(dap(HT[s], c * 128 * nq + i * 512, [[nq, 128], [1, 512]]), HTc[:, sl, :], [('HTc', sl)], [('HT', s)])
        S.barrier()
        o2 = O_PERSIST
        WDN, o2 = carve(o2, [128, 22, 1024], BF)
        HTl, o2 = carve(o2, [128, 22, 512], BF)
        H1l, o2 = carve(o2, [128, 2, 1024], F32)
        YT, o2 = carve(o2, [128, 2, 1024], F32)
        YSQ, o2 = carve(o2, [128, 1024], F32)
        GF, o2 = carve(o2, [128, 1024], F32)
        S5, o2 = carve(o2, [128, 8], F32)
        assert o2 <= 34 * 1024 + 3072, o2
        load_weight(WDN, w_dn_d, D_FF, D, None, None, 'WDN')
        dma(GF[:, :], dap(gfin_d, 0, [[0, 128], [1, D]]), [], ['GF'])
        for s in range(NSEQ):
            nq = NLOC[s] * 512
            for i in range(NLOC[s]):
                dma(HTl[:, :, :], dap(HT[s], i * 512, [[nq, 128], [128 * nq, 22], [1, 512]]), [('HT', s)], ['HTl'])
                for t in range(4):
                    hs = t % 2
                    dma(H1l[:, hs, :], dap(H1[s], (i * 512 + t * 128) * D, [[D, 128], [1, D]]), [('H1', s)], [('H1l', hs)])
                    for half in range(2):
                        ps, pk = psum()
                        for k in range(22):
                            mm(ps[:, 0:512], HTl[:, k, t * 128:(t + 1) * 128], WDN[:, k, half * 512:(half + 1) * 512], k == 0, k == 21,
                               ['HTl', 'WDN'], [pk])
                        tt('dve', YT[:, hs, half * 512:(half + 1) * 512], ps[:, 0:512], H1l[:, hs, half * 512:(half + 1) * 512], ALU.add,
                           [pk, ('H1l', hs)], [('YT', hs)])
                    act(YSQ[:, :], YT[:, hs, :], AF.Square, [('YT', hs)], ['YSQ', ('S5', hs)], accum=S5[:, hs:hs + 1])
                    act(S5[:, 2 + hs:3 + hs], S5[:, hs:hs + 1], AF.Sqrt, [('S5', hs), 'epst'], [('S5b', hs)], bias=epst[:, 0:1], scale=1.0 / D)
                    S.op('dve', lambda e, hs=hs: e.reciprocal(S5[:, 2 + hs:3 + hs], S5[:, 2 + hs:3 + hs]), [('S5b', hs)], [('S5b', hs)])
                    stt(YT[:, hs, :], YT[:, hs, :], S5[:, 2 + hs:3 + hs], GF[:, :], ALU.mult, ALU.mult, [('YT', hs), ('S5b', hs), 'GF'], [('YT', hs)])
                    dma(dap(y[s], (i * 512 + t * 128) * D, [[D, 128], [1, D]]), YT[:, hs, :], [('YT', hs)], [('y', s)])

    kscale_ln = float(np.log(K_SCALE))

    for s in range(NSEQ):
        if stage < 1:
            break
        nsup = NSUP[s]
        nl = NLOC[s]
        nsteps = 1 + nsup
        dma(MSK[:, 0:nsteps * 4], mskd[s].ap(), [], ['MSK'])
        S.op('dve', lambda e: e.memset(ST_[:], 0.0), [], ['Cf', 'Bb'])
        S.op('dve', lambda e: e.memset(SM[:, 0:8], 0.0), [], ['Grun', 'Hrun'])
        S.op('pool', lambda e: e.memset(KRS[32:33, :], 1.0), [], ['KRS1'])

        def mcol(step, j):
            return MSK[:, step * 4 + j:step * 4 + j + 1]

        def load_x(step, what='both'):
            is_meta = (step == 0)
            ntt = 1 if is_meta else 4
            ntok = ntt * 128
            row0 = 0 if is_meta else 128 + (step - 1) * 512
            col0 = 0 if is_meta else (1 + 4 * (step - 1)) * 128
            xs = step % 2
            if what in ('x', 'both'):
                for t in range(ntt):
                    dma(XT[:, t, :], dap(xin[s], (row0 + t * 128) * D, [[D, 128], [1, D]]), [], [('XT', t)])
            if what in ('cs', 'both'):
                dma(CSK[0:32, xs, 0:ntok], dap(cosd[s], col0, [[TK(s), 32], [1, ntok]]), [], [('CSKc', xs)])
                dma(CSK[32:64, xs, 0:ntok], dap(sind[s], col0, [[TK(s), 32], [1, ntok]]), [], [('CSKs', xs)])

        def N_tile(step, t):
            xs = step % 2
            act(XN[:, t % 2, :], XT[:, t, :], AF.Square, [('XT', t)], [('XN', t % 2), ('SSQ', t)],
                accum=SSQ[:, t:t + 1])
            act(RST[:, t:t + 1], SSQ[:, t:t + 1], AF.Sqrt, [('SSQ', t), 'epst'], [('RST', t)],
                bias=epst[:, 0:1], scale=1.0 / D)
            S.op('dve', lambda e, t=t: e.reciprocal(RST[:, t:t + 1], RST[:, t:t + 1]),
                 [('RST', t)], [('RST', t)])
            if t % 2 == 0:
                ts('dve', XN[:, t % 2, :], XT[:, t, :], RST[:, t:t + 1], None,
                   ALU.mult, None, [('XT', t), ('RST', t)], [('XN', t % 2)])
            else:
                act(XN[:, t % 2, :], XT[:, t, :], AF.Copy, [('XT', t), ('RST', t)], [('XN', t % 2)], scale=RST[:, t:t + 1])
            ps, pk = psum()
            psb = ps.bitcast(BF)
            for kt in range(8):
                tr(psb[:, kt * 128:(kt + 1) * 128], XN[:, t % 2, kt * 128:(kt + 1) * 128], [('XN', t % 2), 'Cb'], [pk])
            cp('act', XNT[:, xs, :, t * 128:(t + 1) * 128],
               psb[:, 0:1024].rearrange("p (k t) -> p k t", k=8), [pk], [('XNT', xs)])

        pending = []

        def once(f):
            f()
            if False:
                yield

        def pump(n=1):
            for _ in range(n):
                if not pending:
                    return
                g = pending.pop(0)
                try:
                    next(g)
                    pending.append(g)
                except StopIteration:
                    pass

        def tail_gen(step):
            is_meta = (step == 0)
            ntt = 1 if is_meta else 4
            ntok = ntt * 128
            kb0 = 0 if is_meta else 1 + 4 * (step - 1)
            col0 = kb0 * 128
            xs = step % 2
            ps2, pk2 = psum()
            mm(ps2[:, 0:ntok], ones_b, SQ[:, xs, 0:ntok], True, True, [('SQ', xs), 'Cb'], [pk2])
            act(LNT[:, 0:ntok], ps2[:, 0:ntok], AF.Ln, [pk2, 'epst'], ['LNT'], bias=epst[:, 0:1], scale=1.0 / 128)
            act(RBC[:, 0:ntok], LNT[:, 0:ntok], AF.Exp, ['LNT'], ['RBC'], scale=-0.5)
            yield
            ps3, pk3 = psum()
            for t in range(ntt):
                mm(ps3[:, t:t + 1], SQ[:, xs, t * 128:(t + 1) * 128], ones_b[:, 0:1], True, True, [('SQ', xs), 'Cb'], [pk3])
            act(SSQ[:, 4:4 + ntt], ps3[:, 0:ntt], AF.Sqrt, [pk3, 'epst'], ['RSV'], bias=epst[:, 0:1], scale=1.0 / 128)
            S.op('dve', lambda e: e.reciprocal(RST[:, 4:4 + ntt], SSQ[:, 4:4 + ntt]), ['RSV'], ['RSV2'])
            yield
            for hp in range(4):
                ps, pk = psum()
                mm(ps[:, 0:ntok], WUKV[:, hp * 128:(hp + 1) * 128], CKV[:, xs, 0:ntok], True, True, ['WUKV', ('CKV', xs)], [pk])
                tt('dve', KNS[:, hp, 0:ntok], ps[:, 0:ntok], RBC[:, 0:ntok], ALU.mult, [pk, 'RBC'], [('KNS', hp)])
                dma(dap(KN[s], hp * 128 * TK(s) + col0, [[TK(s), 128], [1, ntok]]), KNS[:, hp, 0:ntok],
                    [('KNS', hp)], [('KN', s)])
                yield
            for t in range(ntt):
                ps, pk = psum()
                mm(ps[:, 0:512], CKV[:, xs, t * 128:(t + 1) * 128], WUKV[:, 512:1024], True, True, ['WUKV', ('CKV', xs)], [pk])
                if is_meta:
                    ts('dve', RST[:, 4:5], RST[:, 4:5], vmeta[:, 0:1], None, ALU.mult, None, ['RSV2', 'vmeta'], ['RSV2'])
                ts('dve', VST[:, t, :, 0:64], ps[:, 0:512].rearrange("p (h d) -> p h d", h=8), RST[:, 4 + t:5 + t],
                   None, ALU.mult, None, [pk, 'RSV2'], [('VST', t)])
                if is_meta:
                    cp('pool', VST[:, t, :, 64:65], vmeta[:, 0:1].unsqueeze(1).to_broadcast([128, 8, 1]), ['vmeta'], [('VST', t)])
                else:
                    S.op('pool', lambda e, t=t: e.memset(VST[:, t, :, 64:65], 1.0), [], [('VST', t)])
                yield
            for h in range(NH):
                nb = TK(s) // 128
                dma(dap(VS[s], (h * 128 * nb + kb0) * 65, [[nb * 65, 128], [65, ntt], [1, 65]]),
                    VST[:, 0:ntt, h, :], [('VST', t) for t in range(ntt)], [('VS', s)])
                if h % 2 == 1:
                    yield


        def A_step(step):
            is_meta = (step == 0)
            ntt = 1 if is_meta else 4
            ntok = ntt * 128
            row0 = 0 if is_meta else 128 + (step - 1) * 512
            kb0 = 0 if is_meta else 1 + 4 * (step - 1)
            col0 = kb0 * 128
            local = (1 <= step <= nl)
            lcol0 = (step - 1) * 512
            xs = step % 2
            rs = step % 3
            xk = ('XNT', xs)
            if step >= 1:
                pending.append(tail_gen(step - 1))
            if step >= 2:
                pending.append(M_step(step - 2))

            def proj_fm(c0, m, n0=0, nn=None):
                nn_ = ntok if nn is None else nn
                ps, pk = psum()
                for kt in range(8):
                    mm(ps[0:m, 0:nn_], WA[:, kt, c0:c0 + m], XNT[:, xs, kt, n0:n0 + nn_], kt == 0, kt == 7,
                       ['WA', xk], [pk])
                return ps, pk

            need_q = is_meta or local or step == nl + 1
            for h in range(8 if need_q else 4):
                if h < 4:
                    ps, pk = proj_fm(192 + h * 128, 128)
                    bias_ap = bAm[:, h:h + 1]; bk = 'bAm'
                else:
                    ps, pk = psum()
                    for kt in range(8):
                        mm(ps[:, 0:ntok], WL[:, kt, 256 + (h - 4) * 128:256 + (h - 3) * 128], XNT[:, xs, kt, 0:ntok],
                           kt == 0, kt == 7, ['WL', xk], [pk])
                    bias_ap = bLq[:, 2 + h - 4:3 + h - 4]; bk = 'bLq'
                act(MKr[:, rs, h, 1:1 + ntok], ps[:, 0:ntok], AF.Identity, [pk, bk], [('MK', rs)], bias=bias_ap)
                pump(2)
            nh_ = 8 if need_q else 4
            if is_meta:
                cp('dve', SM[:, 16:24], MKr[:, rs, :, 1], [('MK', rs)], ['pre'])
                cp('dve', SM[:, 8:16], MKr[:, rs, :, 1 + 126], [('MK', rs)], ['meta15'])
                S.op('pool', lambda e: e.memset(MKr[:, rs, :, 0:1 + META_LO], 0.0), ['pre'], [('MK', rs)])
            else:
                if step == 1:
                    cp('dve', MKr[:, rs, :, 0], SM[:, 16:24], ['pre'], [('MK', rs)])
                    cp('dve', SM[:, 24:32], MKr[:, rs, :, 1], [('MK', rs)], ['first'])
                else:
                    po = (step - 1) % 3
                    ts('dve', MW[:, 0:1], mcol(step, 2), -1.0, 1.0, ALU.mult, ALU.add, ['MSK'], ['MW0'])
                    ts('dve', GT[:, 0, 0:8], MKr[:, po, :, 512], mcol(step, 2), None, ALU.mult, None,
                       [('MK', po), 'MSK'], ['GT0'])
                    stt(MKr[:, rs, 0:nh_, 0], SM[:, 8:8 + nh_], MW[:, 0:1], GT[:, 0, 0:nh_], ALU.mult, ALU.add,
                        ['meta15', 'MW0', 'GT0'], [('MK', rs)])
            ps, pk = proj_fm(0, 128)
            act(CKV[:, xs, 0:ntok], ps[:, 0:ntok], AF.Identity, [pk, 'bA'], [('CKV', xs)], bias=bA[:, 0:1])
            act(SQ[:, xs, 0:ntok], ps[:, 0:ntok], AF.Square, [pk, 'bA'], [('SQ', xs)], bias=bA[:, 0:1])
            pump(3)
            ps, pk = proj_fm(128, 64)
            stt(KT1[:, 0:ntok], ps[0:32, 0:ntok], bA[0:32, 1:2], CSK[0:32, xs, 0:ntok], ALU.add, ALU.mult,
                [pk, 'bA', ('CSKc', xs)], ['KT1'])
            stt(KT2[:, 0:ntok], ps[32:64, 0:ntok], bA[32:64, 1:2], CSK[32:64, xs, 0:ntok], ALU.add, ALU.mult,
                [pk, 'bA', ('CSKs', xs)], ['KT2'])
            tt('dve', KRS[0:32, 0:ntok], KT1[:, 0:ntok], KT2[:, 0:ntok], ALU.add, ['KT1', 'KT2'], ['KRS'])
            dma(dap(KR[s], col0, [[TK(s), 33], [1, ntok]]), KRS[:, 0:ntok], ['KRS', 'KRS1'], [('KR', s)])
            if step + 2 < nsteps:
                load_x(step + 2, 'cs')
            pump(4)
            for t in range(ntt):
                ps, pk = psum()
                for kt in range(8):
                    mm(ps[:, 0:512], XNT[:, xs, kt, t * 128:(t + 1) * 128], WA[:, kt, 704:1216], kt == 0, kt == 7,
                       ['WA', xk], [pk])
                tt('dve', VR[:, rs, t, :], ps[:, 0:512], bbc[:, 0:512], ALU.add, [pk, 'bbc'], [('VR', rs)])
                pump(5)
            ps, pk = psum()
            for t in range(ntt):
                for kt in range(8):
                    mm(ps[:, t * 16:(t + 1) * 16], XNT[:, xs, kt, t * 128:(t + 1) * 128], WA[:, kt, 1216:1232],
                       kt == 0, kt == 7, ['WA', xk], [pk])
            tt('dve', GR[:, rs, 0:ntt, :], ps[:, 0:ntt * 16].rearrange("p (t g) -> p t g", t=ntt),
               bbc[:, 512:528].unsqueeze(1).to_broadcast([128, ntt, 16]), ALU.add, [pk, 'bbc'], [('GR', rs)])
            pump(5)
            while pending:
                pump()
            if local:
                dma(dap(LG[s], (step - 1) * 4 * 16, [[NLOC[s] * 4 * 16, 128], [1, 64]]),
                    GR[:, rs, :, :].rearrange("p t g -> p (t g)"), [('GR', rs)], [('LG', s)])
                A_local(step, xs, rs, xk, lcol0, col0)

        def A_local(step, xs, rs, xk, lcol0, col0):
            nq = NLOC[s] * 512
            ps2, pk2 = psum()
            for j in range(2):
                ps, pk = psum()
                for kt in range(8):
                    mm(ps[:, 0:512], WL[:, kt, j * 128:(j + 1) * 128], XNT[:, xs, kt, :], kt == 0, kt == 7, ['WL', xk], [pk])
                act(CQT[:, j, :], ps[:, 0:512], AF.Identity, [pk, 'bLq'], [('CQT', j)], bias=bLq[:, j:j + 1])
                act(SQ[:, 1 - xs, :], ps[:, 0:512], AF.Square, [pk, 'bLq'], [('SQ', 1 - xs)], bias=bLq[:, j:j + 1])
                mm(ps2[:, 0:512], ones_b, SQ[:, 1 - xs, :], j == 0, j == 1, [('SQ', 1 - xs), 'Cb'], [pk2])
            act(LNT[:, :], ps2[:, 0:512], AF.Ln, [pk2, 'epst'], ['LNT'], bias=epst[:, 0:1], scale=1.0 / 256)
            act(RBC[:, :], LNT[:, :], AF.Exp, ['LNT'], ['RBC'], scale=-0.5)
            dma(CSQ[64:96, 0, :], dap(cosd[s], col0, [[TK(s), 32], [1, 512]]), [], ['CSQc'])
            dma(CSQ[64:96, 1, :], dap(sind[s], col0, [[TK(s), 32], [1, 512]]), [], ['CSQs'])
            tt('pool', CSQ[64:96, 0, :], CSQ[64:96, 0, :], RBC[64:96, :], ALU.mult, ['CSQc', 'RBC'], ['CSQc'])
            tt('pool', CSQ[64:96, 1, :], CSQ[64:96, 1, :], RBC[64:96, :], ALU.mult, ['CSQs', 'RBC'], ['CSQs'])
            S.op('pool', lambda e: e.memset(QTS[96:97, :, :], 0.0), [], ['QTS96'])
            for h in range(NH):
                ps, pk = psum()
                for j in range(2):
                    mm(ps[:, 0:512], WUQ[:, j, h * 128:(h + 1) * 128], CQT[:, j, :], j == 0, j == 1,
                       ['WUQ', ('CQT', 0), ('CQT', 1)], [pk])
                tt('dve', QTS[0:64, h % 2, :], ps[0:64, 0:512], RBC[0:64, :], ALU.mult, [pk, 'RBC'], [('QTS', h % 2)])
                tt('dve', KT1[:, :], ps[64:96, 0:512], CSQ[64:96, 0, :], ALU.mult, [pk, 'CSQc'], ['KT1'])
                tt('dve', KT2[:, :], ps[96:128, 0:512], CSQ[64:96, 1, :], ALU.mult, [pk, 'CSQs'], ['KT2'])
                tt('pool', QTS[64:96, h % 2, :], KT1[:, :], KT2[:, :], ALU.add, ['KT1', 'KT2'], [('QTS', h % 2)])
                dma(dap(QT[s], h * 97 * nq + lcol0, [[nq, 97], [1, 512]]), QTS[:, h % 2, :], [('QTS', h % 2), 'QTS96'], [('QT', s)])
            for t in range(4):
                ps, pk = psum()
                for kt in range(8):
                    mm(ps[:, 0:512], XNT[:, xs, kt, t * 128:(t + 1) * 128], WL[:, kt, 768:1280], kt == 0, kt == 7,
                       ['WL', xk], [pk])
                tt('dve', CV[:, 0, :], ps[:, 0:512], bbc[:, 528:1040], ALU.add, [pk, 'bbc'], [('CV', 0)])
                act(MOS[:, t % 2, :], CV[:, 0, :], AF.Sigmoid, [('CV', 0)], [('MOS', t % 2)])
                dma(dap(LMO[s], ((step - 1) * 4 + t) * 512, [[NLOC[s] * 4 * 512, 128], [1, 512]]),
                    MOS[:, t % 2, :], [('MOS', t % 2)], [('LMO', s)])

        def M_step(step, deferred_meta=False):
            if False:
                yield
            is_meta = (step == 0)
            ntt = 1 if is_meta else 4
            ntok = ntt * 128
            xs = step % 3
            local = (1 <= step <= nl)
            nh_ = 8 if (is_meta or local) else 4
            lcol0 = (step - 1) * 512
            if not is_meta:
                if step == nsup:
                    src = SM[:, 24:24 + nh_]; sk = 'first'
                else:
                    src = MKr[:, (step + 1) % 3, 0:nh_, 1]; sk = ('MK', (step + 1) % 3)
                ts('dve', MKr[:, xs, 0:nh_, 513], src, mcol(step, 3), None, ALU.mult, None, [sk, 'MSK'], [('MK', xs)])
            if not is_meta or not deferred_meta:
                for h in range(nh_):
                    x0 = MKr[:, xs, h, 0:ntok]; x1 = MKr[:, xs, h, 1:1 + ntok]; x2 = MKr[:, xs, h, 2:2 + ntok]
                    acc = CV[:, h % 2, 0:ntok]
                    ck = ('CV', h % 2)
                    ts('dve', acc, x1, cw[:, h * 3 + 1:h * 3 + 2], cb[:, h:h + 1], ALU.mult, ALU.add,
                       [('MK', xs), 'cw', 'cb'], [ck])
                    stt(acc, x0, cw[:, h * 3:h * 3 + 1], acc, ALU.mult, ALU.add, [('MK', xs), 'cw', ck], [ck])
                    stt(acc, x2, cw[:, h * 3 + 2:h * 3 + 3], acc, ALU.mult, ALU.add, [('MK', xs), 'cw', ck], [ck])
                    act(KTb[:, h, 0:ntok], acc, AF.Silu, [ck], [('KTb', h)])
                    yield
                for t in range(ntt):
                    ps, pk = psum()
                    psb = ps.bitcast(BF)
                    for h in range(4):
                        tr(psb[:, h * 128:(h + 1) * 128], KTb[:, h, t * 128:(t + 1) * 128], [('KTb', h), 'Cb'], [pk])
                    kdst = KTK[:, t, :] if not is_meta else KTKm[:, :]
                    cp('act', kdst, psb[:, 0:512], [pk], [('KTK', t) if not is_meta else 'KTKm'])
                    yield
            if local:
                nq = NLOC[s] * 512
                for h in range(4):
                    dma(dap(LK[s], h * 128 * nq + lcol0, [[nq, 128], [1, 512]]), KTb[:, h, :], [('KTb', h)], [('LK', s)])
                    dma(dap(LQ[s], h * 128 * nq + lcol0, [[nq, 128], [1, 512]]), KTb[:, 4 + h, :], [('KTb', 4 + h)], [('LQ', s)])
                dma(dap(LKT[s], (step - 1) * 4 * 512, [[NLOC[s] * 4 * 512, 128], [1, 2048]]),
                    KTK[:, :, :].rearrange("p t c -> p (t c)"), [('KTK', t) for t in range(4)], [('LKT', s)])
                for t in range(4):
                    cp('pool', VAL[:, t, :].rearrange("p (h d) -> p h d", h=4)[:, :, 0:128],
                       VR[:, xs, t, :].rearrange("p (h d) -> p h d", h=4), [('VR', xs)], [('VAL', t)])
                    S.op('pool', lambda e, t=t: e.memset(VAL[:, t, :].rearrange("p (h d) -> p h d", h=4)[:, :, 128:129], 1.0),
                         [], [('VAL', t)])
                dma(dap(LV[s], (step - 1) * 4 * 516, [[NLOC[s] * 4 * 516, 128], [1, 4 * 516]]),
                    VAL[:, :, :].rearrange("p t c -> p (t c)"), [('VAL', t) for t in range(4)], [('LV', s)])
                return
            if is_meta and not deferred_meta:
                cp('pool', VRm[:, :], VR[:, xs, 0, :], [('VR', xs)], ['VRm'])
                cp('pool', GRm[:, :], GR[:, xs, 0, :], [('GR', xs)], ['GRm'])
                return
            if is_meta:
                G = GRm[:, :].unsqueeze(1)
                gk = 'GRm'
            else:
                G = GR[:, xs, :, :]
                gk = ('GR', xs)
            nt = ntt
            G4 = G.rearrange("p t (d j h) -> p t d j h", d=2, j=2)
            fpre = G4[:, :, :, 1, :]
            ipre = G4[:, :, :, 0, :]
            E1 = GT[:, 0, :].rearrange("p (a b) -> p a b", a=2)[:, :, :]
            A_ = GT[:, 1:1 + nt, 0:8].rearrange("p t (d h) -> p t d h", d=2)
            T_ = GT[:, 5:5 + nt, 0:8].rearrange("p t (d h) -> p t d h", d=2)
            act(T_, fpre, AF.Exp, [gk], ['T_'], scale=-1.0)
            act(T_, T_, AF.Ln, ['T_', 'onet'], ['T_'], bias=onet[:, 0:1])
            if is_meta:
                ts('dve', MW[:, 2:3], vmeta[:, 0:1], -1.0, None, ALU.mult, None, ['vmeta'], ['MW2'])
                ts('dve', MW[:, 3:4], vmeta[:, 0:1], 0.0, None, ALU.mult, None, ['vmeta'], ['MW3'])
                wm0 = vmeta[:, 0:1]
            else:
                ts('dve', MW[:, 2:3], mcol(step, 0), -1.0, None, ALU.mult, None, ['MSK'], ['MW2'])
                ts('dve', MW[:, 3:4], mcol(step, 1), -1.0, None, ALU.mult, None, ['MSK'], ['MW3'])
            for d_ in range(2):
                ts('dve', A_[:, :, d_, :], T_[:, :, d_, :], MW[:, 2 + d_:3 + d_], None, ALU.mult, None,
                   ['T_', 'MW2', 'MW3'], ['A_'])
            AH = AHL[:, 0, 0:nt, :]; AL = AHL[:, 1, 0:nt, :]
            cp('dve', AH, GT[:, 1:1 + nt, 0:8], ['A_'], ['AH'])
            tt('dve', AL, GT[:, 1:1 + nt, 0:8], AH, ALU.subtract, ['A_', 'AH'], ['AL'])
            ps, pk = psum()
            for t in range(nt):
                for (pp, kk, first) in ((AH, 'AH', True), (AL, 'AL', False)):
                    mm(ps[:, t * 16:t * 16 + 4], triU_b, pp[:, t, 0:4], first, not first, ['Cb', kk], [pk])
                for (pp, kk, first) in ((AH, 'AH', True), (AL, 'AL', False)):
                    mm(ps[:, t * 16 + 4:t * 16 + 8], triL_b, pp[:, t, 4:8], first, not first, ['Cb', kk], [pk])
                for (pp, kk, first) in ((AH, 'AH', True), (AL, 'AL', False)):
                    mm(ps[:, t * 16 + 8:t * 16 + 16], ones_b, pp[:, t, 0:8], first, not first, ['Cb', kk], [pk])
            CS = GT[:, 9:9 + nt, :]
            cp('dve', CS, ps[:, 0:nt * 16].rearrange("p (t c) -> p t c", t=nt), [pk], ['CS'])
            yield
            SFX = GT[:, 13, :].rearrange("p (t h) -> p t h", t=4)
            PFX = GT[:, 14, :].rearrange("p (t h) -> p t h", t=4)
            S.op('dve', lambda e: e.memset(GT[:, 13:15, :], 0.0), [], ['SFX', 'PFX'])
            if is_meta:
                cp('dve', SFX[:, 0, :], SM[:, 4:8], ['Hrun'], ['SFX'])
            else:
                for t in range(nt - 2, -1, -1):
                    tt('dve', SFX[:, t, :], SFX[:, t + 1, :], CS[:, t + 1, 8:12], ALU.add, ['SFX', 'CS'], ['SFX'])
                cp('dve', PFX[:, 0, :], SM[:, 0:4], ['Grun'], ['PFX'])
                for t in range(1, nt):
                    tt('dve', PFX[:, t, :], PFX[:, t - 1, :], CS[:, t - 1, 12:16], ALU.add, ['PFX', 'CS'], ['PFX'])
                tt('dve', SM[:, 0:4], PFX[:, nt - 1, :], CS[:, nt - 1, 12:16], ALU.add, ['PFX', 'CS'], ['Grun'])
                tt('dve', GT[:, 15, 0:4], SFX[:, 0, :], CS[:, 0, 8:12], ALU.add, ['SFX', 'CS'], ['FLS'])
                tt('dve', SM[:, 4:8], SM[:, 4:8], GT[:, 15, 0:4], ALU.add, ['Hrun', 'FLS'], ['Hrun'])
            ARG = GT[:, 5:5 + nt, 8:16].rearrange("p t (d h) -> p t d h", d=2)
            tt('dve', ARG[:, :, 0, :], ipre[:, :, 0, :], CS[:, :, 0:4], ALU.subtract, [gk, 'CS'], ['ARG'])
            tt('dve', ARG[:, :, 1, :], ipre[:, :, 1, :], CS[:, :, 4:8], ALU.subtract, [gk, 'CS'], ['ARG'])
            tt('dve', ARG[:, :, 0, :], ARG[:, :, 0, :], CS[:, :, 8:12], ALU.add, ['ARG', 'CS'], ['ARG'])
            tt('dve', ARG[:, :, 1, :], ARG[:, :, 1, :], CS[:, :, 12:16], ALU.add, ['ARG', 'CS'], ['ARG'])
            tt('dve', ARG[:, :, 0, :], ARG[:, :, 0, :], SFX[:, 0:nt, :], ALU.add, ['ARG', 'SFX'], ['ARG'])
            tt('dve', ARG[:, :, 1, :], ARG[:, :, 1, :], PFX[:, 0:nt, :], ALU.add, ['ARG', 'PFX'], ['ARG'])
            Wt = GT[:, 1:1 + nt, 8:16].rearrange("p t (d h) -> p t d h", d=2)
            yield
            act(Wt, ARG, AF.Exp, ['ARG', 'lnks'], ['Wt'], bias=lnks[:, 0:1])
            if is_meta:
                ts('dve', Wt[:, :, 0, :], Wt[:, :, 0, :], vmeta[:, 0:1], None, ALU.mult, None, ['Wt', 'vmeta'], ['Wt'])
            else:
                for d_ in range(2):
                    ts('dve', Wt[:, :, d_, :], Wt[:, :, d_, :], mcol(step, d_), None, ALU.mult, None, ['Wt', 'MSK'], ['Wt'])
            if not is_meta:
                act(GT[:, 15, 4:8], GT[:, 15, 0:4], AF.Exp, ['FLS'], ['EFL'])
            ndir = 1 if is_meta else 2
            for t in range(nt):
                for d_ in range(ndir):
                    vsrc = (VRm[:, :] if is_meta else VR[:, xs, t, :]).rearrange("p (h d) -> p h d", h=4)
                    vk = 'VRm' if is_meta else ('VR', xs)
                    wv = WV[:, t, d_, :].rearrange("p (h d) -> p h d", h=4)
                    tt('dve', wv[:, :, 0:128], vsrc,
                       Wt[:, t, d_, :].unsqueeze(2).to_broadcast([128, 4, 128]), ALU.mult, [vk, 'Wt'], [('WV', t, d_)])
                    cp('pool', wv[:, :, 128:129], Wt[:, t, d_, :].unsqueeze(2), ['Wt'], [('WV', t, d_)])
                    yield
            for d_ in range(ndir):
                for h in range(4):
                    ps, pk = psum()
                    for t in range(nt):
                        ksrc = KTKm[:, h * 128:(h + 1) * 128] if is_meta else KTK[:, t, h * 128:(h + 1) * 128]
                        kk = 'KTKm' if is_meta else ('KTK', t)
                        mm(ps[:, 0:129], ksrc, WV[:, t, d_, h * 129:(h + 1) * 129], t == 0, t == nt - 1,
                           [kk, ('WV', t, d_)], [pk])
                    if d_ == 0 and not is_meta:
                        stt(ST_[:, 0, h * 129:(h + 1) * 129], ST_[:, 0, h * 129:(h + 1) * 129], GT[:, 15, 4 + h:5 + h],
                            ps[:, 0:129], ALU.mult, ALU.add, ['Cf', 'EFL', pk], ['Cf'])
                    else:
                        key = 'Cf' if d_ == 0 else 'Bb'
                        tt('dve', ST_[:, d_, h * 129:(h + 1) * 129], ST_[:, d_, h * 129:(h + 1) * 129], ps[:, 0:129],
                           ALU.add, [key, pk], [key])
                    yield

        oo = o
        KTKm, oo = carve(oo, [128, 512], BF)
        VRm, oo = carve(oo, [128, 512], BF)
        GRm, oo = carve(oo, [128, 16], F32)
        assert oo <= 34 * 1024 + 3072

        def ntiles(step):
            return 1 if step == 0 else 4

        load_x(0)
        for t in range(ntiles(0)):
            N_tile(0, t)
        load_x(1)
        for step in range(nsteps):
            if step + 1 < nsteps:
                for t in range(ntiles(step + 1)):
                    pending.append(once(lambda st_=step + 1, t=t: N_tile(st_, t)))
                if step + 2 < nsteps:
                    pending.append(once(lambda st_=step + 2: load_x(st_, 'x')))
            A_step(step)
            while pending:
                pump()
        for _ in tail_gen(nsteps - 1):
            pass
        for _ in M_step(nsteps - 2):
            pass
        for _ in M_step(nsteps - 1):
            pass
        for _ in M_step(0, deferred_meta=True):
            pass
        if DEBUG and s == 1:
            dma(dbg['kn'].ap(), KN[s].ap(), [('KN', s)], ['dbg1'])
            dma(dbg['kr'].ap(), KR[s].ap(), [('KR', s)], ['dbg2'])
            dma(dbg['vs'].ap().rearrange("h p n c -> (h p) (n c)"), VS[s].ap().rearrange("h p n c -> (h p) (n c)"), [('VS', s)], ['dbg3'])
            dma(dbg['qt'].ap().rearrange("h r n -> (h r) n"), QT[s].ap().rearrange("h r n -> (h r) n"), [('QT', s)], ['dbg4'])
            dma(dbg['lq'].ap(), LQ[s].ap(), [('LQ', s)], ['dbg5'])
            dma(dbg['lk'].ap(), LK[s].ap(), [('LK', s)], ['dbg6'])
        if DEBUG:
            dma(dap(dbg['cf'], s * 128 * 516, [[516, 128], [1, 516]]), ST_[:, 0, :], ['Cf'], ['dbgcf'])
            dma(dap(dbg['bb'], s * 128 * 516, [[516, 128], [1, 516]]), ST_[:, 1, :], ['Bb'], ['dbgbb'])
        if stage < 2:
            continue
        S.barrier()
        mlstm_local(s)
        S.barrier()

    if stage >= 3:
        S.barrier()
        for s in range(NSEQ):
            attention(s)
    if stage >= 4:
        S.barrier()
        ffn_phases()

    S.emit(nc, st)
    st.close()
    return nc


def _rope_tables(pos):
    half = QK_ROPE // 2
    freqs = (10000.0 ** (-np.arange(half, dtype=np.float32) / half)).astype(np.float32)
    ang = pos.astype(np.float32)[None, :] * freqs[:, None]
    c = np.cos(ang).astype(np.float32)
    s_ = np.sin(ang).astype(np.float32)
    return np.concatenate([c, c], 0), np.concatenate([-s_, s_], 0)


def _consts():
    i = np.arange(128)
    ident = np.eye(128, dtype=np.float32)
    triU = (i[:, None] <= i[None, :]).astype(np.float32)
    triL = (i[:, None] >= i[None, :]).astype(np.float32)
    ones = np.ones((128, 128), np.float32)
    return np.concatenate([ident, triU, triL, ones, triU, triL], 1)


def prep_inputs(inp):
    f = lambda a: np.ascontiguousarray(np.asarray(a, dtype=np.float32))
    xs = [f(inp["x_prompt"])[0], f(inp["x_sample"])[0], f(inp["x_sample"])[1]]
    meta = f(inp["meta_tokens"])
    w_in = f(inp["w_in"])[0]; b_in = f(inp["b_in"])[0]
    o_cq, o_ckv, o_kr, o_mq, o_mk, o_mv, o_mo, o_g, o_ga, o_gb = np.cumsum([0, 256, 128, 32, 512, 512, 512, 512, 16, 1024])
    rot = np.concatenate([np.arange(16, 32), np.arange(0, 16)])
    cols_a = np.concatenate([np.arange(o_ckv, o_ckv + 128), o_kr + np.arange(32), o_kr + rot,
                             np.arange(o_mk, o_mk + 512), np.arange(o_mv, o_mv + 512), np.arange(o_g, o_g + 16)])
    cols_l = np.concatenate([np.arange(o_cq, o_cq + 256), np.arange(o_mq, o_mq + 512), np.arange(o_mo, o_mo + 512)])
    cols_g = np.arange(o_ga, o_ga + 2048)
    w_uq = f(inp["w_uq"])[0]
    cu = []
    for h in range(NH):
        b = h * QK_DIM
        cu += [b + np.arange(64), b + 64 + np.arange(32), b + 64 + rot]
    w_uq_e = np.ascontiguousarray(w_uq[:, np.concatenate(cu)])
    w_ukv = f(inp["w_ukv"])[0]
    ck = np.concatenate([h * 128 + np.arange(64) for h in range(NH)])
    cv = np.concatenate([h * 128 + 64 + np.arange(64) for h in range(NH)])
    w_ukv_e = np.ascontiguousarray(w_ukv[:, np.concatenate([ck, cv])])
    conv_w = f(inp["conv_w"])[0]; conv_b = f(inp["conv_b"])[0]
    cwt = np.zeros((128, 8, 3), np.float32); cbt = np.zeros((128, 8), np.float32)
    for h in range(4):
        cwt[:, h, :] = conv_w[:, 512 + h * 128:512 + (h + 1) * 128].T
        cwt[:, 4 + h, :] = conv_w[:, h * 128:(h + 1) * 128].T
        cbt[:, h] = conv_b[512 + h * 128:512 + (h + 1) * 128]
        cbt[:, 4 + h] = conv_b[h * 128:(h + 1) * 128]
    colmaj = lambda v, n: np.ascontiguousarray(f(v).reshape(n, 128).T)
    shared = {
        "cst": _consts(),
        "vmeta": ((np.arange(128) >= META_LO) & (np.arange(128) < META_HI)).astype(np.float32)[:, None].copy(),
        "w_a": np.ascontiguousarray(w_in[:, cols_a]), "b_a": np.ascontiguousarray(b_in[cols_a])[None],
        "w_l": np.ascontiguousarray(w_in[:, cols_l]), "b_l": np.ascontiguousarray(b_in[cols_l])[None],
        "w_g": np.ascontiguousarray(w_in[:, cols_g]), "b_g": np.ascontiguousarray(b_in[cols_g])[None],
        "w_uq": w_uq_e, "w_ukv": w_ukv_e,
        "w_pa": f(inp["w_pa"])[0], "w_pb": f(inp["w_pb"])[0], "w_o": f(inp["w_o"])[0],
        "w_gu": np.ascontiguousarray(np.concatenate([f(inp["w_ffn_gate"])[0], f(inp["w_ffn_up"])[0]], 1)),
        "w_dn": f(inp["w_ffn_down"])[0],
        "g1": colmaj(inp["norm1_g"], 8), "g2": colmaj(inp["norm2_g"], 8),
        "gq": colmaj(inp["q_norm_g"], 2), "gkv": colmaj(inp["kv_norm_g"], 1), "gm": colmaj(inp["m_norm_g"], 4),
        "gfin": f(inp["final_norm_g"])[None],
        "cw": cwt.reshape(128, 24), "cb": cbt,
    }
    in_maps = []
    for c in range(NCORE):
        m = dict(shared)
        for s in range(NSEQ):
            nsup, nl = NSUP[s], NLOC[s]
            L0 = c * nl
            order = [(L0 + i) % nsup for i in range(nsup)]
            x = xs[s]
            mt = np.zeros((128, D), np.float32)
            mt[0] = meta[15] if L0 == 0 else x[L0 * 512 - 1]
            mt[META_LO:META_HI] = meta
            mt[127] = x[0]
            xr = x.reshape(nsup, 512, D)[order].reshape(nsup * 512, D)
            m["xin%d" % s] = np.concatenate([mt, xr], 0)
            m["xloc%d" % s] = np.ascontiguousarray(x[L0 * 512:(L0 + nl) * 512])
            pos = np.zeros(TK(s), np.float32)
            pos[META_LO:META_HI] = np.arange(16)
            for i, su in enumerate(order):
                pos[128 + i * 512:128 + (i + 1) * 512] = 16 + su * 512 + np.arange(512)
            ct, sn = _rope_tables(pos)
            m["cos%d" % s] = ct; m["sin%d" % s] = sn
            mk = np.zeros((1 + nsup, 4), np.float32)
            mk[0] = [1, 0, 0, 0]
            for i, su in enumerate(order):
                st_ = i + 1
                before = su < L0
                after = su >= L0 + nl
                wl = 0.0 if su == 0 else 1.0
                wr = 0.0 if su == nsup - 1 else 1.0
                mk[st_] = [float(before), float(after), wl, wr]
            m["msk%d" % s] = np.ascontiguousarray(np.broadcast_to(mk.reshape(1, -1), (128, (1 + nsup) * 4)))
        in_maps.append(m)
    return in_maps


_NC_CACHE = {}


def kernel(**inputs):
    in_maps = prep_inputs(inputs)
    if 'nc' not in _NC_CACHE:
        _NC_CACHE['nc'] = build()
    nc = _NC_CACHE['nc']
    res = run_bass_kernel_spmd(nc, in_maps, core_ids=list(range(NCORE)))
    outs = []
    yp = np.concatenate([res.results[c]["y0"] for c in range(NCORE)], 0)[None]
    ys = np.stack([np.concatenate([res.results[c]["y%d" % s] for c in range(NCORE)], 0) for s in (1, 2)], 0)
    return (np.ascontiguousarray(yp.astype(np.float32)), np.ascontiguousarray(ys.astype(np.float32)))
```

```python
import numpy as np
from contextlib import ExitStack
import concourse.bass as bass
import concourse.mybir as mybir
from concourse.bass_utils import run_bass_kernel_spmd

F32 = mybir.dt.float32
BF = mybir.dt.bfloat16
AF = mybir.ActivationFunctionType
ALU = mybir.AluOpType
AX = mybir.AxisListType

NCORE = 8
D = 1024
P = 128
NSUP = (32, 16, 16)
NLOC = (4, 2, 2)
NSEQ = 3
QK_NOPE, QK_ROPE, V_HEAD, NH = 64, 32, 64, 8
QK_DIM = 96
MH, MD = 4, 128
D_FF = 2816
EPS = 1e-6
SM_SCALE = QK_DIM ** -0.5
K_SCALE = MD ** -0.5
META_LO, META_HI = 111, 127
NA = 128 + 64 + 512 + 512 + 16
NL_ = 256 + 512 + 512
DEBUG = False


def TK(s):
    return (1 + 4 * NSUP[s]) * 128


class Sched:
    LIMIT = 20000
    R = 24
    DMAQ = ('sp',)

    def __init__(self):
        self.ops = []
        self.lw = {}
        self.rd = {}
        self.bar = None

    def _last_ops(self):
        last = {}
        dl = {}
        for i, o in enumerate(self.ops):
            if o[0] in self.DMAQ:
                dl.setdefault(o[0], []).append(i)
            else:
                last[o[0]] = i
        s = set(last.values())
        for e, l in dl.items():
            s.update(l[-self.R:])
        return s

    def barrier(self):
        self.bar = (self._last_ops(), set())
        self.lw = {}
        self.rd = {}

    def op(self, eng, fn, r=(), w=()):
        i = len(self.ops)
        hard, soft = set(), set()
        if self.bar is not None and eng not in self.bar[1]:
            hard.update(self.bar[0])
            self.bar[1].add(eng)
        isd = eng in self.DMAQ
        for k in r:
            hard.update(self.lw.get(k, ()))
        for k in w:
            hard.update(self.lw.get(k, ()))
            for kk, v in self.rd.get(k, {}).items():
                if isinstance(v, list):
                    soft.update(v)
                else:
                    soft.add(v)
        self.ops.append([eng, fn, hard, soft])
        for k in r:
            d = self.rd.setdefault(k, {})
            if isd:
                d.setdefault(('dma', eng), []).append(i)
            else:
                d[eng] = i
        for k in w:
            if isd and not self.rd.get(k) and k in self.lw and all(self.ops[x][0] in self.DMAQ for x in self.lw[k]):
                self.lw[k] = set(self.lw[k]) | {i}
            else:
                self.lw[k] = {i}
            self.rd[k] = {}
        return i

    def emit(self, nc, stack):
        ops = self.ops
        n = len(ops)
        dmaq = self.DMAQ
        need = [False] * n
        deps = [None] * n
        for i, (eng, fn, hard, soft) in enumerate(ops):
            dl = {}
            dd = set()
            for d, is_hard in [(x, True) for x in hard] + [(x, False) for x in soft]:
                if d >= i:
                    continue
                e2 = ops[d][0]
                if e2 in dmaq:
                    dd.add(d)
                    continue
                if e2 == eng:
                    if eng == 'pe' or not is_hard:
                        continue
                if e2 not in dl or dl[e2] < d:
                    dl[e2] = d
            deps[i] = (dl, sorted(dd))
            for d in dl.values():
                need[d] = True
        sig = [None] * n
        cnt, ep = {}, {}
        dcount = {}
        for i, o in enumerate(ops):
            e = o[0]
            if e in dmaq:
                j = dcount.get(e, 0)
                dcount[e] = j + 1
                sig[i] = (e, 'd%d' % (j % self.R), j // self.R + 1)
                continue
            if not need[i]:
                continue
            c = cnt.get(e, 0) + 1
            if c > self.LIMIT:
                ep[e] = ep.get(e, 0) + 1
                c = 1
            cnt[e] = c
            sig[i] = (e, ep.get(e, 0), c)
        sems = {}
        for s in sig:
            if s is not None and (s[0], s[1]) not in sems:
                sems[(s[0], s[1])] = stack.enter_context(nc.semaphore("s_%s_%s" % (s[0], s[1])))
        per_eng = {}
        for i, o in enumerate(ops):
            per_eng.setdefault(o[0], []).append(i)
        final = self._last_ops()

        def run(eng_name, handle):
            waited = {}

            def wait_for(d):
                s = sig[d]
                key = (s[0], s[1])
                if waited.get(key, 0) >= s[2]:
                    return
                handle.wait_ge(sems[key], s[2] * (16 if s[0] in dmaq else 1))
                waited[key] = s[2]

            for i in per_eng.get(eng_name, []):
                dl, dd = deps[i]
                for e2, d in dl.items():
                    wait_for(d)
                for d in dd:
                    wait_for(d)
                if eng_name in dmaq:
                    s = sig[i]
                    if s[2] > 1:
                        key = (s[0], s[1])
                        if waited.get(key, 0) < s[2] - 1:
                            handle.wait_ge(sems[key], (s[2] - 1) * 16)
                            waited[key] = s[2] - 1
                ins = ops[i][1](handle)
                if sig[i] is not None:
                    s = sig[i]
                    ins.then_inc(sems[(s[0], s[1])], 16 if eng_name in dmaq else 1)
            if eng_name in dmaq:
                for d in sorted(final):
                    if ops[d][0] in dmaq:
                        wait_for(d)

        with nc.Block() as block:
            @block.tensor
            def _(t):
                run('pe', t)

            @block.scalar
            def _(t):
                run('act', t)

            @block.vector
            def _(t):
                run('dve', t)

            @block.gpsimd
            def _(t):
                run('pool', t)

            @block.sync
            def _(t):
                run('sp', t)


def dap(t, off, pat):
    return bass.AP(t, off, [list(p) for p in pat])


def build(stage=99):
    nc = bass.Bass("TRN2", target_bir_lowering=False)
    S = Sched()
    st = ExitStack()
    di = {}

    def din(name, shape, dt=F32):
        di[name] = nc.dram_tensor(name, list(shape), dt, kind="ExternalInput")
        return di[name]

    def dscr(name, shape, dt=BF):
        return nc.dram_tensor(name, list(shape), dt, kind="Internal")

    def dout(name, shape, dt=F32):
        return nc.dram_tensor(name, list(shape), dt, kind="ExternalOutput")

    def sb(name, shape, dt=F32):
        return st.enter_context(nc.sbuf_tensor("sb_" + name, list(shape), dt))

    xin = [din("xin%d" % s, [128 + NSUP[s] * 512, D]) for s in range(NSEQ)]
    xloc = [din("xloc%d" % s, [NLOC[s] * 512, D]) for s in range(NSEQ)]
    cosd = [din("cos%d" % s, [32, TK(s)]) for s in range(NSEQ)]
    sind = [din("sin%d" % s, [32, TK(s)]) for s in range(NSEQ)]
    mskd = [din("msk%d" % s, [128, (1 + NSUP[s]) * 4]) for s in range(NSEQ)]
    cst = din("cst", [128, 6 * 128])
    vmeta_d = din("vmeta", [128, 1])
    w_a_d = din("w_a", [D, NA]); b_a_d = din("b_a", [1, NA])
    w_l_d = din("w_l", [D, NL_]); b_l_d = din("b_l", [1, NL_])
    w_g_d = din("w_g", [D, 2048]); b_g_d = din("b_g", [1, 2048])
    w_uq_d = din("w_uq", [256, 1024])
    w_ukv_d = din("w_ukv", [128, 1024])
    w_pa_d = din("w_pa", [512, D]); w_pb_d = din("w_pb", [512, D]); w_o_d = din("w_o", [D, D])
    w_gu_d = din("w_gu", [D, 2 * D_FF]); w_dn_d = din("w_dn", [D_FF, D])
    g1_d = din("g1", [128, 8]); g2_d = din("g2", [128, 8])
    gq_d = din("gq", [128, 2]); gkv_d = din("gkv", [128, 1]); gm_d = din("gm", [128, 4])
    gfin_d = din("gfin", [1, D])
    cw_d = din("cw", [128, 8 * 3]); cb_d = din("cb", [128, 8])

    y = [dout("y%d" % s, [NLOC[s] * 512, D]) for s in range(NSEQ)]

    KN = [dscr("kn%d" % s, [NH * 64, TK(s)]) for s in range(NSEQ)]
    KR = [dscr("kr%d" % s, [33, TK(s)]) for s in range(NSEQ)]
    VS = [dscr("vs%d" % s, [NH, 128, TK(s) // 128, 65]) for s in range(NSEQ)]
    QT = [dscr("qt%d" % s, [NH, 97, NLOC[s] * 512]) for s in range(NSEQ)]
    LQ = [dscr("lq%d" % s, [MH * 128, NLOC[s] * 512]) for s in range(NSEQ)]
    LK = [dscr("lk%d" % s, [MH * 128, NLOC[s] * 512]) for s in range(NSEQ)]
    LKT = [dscr("lkt%d" % s, [128, NLOC[s] * 4, 512]) for s in range(NSEQ)]
    LV = [dscr("lv%d" % s, [128, NLOC[s] * 4, MH * 129]) for s in range(NSEQ)]
    LG = [dscr("lg%d" % s, [128, NLOC[s] * 4, 16], F32) for s in range(NSEQ)]
    LMO = [dscr("lmo%d" % s, [128, NLOC[s] * 4, 512]) for s in range(NSEQ)]
    AOT = [dscr("aot%d" % s, [512, NLOC[s] * 512]) for s in range(NSEQ)]
    RIV = [dscr("riv%d" % s, [NH, NLOC[s] * 512], F32) for s in range(NSEQ)]
    MOT = [dscr("mot%d" % s, [512, NLOC[s] * 512]) for s in range(NSEQ)]
    H1 = [dscr("h1_%d" % s, [NLOC[s] * 512, D], F32) for s in range(NSEQ)]
    X2T = [dscr("x2t%d" % s, [D, NLOC[s] * 512]) for s in range(NSEQ)]
    HT = [dscr("ht%d" % s, [D_FF, NLOC[s] * 512]) for s in range(NSEQ)]
    dbg = {}
    if DEBUG:
        dbg['cf'] = dout("dbg_cf", [NSEQ, 128, MH * 129])
        dbg['bb'] = dout("dbg_bb", [NSEQ, 128, MH * 129])
        dbg['kn'] = dout("dbg_kn", [NH * 64, TK(1)], BF)
        dbg['kr'] = dout("dbg_kr", [33, TK(1)], BF)
        dbg['vs'] = dout("dbg_vs", [NH, 128, TK(1) // 128, 65], BF)
        dbg['qt'] = dout("dbg_qt", [NH, 97, NLOC[1] * 512], BF)
        dbg['lq'] = dout("dbg_lq", [MH * 128, NLOC[1] * 512], BF)
        dbg['lk'] = dout("dbg_lk", [MH * 128, NLOC[1] * 512], BF)
        dbg['aot'] = dout("dbg_aot", [512, NLOC[1] * 512], BF)
        dbg['mot'] = dout("dbg_mot", [512, NLOC[1] * 512], BF)

    C = sb("cst", [128, 6 * 128])
    Cb = sb("cstb", [128, 6 * 128], BF)
    ident_b = Cb[:, 0:128]
    ones_b = Cb[:, 3 * 128:4 * 128]
    triU_b = Cb[:, 128:256]; triL_b = Cb[:, 256:384]
    triU = C[:, 128:256]; triL = C[:, 256:384]; onesF = C[:, 384:512]
    maskF = C[:, 512:640]; maskB = C[:, 640:768]
    vmeta = sb("vmeta", [128, 1])
    epst = sb("epst", [128, 1]); onet = sb("onet", [128, 1]); lnks = sb("lnks", [128, 1])
    g1 = sb("g1", [128, 8]); g2 = sb("g2", [128, 8]); gq = sb("gq", [128, 2]); gkv = sb("gkv", [128, 1])
    gm = sb("gm", [128, 4])
    cw = sb("cw", [128, 24]); cb = sb("cb", [128, 8])
    WA = sb("WA", [128, 8, NA], BF)
    WL = sb("WL", [128, 8, NL_], BF)
    WUQ = sb("WUQ", [128, 2, 1024], BF)
    WUKV = sb("WUKV", [128, 1024], BF)
    bA = sb("bA", [128, 2])
    bAm = sb("bAm", [128, 4])
    bLq = sb("bLq", [128, 2 + 4])
    bbc = sb("bbc", [128, 512 + 16 + 512])
    stage_t = sb("stage", [128, 1024])
    bG_t = sb("bG", [128, 16])
    S4_t = sb("S4", [128, 16])
    BIG = sb("BIG", [128, 34 * 1024 + 3072])

    def carve(off_words, shape, dt):
        n = int(np.prod(shape[1:]))
        words = n if dt == F32 else (n + 1) // 2
        v = BIG[0:shape[0], off_words:off_words + words]
        if dt == BF:
            v = v.bitcast(BF)[:, 0:n]
        if len(shape) == 3:
            v = v.rearrange("p (a b) -> p a b", a=shape[1])
        elif len(shape) == 4:
            v = v.rearrange("p (a b c) -> p a b c", a=shape[1], b=shape[2])
        return v, off_words + words

    PSA = st.enter_context(nc.psum_tensor("psa", [128, 8 * 512], F32))
    PSAb = PSA.bitcast(BF)

    class _Bank:
        def __init__(self, i):
            self.i = i

        def __getitem__(self, idx):
            p, c = idx
            c0 = 0 if c.start is None else c.start
            c1 = 512 if c.stop is None else c.stop
            return PSA[p, self.i * 512 + c0:self.i * 512 + c1]

        def bitcast(self, dt):
            b = self

            class _B:
                def __getitem__(self, idx):
                    p, c = idx
                    c0 = 0 if c.start is None else c.start
                    c1 = 1024 if c.stop is None else c.stop
                    return PSAb[p, b.i * 1024 + c0:b.i * 1024 + c1]
            return _B()
    PS = [_Bank(i) for i in range(8)]
    psn = [0]

    def psum():
        i = psn[0] % 8
        psn[0] += 1
        return PS[i], ('ps', i)

    def dma(out, in_, r, w, q='sp'):
        S.op(q, lambda e: e.dma_start(out=out, in_=in_, allow_slow_non_contiguous=True), r, w)

    def act(out, in_, func, r, w, bias=None, scale=None, accum=None):
        kw = {}
        if bias is not None:
            kw['bias'] = bias
        if scale is not None:
            kw['scale'] = scale
        if accum is not None:
            kw['accum_out'] = accum
        S.op('act', lambda e: e.activation(out, in_, func, **kw), r, w)

    def ts(eng, out, in0, s1, s2, op0, op1, r, w):
        if op1 is None:
            S.op(eng, lambda e: e.tensor_scalar(out, in0, s1, None, op0), r, w)
        else:
            S.op(eng, lambda e: e.tensor_scalar(out, in0, s1, s2, op0, op1), r, w)

    def tt(eng, out, in0, in1, op, r, w):
        S.op(eng, lambda e: e.tensor_tensor(out, in0, in1, op), r, w)

    def stt(out, in0, sc, in1, op0, op1, r, w):
        S.op('dve', lambda e: e.scalar_tensor_tensor(out, in0, sc, in1, op0, op1), r, w)

    def cp(eng, out, in_, r, w):
        if eng == 'act':
            S.op('act', lambda e: e.copy(out, in_), r, w)
        else:
            S.op(eng, lambda e: e.tensor_copy(out, in_), r, w)

    def mm(out, lhsT, rhs, start, stop, r, w):
        S.op('pe', lambda e: e.matmul(out, lhsT, rhs, start=start, stop=stop), r, w)

    def tr(out, in_, r, w):
        S.op('pe', lambda e: e.transpose(out, in_, ident_b), r, w)

    S.op('pool', lambda e: e.memset(BIG[:, :], 0.0), [], ['BIGZ'])
    S.barrier()
    dma(C[:], cst.ap(), [], ['C'])
    cp('dve', Cb[:], C[:], ['C'], ['Cb'])
    dma(vmeta[:], vmeta_d.ap(), [], ['vmeta'])
    S.op('dve', lambda e: e.memset(epst[:], EPS), [], ['epst'])
    S.op('dve', lambda e: e.memset(onet[:], 1.0), [], ['onet'])
    S.op('dve', lambda e: e.memset(lnks[:], float(np.log(K_SCALE))), [], ['lnks'])
    for t_, d_, k_ in ((g1, g1_d, 'g1'), (g2, g2_d, 'g2'), (gq, gq_d, 'gq'), (gkv, gkv_d, 'gkv'), (gm, gm_d, 'gm'),
                       (cw, cw_d, 'cw'), (cb, cb_d, 'cb')):
        dma(t_[:], d_.ap(), [], [k_])
    dma(bA[:, 0:1], dap(b_a_d, 0, [[1, 128], [1, 1]]), [], ['bA'])
    dma(bA[0:64, 1:2], dap(b_a_d, 128, [[1, 64], [1, 1]]), [], ['bA'])
    dma(bAm[:], dap(b_a_d, 192, [[1, 128], [128, 4]]), [], ['bAm'])
    dma(bLq[:, 0:2], dap(b_l_d, 0, [[1, 128], [128, 2]]), [], ['bLq'])
    dma(bLq[:, 2:6], dap(b_l_d, 256, [[1, 128], [128, 4]]), [], ['bLq'])
    dma(bbc[:, 0:528], dap(b_a_d, 704, [[0, 128], [1, 528]]), [], ['bbc'])
    dma(bbc[:, 528:1040], dap(b_l_d, 768, [[0, 128], [1, 512]]), [], ['bbc'])

    def load_weight(dst, src, rows, cols, gain, gkey, wkey, engines=('dve', 'act'), jobs=None):
        nk = rows // 128

        def one_chunk(kt, c0, ci):
            cn = min(1024, cols - c0)
            dma(stage_t[:, 0:cn], dap(src, kt * 128 * cols + c0, [[cols, 128], [1, cn]]), [], ['stage'])
            o = dst[:, kt, c0:c0 + cn] if nk > 1 or len(dst.shape) == 3 else dst[:, c0:c0 + cn]
            eng = engines[ci % len(engines)]
            if gain is None:
                cp(eng, o, stage_t[:, 0:cn], ['stage'], [wkey])
            elif eng == 'dve':
                ts(eng, o, stage_t[:, 0:cn], gain[:, kt:kt + 1], None, ALU.mult, None,
                   ['stage', gkey], [wkey])
            else:
                act(o, stage_t[:, 0:cn], AF.Copy, ['stage', gkey], [wkey], scale=gain[:, kt:kt + 1])

        ci = 0
        for kt in range(nk):
            for c0 in range(0, cols, 1024):
                if jobs is not None:
                    jobs.append(lambda kt=kt, c0=c0, ci=ci: one_chunk(kt, c0, ci))
                else:
                    one_chunk(kt, c0, ci)
                ci += 1

    load_weight(WA, w_a_d, D, NA, g1, 'g1', 'WA')
    load_weight(WL, w_l_d, D, NL_, g1, 'g1', 'WL')
    load_weight(WUQ, w_uq_d, 256, 1024, gq, 'gq', 'WUQ')
    load_weight(WUKV, w_ukv_d, 128, 1024, gkv, 'gkv', 'WUKV')

    o = 0
    ST_, o = carve(o, [128, 2, 4 * 129], F32)
    O_PERSIST = o
    XT, o = carve(o, [128, 4, 1024], F32)
    XN, o = carve(o, [128, 2, 1024], BF)
    XNT, o = carve(o, [128, 2, 8, 512], BF)
    MKr, o = carve(o, [128, 3, 8, 514], BF)
    VR, o = carve(o, [128, 3, 4, 512], BF)
    GR, o = carve(o, [128, 3, 4, 16], F32)
    CKV, o = carve(o, [128, 2, 512], BF)
    SQ, o = carve(o, [128, 2, 512], BF)
    LNT, o = carve(o, [128, 512], F32)
    RBC, o = carve(o, [128, 512], F32)
    KNS, o = carve(o, [128, 4, 512], BF)
    KRS, o = carve(o, [33, 512], BF)
    KT1, o = carve(o, [32, 512], F32)
    KT2, o = carve(o, [32, 512], F32)
    VST, o = carve(o, [128, 4, 8, 65], BF)
    CSK, o = carve(o, [64, 2, 512], F32)
    SSQ, o = carve(o, [128, 8], F32)
    RST, o = carve(o, [128, 8], F32)
    CV, o = carve(o, [128, 2, 512], F32)
    KTb, o = carve(o, [128, 8, 512], BF)
    KTK, o = carve(o, [128, 4, 512], BF)
    WV, o = carve(o, [128, 4, 2, 4 * 129], BF)
    GT, o = carve(o, [128, 16, 16], F32)
    AHL, o = carve(o, [128, 2, 4, 8], BF)
    CQT, o = carve(o, [128, 2, 512], BF)
    QTS, o = carve(o, [97, 2, 512], BF)
    CSQ, o = carve(o, [128, 2, 512], F32)
    MOS, o = carve(o, [128, 2, 512], BF)
    VAL, o = carve(o, [128, 4, 4 * 129], BF)
    SM, o = carve(o, [128, 64], F32)
    MSK, o = carve(o, [128, 33 * 4], F32)
    MW, o = carve(o, [128, 8], F32)
    assert o <= 34 * 1024 + 3072, o


    def mlstm_local(s):
        nl = NLOC[s]
        nch = nl * 4
        nq = nl * 512
        o2 = O_PERSIST
        QTl, o2 = carve(o2, [128, 4, 512], BF)
        KTl, o2 = carve(o2, [128, 4, 512], BF)
        KKl, o2 = carve(o2, [128, 4, 512], BF)
        VAl, o2 = carve(o2, [128, 4, 516], BF)
        GRl, o2 = carve(o2, [128, 4, 16], F32)
        MOl, o2 = carve(o2, [128, 4, 512], BF)
        HF, o2 = carve(o2, [128, nch, 512], F32)
        G2, o2 = carve(o2, [128, 16, 16], F32)
        AH2, o2 = carve(o2, [128, 2, 4, 4], BF)
        SST, o2 = carve(o2, [128, 2, 128], BF)
        WVl, o2 = carve(o2, [128, 2, 130], BF)
        CBF, o2 = carve(o2, [128, 2, 4 * 130], BF)
        RC, o2 = carve(o2, [128, 8], F32)
        HG, o2 = carve(o2, [128, 512], F32)
        HSQ, o2 = carve(o2, [128, 128], F32)
        MOb, o2 = carve(o2, [128, 512], BF)
        MTs, o2 = carve(o2, [128, 4, 128], BF)
        assert o2 <= 34 * 1024 + 3072, o2

        def load_sup(i, with_mo=False):
            for h in range(4):
                dma(QTl[:, h, :], dap(LQ[s], h * 128 * nq + i * 512, [[nq, 128], [1, 512]]), [('LQ', s)], ['QTl'])
                dma(KTl[:, h, :], dap(LK[s], h * 128 * nq + i * 512, [[nq, 128], [1, 512]]), [('LK', s)], ['KTl'])
            dma(KKl[:, :, :].rearrange("p t c -> p (t c)"), dap(LKT[s], i * 4 * 512, [[nch * 512, 128], [1, 2048]]), [('LKT', s)], ['KKl'])
            dma(VAl[:, :, :].rearrange("p t c -> p (t c)"), dap(LV[s], i * 4 * 516, [[nch * 516, 128], [1, 4 * 516]]), [('LV', s)], ['VAl'])
            dma(GRl[:, :, :].rearrange("p t c -> p (t c)"), dap(LG[s], i * 64, [[nch * 16, 128], [1, 64]]), [('LG', s)], ['GRl'])

        for d_ in range(2):
            skey = 'Cf' if d_ == 0 else 'Bb'
            for h in range(4):
                cp('pool', CBF[:, d_, h * 130:h * 130 + 129], ST_[:, d_, h * 129:(h + 1) * 129], [skey], [('CBF', d_)])
            sups = range(nl) if d_ == 0 else range(nl - 1, -1, -1)
            for i in sups:
                load_sup(i)
                G4 = GRl[:, :, :].rearrange("p t (d j h) -> p t d j h", d=2, j=2)
                fpre = G4[:, :, d_, 1, :]; ipre = G4[:, :, d_, 0, :]
                T_ = G2[:, 0:4, 0:4]; A_ = G2[:, 0:4, 4:8]; CS = G2[:, 4:8, 0:8]
                ARG = G2[:, 0:4, 8:12]; DD = G2[:, 8:12, 0:4]; WW = G2[:, 8:12, 4:8]; EF = G2[:, 8:12, 8:12]
                FLO = G2[:, 8:12, 12:16]
                act(T_, fpre, AF.Exp, ['GRl'], ['T2'], scale=-1.0)
                act(T_, T_, AF.Ln, ['T2', 'onet'], ['T2'], bias=onet[:, 0:1])
                ts('dve', A_, T_, -1.0, None, ALU.mult, None, ['T2'], ['A2'])
                cp('dve', AH2[:, 0, :, :], A_, ['A2'], ['AH2'])
                tt('dve', AH2[:, 1, :, :], A_, AH2[:, 0, :, :], ALU.subtract, ['A2', 'AH2'], ['AL2'])
                ps, pk = psum()
                tri = triU_b if d_ == 0 else triL_b
                for t in range(4):
                    for j_, kk in ((0, 'AH2'), (1, 'AL2')):
                        mm(ps[:, t * 8:t * 8 + 4], tri, AH2[:, j_, t, :], j_ == 0, j_ == 1, ['Cb', kk], [pk])
                    for j_, kk in ((0, 'AH2'), (1, 'AL2')):
                        mm(ps[:, t * 8 + 4:t * 8 + 8], ones_b, AH2[:, j_, t, :], j_ == 0, j_ == 1, ['Cb', kk], [pk])
                cp('dve', CS, ps[:, 0:32].rearrange("p (t c) -> p t c", t=4), [pk], ['CS2'])
                tt('dve', ARG, ipre, CS[:, :, 0:4], ALU.subtract, ['GRl', 'CS2'], ['ARG2'])
                act(DD, ARG, AF.Exp, ['ARG2', 'lnks'], ['DD'], bias=lnks[:, 0:1])
                tt('dve', ARG, ARG, CS[:, :, 4:8], ALU.add, ['ARG2', 'CS2'], ['ARG2'])
                act(WW, ARG, AF.Exp, ['ARG2', 'lnks'], ['WW'], bias=lnks[:, 0:1])
                act(EF, CS[:, :, 4:8], AF.Exp, ['CS2'], ['EF'])
                act(FLO, CS[:, :, 0:4], AF.Exp, ['CS2'], ['FLO'], scale=-1.0)
                chunks = range(4) if d_ == 0 else range(3, -1, -1)
                msk = maskF if d_ == 0 else maskB
                for t in chunks:
                    c = i * 4 + t
                    for h in range(4):
                        sl = h % 2
                        ps1, pk1 = psum()
                        mm(ps1[:, 0:128], KTl[:, h, t * 128:(t + 1) * 128], QTl[:, h, t * 128:(t + 1) * 128], True, True,
                           ['KTl', 'QTl'], [pk1])
                        stt(SST[:, sl, :], ps1[:, 0:128], DD[:, t, h:h + 1], msk, ALU.mult, ALU.mult, [pk1, 'DD', 'C'], [('SST', sl)])
                        ps2, pk2 = psum()
                        mm(ps2[:, 0:129], QTl[:, h, t * 128:(t + 1) * 128], CBF[:, d_, h * 130:h * 130 + 129], True, False,
                           ['QTl', ('CBF', d_)], [pk2])
                        mm(ps2[:, 0:129], SST[:, sl, :], VAl[:, t, h * 129:(h + 1) * 129], False, True, [('SST', sl), 'VAl'], [pk2])
                        ts('dve', RC[:, sl:sl + 1], ps2[:, 128:129], FLO[:, t, h:h + 1], None, ALU.max, None, [pk2, 'FLO'], [('RC', sl)])
                        stt(RC[:, sl:sl + 1], ps2[:, 128:129], -1.0, RC[:, sl:sl + 1], ALU.mult, ALU.max, [pk2, ('RC', sl)], [('RC', sl)])
                        S.op('dve', lambda e, sl=sl: e.reciprocal(RC[:, sl:sl + 1], RC[:, sl:sl + 1]), [('RC', sl)], [('RC', sl)])
                        hdst = HF[:, c, h * 128:(h + 1) * 128]
                        if d_ == 0:
                            ts('dve', hdst, ps2[:, 0:128], RC[:, sl:sl + 1], None, ALU.mult, None, [pk2, ('RC', sl)], [('HF', c)])
                        else:
                            stt(hdst, ps2[:, 0:128], RC[:, sl:sl + 1], hdst, ALU.mult, ALU.add, [pk2, ('RC', sl), ('HF', c)], [('HF', c)])
                        ts('dve', WVl[:, sl, 0:129], VAl[:, t, h * 129:(h + 1) * 129], WW[:, t, h:h + 1], None, ALU.mult, None,
                           ['VAl', 'WW'], [('WVl', sl)])
                        ps3, pk3 = psum()
                        mm(ps3[:, 0:129], KKl[:, t, h * 128:(h + 1) * 128], WVl[:, sl, 0:129], True, True, ['KKl', ('WVl', sl)], [pk3])
                        stt(ST_[:, d_, h * 129:(h + 1) * 129], ST_[:, d_, h * 129:(h + 1) * 129], EF[:, t, h:h + 1], ps3[:, 0:129],
                            ALU.mult, ALU.add, [skey, 'EF', pk3], [skey])
                        cp('pool', CBF[:, d_, h * 130:h * 130 + 129], ST_[:, d_, h * 129:(h + 1) * 129], [skey], [('CBF', d_)])
        for i in range(nl):
            dma(MOl[:, :, :].rearrange("p t c -> p (t c)"), dap(LMO[s], i * 4 * 512, [[nch * 512, 128], [1, 2048]]), [('LMO', s)], ['MOl'])
            for t in range(4):
                c = i * 4 + t
                tt('dve', HG[:, :], HF[:, c, :], MOl[:, t, :], ALU.mult, [('HF', c), 'MOl'], ['HG'])
                for h in range(4):
                    act(HSQ[:, :], HG[:, h * 128:(h + 1) * 128], AF.Square, ['HG'], ['HSQ', 'RC2'], accum=RC[:, 4 + h:5 + h])
                act(RC[:, 4:8], RC[:, 4:8], AF.Sqrt, ['RC2', 'epst'], ['RC2'], bias=epst[:, 0:1], scale=1.0 / 128)
                S.op('dve', lambda e: e.reciprocal(RC[:, 4:8], RC[:, 4:8]), ['RC2'], ['RC2'])
                tt('dve', MOb[:, :].rearrange("p (h d) -> p h d", h=4), HG[:, :].rearrange("p (h d) -> p h d", h=4),
                   RC[:, 4:8].unsqueeze(2).to_broadcast([128, 4, 128]), ALU.mult, ['HG', 'RC2'], ['MOb'])
                ps, pk = psum()
                psb = ps.bitcast(BF)
                for h in range(4):
                    tr(psb[:, h * 128:(h + 1) * 128], MOb[:, h * 128:(h + 1) * 128], ['MOb', 'Cb'], [pk])
                cp('act', MTs[:, :, :], psb[:, 0:512].rearrange("p (h t) -> p h t", h=4), [pk], ['MTs'])
                dma(dap(MOT[s], c * 128, [[nq, 128], [128 * nq, 4], [1, 128]]), MTs[:, :, :], ['MTs'], [('MOT', s)])
        if DEBUG and s == 1:
            dma(dbg['mot'].ap(), MOT[s].ap(), [('MOT', s)], ['dbg8'])

    ARENA_END = 34 * 1024 + 3072
    O_B4W = ARENA_END - 16384
    b4a_jobs = []

    def b4a_weights():
        o3 = O_B4W
        WG_, o3 = carve(o3, [128, 8, 2048], BF)
        WPA_, o3 = carve(o3, [128, 4, 1024], BF)
        WPB_, o3 = carve(o3, [128, 4, 1024], BF)
        WO_, o3 = carve(o3, [128, 8, 1024], BF)
        assert o3 <= ARENA_END
        return WG_, WPA_, WPB_, WO_

    def prefetch_b4a_weights():
        WG_, WPA_, WPB_, WO_ = b4a_weights()
        load_weight(WG_, w_g_d, D, 2048, g1, 'g1', 'WG', engines=('dve',), jobs=b4a_jobs)
        load_weight(WPA_, w_pa_d, 512, D, None, None, 'WPA', engines=('dve',), jobs=b4a_jobs)
        load_weight(WPB_, w_pb_d, 512, D, gm, 'gm', 'WPB', engines=('dve',), jobs=b4a_jobs)
        load_weight(WO_, w_o_d, D, D, None, None, 'WO', engines=('dve',), jobs=b4a_jobs)
        b4a_jobs.append(lambda: dma(bG_t[:, :], dap(b_g_d, 0, [[1, 128], [128, 16]]), [], ['bG']))

    def attention(s):
        nq = NLOC[s] * 512
        nb = TK(s) // 128
        nqb = nq // 512
        o2 = O_PERSIST
        KTt, o2 = carve(o2, [97, TK(s)], BF)
        Vh, o2 = carve(o2, [128, nb, 65], BF)
        if o2 % 2:
            o2 += 1
        QTh, o2 = carve(o2, [97, nq], BF)
        PT, o2 = carve(o2, [128, 3, 1024], BF)
        RR, o2 = carve(o2, [128, 2, 512], F32)
        AO, o2 = carve(o2, [64, 2, 512], BF)
        assert o2 <= O_B4W, o2
        NSEG = 8
        segb = [(nb * g) // NSEG for g in range(NSEG + 1)]
        seg_of = {}
        for g in range(NSEG):
            for kb in range(segb[g], segb[g + 1]):
                seg_of[kb] = g
        for g in range(NSEG):
            c0, c1 = segb[g] * 128, segb[g + 1] * 128
            dma(KTt[64:97, c0:c1], dap(KR[s], c0, [[TK(s), 33], [1, c1 - c0]]), [('KR', s)], [('KTr', g)])
        for h in range(NH):
            for g in range(NSEG):
                c0, c1 = segb[g] * 128, segb[g + 1] * 128
                dma(KTt[0:64, c0:c1], dap(KN[s], h * 64 * TK(s) + c0, [[TK(s), 64], [1, c1 - c0]]), [('KN', s)], [('KTn', g)])
                b0, b1 = segb[g], segb[g + 1]
                dma(Vh[:, b0:b1, :], dap(VS[s], (h * 128 * nb + b0) * 65, [[nb * 65, 128], [65, b1 - b0], [1, 65]]),
                    [('VS', s)], [('Vh', g)])
            dma(QTh[:, :], dap(QT[s], h * 97 * nq, [[nq, 97], [1, nq]]), [('QT', s)], ['QTh'])
            if s == 0 and h == 1:
                prefetch_b4a_weights()
            for qb in range(nqb):
                pob = 6 + (qb % 2)
                po = PS[pob]; pok = ('ps', pob)
                groups = [(kb0, min(2, nb - kb0)) for kb0 in range(0, nb, 2)]

                def emit_S(gi):
                    kb0, n2 = groups[gi]
                    sb0 = (gi % 3) * 2
                    for j in range(n2):
                        kb = kb0 + j
                        g = seg_of[kb]
                        mm(PS[sb0 + j][:, 0:512], KTt[:, kb * 128:(kb + 1) * 128], QTh[:, qb * 512:(qb + 1) * 512], True, True,
                           [('KTr', g), ('KTn', g), 'QTh'], [('ps', sb0 + j)])
                    act(PT[:, gi % 3, 0:n2 * 512], PSA[:, sb0 * 512:(sb0 + n2) * 512], AF.Exp,
                        [('ps', sb0 + j) for j in range(n2)], [('PT', gi % 3)], scale=SM_SCALE)

                def emit_PV(gi):
                    kb0, n2 = groups[gi]
                    for j in range(n2):
                        kb = kb0 + j
                        g = seg_of[kb]
                        mm(po[0:65, 0:512], Vh[:, kb, :], PT[:, gi % 3, j * 512:(j + 1) * 512], kb == 0, kb == nb - 1,
                           [('Vh', g), ('PT', gi % 3)], [pok])

                emit_S(0)
                emit_S(1)
                for gi in range(len(groups)):
                    if gi + 2 < len(groups):
                        emit_S(gi + 2)
                    emit_PV(gi)
                asl = qb % 2
                S.op('dve', lambda e, po=po, asl=asl: e.reciprocal(RR[64:65, asl, :], po[64:65, 0:512]), [pok], [('RR', asl)])
                cp('dve', AO[:, asl, :], po[0:64, 0:512], [pok], [('AO', asl)])
                dma(dap(AOT[s], h * 64 * nq + qb * 512, [[nq, 64], [1, 512]]), AO[:, asl, :], [('AO', asl)], [('AOT', s)])
                dma(dap(RIV[s], h * nq + qb * 512, [[nq, 1], [1, 512]]), RR[64:65, asl, :], [('RR', asl)], [('RIV', s)])
                for _ in range(3):
                    if b4a_jobs:
                        b4a_jobs.pop(0)()
        if DEBUG and s == 1:
            dma(dbg['aot'].ap(), AOT[s].ap(), [('AOT', s)], ['dbg7'])


    def ffn_phases():
        o2 = O_PERSIST
        WG, WPA, WPB, WO = b4a_weights()
        XT4, o2 = carve(o2, [128, 4, 1024], F32)
        XN4, o2 = carve(o2, [128, 2, 1024], BF)
        XNT4, o2 = carve(o2, [128, 8, 512], BF)
        AOl, o2 = carve(o2, [128, 4, 512], BF)
        RBl, o2 = carve(o2, [128, 4, 512], F32)
        MOl4, o2 = carve(o2, [128, 4, 512], BF)
        MRG, o2 = carve(o2, [128, 8, 512], BF)
        H1t, o2 = carve(o2, [128, 1, 1024], F32)
        X2N, o2 = carve(o2, [128, 2, 1024], BF)
        X2Ts, o2 = carve(o2, [128, 1, 8, 128], BF)
        SGA, o2 = carve(o2, [128, 512], F32)
        SGB, o2 = carve(o2, [128, 512], F32)
        T1, o2 = carve(o2, [128, 512], F32)
        assert o2 <= O_B4W, o2
        assert not b4a_jobs
        bG = bG_t
        S4 = S4_t
        assert o2 <= 34 * 1024 + 3072, o2
        for s in range(NSEQ):
            nq = NLOC[s] * 512
            for i in range(NLOC[s]):
                for t in range(4):
                    dma(XT4[:, t, :], dap(xloc[s], (i * 512 + t * 128) * D, [[D, 128], [1, D]]), [], [('XT4', t)])
                    act(XN4[:, t % 2, :], XT4[:, t, :], AF.Square, [('XT4', t)], [('XN4', t % 2), ('S4', t)], accum=S4[:, t:t + 1])
                    act(S4[:, 4 + t:5 + t], S4[:, t:t + 1], AF.Sqrt, [('S4', t), 'epst'], [('S4b', t)], bias=epst[:, 0:1], scale=1.0 / D)
                    S.op('dve', lambda e, t=t: e.reciprocal(S4[:, 4 + t:5 + t], S4[:, 4 + t:5 + t]), [('S4b', t)], [('S4b', t)])
                    ts('dve', XN4[:, t % 2, :], XT4[:, t, :], S4[:, 4 + t:5 + t], None, ALU.mult, None,
                       [('XT4', t), ('S4b', t)], [('XN4', t % 2)])
                    ps, pk = psum()
                    psb = ps.bitcast(BF)
                    for kt in range(8):
                        tr(psb[:, kt * 128:(kt + 1) * 128], XN4[:, t % 2, kt * 128:(kt + 1) * 128], [('XN4', t % 2), 'Cb'], [pk])
                    cp('act', XNT4[:, :, t * 128:(t + 1) * 128], psb[:, 0:1024].rearrange("p (k t) -> p k t", k=8), [pk], ['XNT4'])
                for j in range(4):
                    dma(AOl[:, j, :], dap(AOT[s], j * 128 * nq + i * 512, [[nq, 128], [1, 512]]), [('AOT', s)], ['AOl'])
                    for hh in range(2):
                        dma(RBl[hh * 64:(hh + 1) * 64, j, :], dap(RIV[s], (2 * j + hh) * nq + i * 512, [[0, 64], [1, 512]]),
                            [('RIV', s)], ['RBl'])
                    dma(MOl4[:, j, :], dap(MOT[s], j * 128 * nq + i * 512, [[nq, 128], [1, 512]]), [('MOT', s)], ['MOl4'])
                tt('dve', AOl[:, :, :], AOl[:, :, :], RBl[:, :, :], ALU.mult, ['AOl', 'RBl'], ['AOl'])
                for c in range(8):
                    pa, pka = psum()
                    for k in range(4):
                        mm(pa[:, 0:512], WPA[:, k, c * 128:(c + 1) * 128], AOl[:, k, :], k == 0, k == 3, ['WPA', 'AOl'], [pka])
                    pb, pkb = psum()
                    for k in range(4):
                        mm(pb[:, 0:512], WPB[:, k, c * 128:(c + 1) * 128], MOl4[:, k, :], k == 0, k == 3, ['WPB', 'MOl4'], [pkb])
                    ga, pkga = psum()
                    for k in range(8):
                        mm(ga[:, 0:512], WG[:, k, c * 128:(c + 1) * 128], XNT4[:, k, :], k == 0, k == 7, ['WG', 'XNT4'], [pkga])
                    gb, pkgb = psum()
                    for k in range(8):
                        mm(gb[:, 0:512], WG[:, k, 1024 + c * 128:1024 + (c + 1) * 128], XNT4[:, k, :], k == 0, k == 7, ['WG', 'XNT4'], [pkgb])
                    act(SGA[:, :], ga[:, 0:512], AF.Sigmoid, [pkga, 'bG'], ['SGA'], bias=bG[:, c:c + 1])
                    act(SGB[:, :], gb[:, 0:512], AF.Sigmoid, [pkgb, 'bG'], ['SGB'], bias=bG[:, 8 + c:9 + c])
                    tt('dve', T1[:, :], pa[:, 0:512], SGA[:, :], ALU.mult, [pka, 'SGA'], ['T1'])
                    tt('dve', SGB[:, :], pb[:, 0:512], SGB[:, :], ALU.mult, [pkb, 'SGB'], ['SGB'])
                    tt('pool', MRG[:, c, :], T1[:, :], SGB[:, :], ALU.add, ['T1', 'SGB'], ['MRG'])
                for t in range(4):
                    hs = t % 2
                    for half in range(2):
                        ps, pk = psum()
                        for k in range(8):
                            mm(ps[:, 0:512], MRG[:, k, t * 128:(t + 1) * 128], WO[:, k, half * 512:(half + 1) * 512], k == 0, k == 7,
                               ['MRG', 'WO'], [pk])
                        tt('dve', H1t[:, 0, half * 512:(half + 1) * 512], ps[:, 0:512], XT4[:, t, half * 512:(half + 1) * 512], ALU.add,
                           [pk, ('XT4', t)], ['H1t'])
                    dma(dap(H1[s], (i * 512 + t * 128) * D, [[D, 128], [1, D]]), H1t[:, 0, :], ['H1t'], [('H1', s)])
                    act(X2N[:, hs, :], H1t[:, 0, :], AF.Square, ['H1t'], [('X2N', hs), ('S4c', hs)], accum=S4[:, 8 + hs:9 + hs])
                    act(S4[:, 10 + hs:11 + hs], S4[:, 8 + hs:9 + hs], AF.Sqrt, [('S4c', hs), 'epst'], [('S4d', hs)], bias=epst[:, 0:1], scale=1.0 / D)
                    S.op('dve', lambda e, hs=hs: e.reciprocal(S4[:, 10 + hs:11 + hs], S4[:, 10 + hs:11 + hs]), [('S4d', hs)], [('S4d', hs)])
                    ts('dve', X2N[:, hs, :], H1t[:, 0, :], S4[:, 10 + hs:11 + hs], None, ALU.mult, None, ['H1t', ('S4d', hs)], [('X2N', hs)])
                    ps, pk = psum()
                    psb = ps.bitcast(BF)
                    for kt in range(8):
                        tr(psb[:, kt * 128:(kt + 1) * 128], X2N[:, hs, kt * 128:(kt + 1) * 128], [('X2N', hs), 'Cb'], [pk])
                    cp('act', X2Ts[:, 0, :, :], psb[:, 0:1024].rearrange("p (k t) -> p k t", k=8), [pk], ['X2Ts'])
                    dma(dap(X2T[s], i * 512 + t * 128, [[nq, 128], [128 * nq, 8], [1, 128]]), X2Ts[:, 0, :, :], ['X2Ts'], [('X2T', s)])
        S.barrier()
        o2 = O_PERSIST
        WGU, o2 = carve(o2, [128, 8, 2 * D_FF], BF)
        X2l, o2 = carve(o2, [128, 2, 8, 512], BF)
        SIL, o2 = carve(o2, [128, 2, 512], F32)
        HTc, o2 = carve(o2, [128, 2, 512], BF)
        assert o2 <= 34 * 1024 + 3072, o2
        load_weight(WGU, w_gu_d, D, 2 * D_FF, g2, 'g2', 'WGU')
        it = 0
        for s in range(NSEQ):
            nq = NLOC[s] * 512
            for i in range(NLOC[s]):
                xsl = it % 2
                it += 1
                dma(X2l[:, xsl, :, :], dap(X2T[s], i * 512, [[nq, 128], [128 * nq, 8], [1, 512]]), [('X2T', s)], [('X2l', xsl)])
                for c in range(D_FF // 128):
                    sl = c % 2
                    pg, pkg = psum()
                    for k in range(8):
                        mm(pg[:, 0:512], WGU[:, k, c * 128:(c + 1) * 128], X2l[:, xsl, k, :], k == 0, k == 7, ['WGU', ('X2l', xsl)], [pkg])
                    pu, pku = psum()
                    for k in range(8):
                        mm(pu[:, 0:512], WGU[:, k, D_FF + c * 128:D_FF + (c + 1) * 128], X2l[:, xsl, k, :], k == 0, k == 7,
                           ['WGU', ('X2l', xsl)], [pku])
                    act(SIL[:, sl, :], pg[:, 0:512], AF.Silu, [pkg], [('SIL', sl)])
                    tt('dve', HTc[:, sl, :], pu[:, 0:512], SIL[:, sl, :], ALU.mult, [pku, ('SIL', sl)], [('HTc', sl)])
                    dma(dap(HT[s], c * 128 * nq + i * 512, [[nq, 128], [1, 512]]), HTc[:, sl, :], [('HTc', sl)], [('HT', s)])
        S.barrier()
        o2 = O_PERSIST
        WDN, o2 = carve(o2, [128, 22, 1024], BF)
        HTl, o2 = carve(o2, [128, 22, 512], BF)
        H1l, o2 = carve(o2, [128, 2, 1024], F32)
        YT, o2 = carve(o2, [128, 2, 1024], F32)
        YSQ, o2 = carve(o2, [128, 1024], F32)
        GF, o2 = carve(o2, [128, 1024], F32)
        S5, o2 = carve(o2, [128, 8], F32)
        assert o2 <= 34 * 1024 + 3072, o2
        load_weight(WDN, w_dn_d, D_FF, D, None, None, 'WDN')
        dma(GF[:, :], dap(gfin_d, 0, [[0, 128], [1, D]]), [], ['GF'])
        for s in range(NSEQ):
            nq = NLOC[s] * 512
            for i in range(NLOC[s]):
                dma(HTl[:, :, :], dap(HT[s], i * 512, [[nq, 128], [128 * nq, 22], [1, 512]]), [('HT', s)], ['HTl'])
                for t in range(4):
                    hs = t % 2
                    dma(H1l[:, hs, :], dap(H1[s], (i * 512 + t * 128) * D, [[D, 128], [1, D]]), [('H1', s)], [('H1l', hs)])
                    for half in range(2):
                        ps, pk = psum()
                        for k in range(22):
                            mm(ps[:, 0:512], HTl[:, k, t * 128:(t + 1) * 128], WDN[:, k, half * 512:(half + 1) * 512], k == 0, k == 21,
                               ['HTl', 'WDN'], [pk])
                        tt('dve', YT[:, hs, half * 512:(half + 1) * 512], ps[:, 0:512], H1l[:, hs, half * 512:(half + 1) * 512], ALU.add,
                           [pk, ('H1l', hs)], [('YT', hs)])
                    act(YSQ[:, :], YT[:, hs, :], AF.Square, [('YT', hs)], ['YSQ', ('S5', hs)], accum=S5[:, hs:hs + 1])
                    act(S5[:, 2 + hs:3 + hs], S5[:, hs:hs + 1], AF.Sqrt, [('S5', hs), 'epst'], [('S5b', hs)], bias=epst[:, 0:1], scale=1.0 / D)
                    S.op('dve', lambda e, hs=hs: e.reciprocal(S5[:, 2 + hs:3 + hs], S5[:, 2 + hs:3 + hs]), [('S5b', hs)], [('S5b', hs)])
                    stt(YT[:, hs, :], YT[:, hs, :], S5[:, 2 + hs:3 + hs], GF[:, :], ALU.mult, ALU.mult, [('YT', hs), ('S5b', hs), 'GF'], [('YT', hs)])
                    dma(dap(y[s], (i * 512 + t * 128) * D, [[D, 128], [1, D]]), YT[:, hs, :], [('YT', hs)], [('y', s)])

    kscale_ln = float(np.log(K_SCALE))

    for s in range(NSEQ):
        if stage < 1:
            break
        nsup = NSUP[s]
        nl = NLOC[s]
        nsteps = 1 + nsup
        dma(MSK[:, 0:nsteps * 4], mskd[s].ap(), [], ['MSK'])
        S.op('dve', lambda e: e.memset(ST_[:], 0.0), [], ['Cf', 'Bb'])
        S.op('dve', lambda e: e.memset(SM[:, 0:8], 0.0), [], ['Grun', 'Hrun'])
        S.op('pool', lambda e: e.memset(KRS[32:33, :], 1.0), [], ['KRS1'])

        def mcol(step, j):
            return MSK[:, step * 4 + j:step * 4 + j + 1]

        def load_x(step, what='both'):
            is_meta = (step == 0)
            ntt = 1 if is_meta else 4
            ntok = ntt * 128
            row0 = 0 if is_meta else 128 + (step - 1) * 512
            col0 = 0 if is_meta else (1 + 4 * (step - 1)) * 128
            xs = step % 2
            if what in ('x', 'both'):
                for t in range(ntt):
                    dma(XT[:, t, :], dap(xin[s], (row0 + t * 128) * D, [[D, 128], [1, D]]), [], [('XT', t)])
            if what in ('cs', 'both'):
                dma(CSK[0:32, xs, 0:ntok], dap(cosd[s], col0, [[TK(s), 32], [1, ntok]]), [], [('CSKc', xs)])
                dma(CSK[32:64, xs, 0:ntok], dap(sind[s], col0, [[TK(s), 32], [1, ntok]]), [], [('CSKs', xs)])

        def N_tile(step, t):
            xs = step % 2
            act(XN[:, t % 2, :], XT[:, t, :], AF.Square, [('XT', t)], [('XN', t % 2), ('SSQ', t)],
                accum=SSQ[:, t:t + 1])
            act(RST[:, t:t + 1], SSQ[:, t:t + 1], AF.Sqrt, [('SSQ', t), 'epst'], [('RST', t)],
                bias=epst[:, 0:1], scale=1.0 / D)
            S.op('dve', lambda e, t=t: e.reciprocal(RST[:, t:t + 1], RST[:, t:t + 1]),
                 [('RST', t)], [('RST', t)])
            if t % 2 == 0:
                ts('dve', XN[:, t % 2, :], XT[:, t, :], RST[:, t:t + 1], None,
                   ALU.mult, None, [('XT', t), ('RST', t)], [('XN', t % 2)])
            else:
                act(XN[:, t % 2, :], XT[:, t, :], AF.Copy, [('XT', t), ('RST', t)], [('XN', t % 2)], scale=RST[:, t:t + 1])
            ps, pk = psum()
            psb = ps.bitcast(BF)
            for kt in range(8):
                tr(psb[:, kt * 128:(kt + 1) * 128], XN[:, t % 2, kt * 128:(kt + 1) * 128], [('XN', t % 2), 'Cb'], [pk])
            cp('act', XNT[:, xs, :, t * 128:(t + 1) * 128],
               psb[:, 0:1024].rearrange("p (k t) -> p k t", k=8), [pk], [('XNT', xs)])

        pending = []

        def once(f):
            f()
            if False:
                yield

        def pump(n=1):
            for _ in range(n):
                if not pending:
                    return
                g = pending.pop(0)
                try:
                    next(g)
                    pending.append(g)
                except StopIteration:
                    pass

        def tail_gen(step):
            is_meta = (step == 0)
            ntt = 1 if is_meta else 4
            ntok = ntt * 128
            kb0 = 0 if is_meta else 1 + 4 * (step - 1)
            col0 = kb0 * 128
            xs = step % 2
            ps2, pk2 = psum()
            mm(ps2[:, 0:ntok], ones_b, SQ[:, xs, 0:ntok], True, True, [('SQ', xs), 'Cb'], [pk2])
            act(LNT[:, 0:ntok], ps2[:, 0:ntok], AF.Ln, [pk2, 'epst'], ['LNT'], bias=epst[:, 0:1], scale=1.0 / 128)
            act(RBC[:, 0:ntok], LNT[:, 0:ntok], AF.Exp, ['LNT'], ['RBC'], scale=-0.5)
            yield
            ps3, pk3 = psum()
            for t in range(ntt):
                mm(ps3[:, t:t + 1], SQ[:, xs, t * 128:(t + 1) * 128], ones_b[:, 0:1], True, True, [('SQ', xs), 'Cb'], [pk3])
            act(SSQ[:, 4:4 + ntt], ps3[:, 0:ntt], AF.Sqrt, [pk3, 'epst'], ['RSV'], bias=epst[:, 0:1], scale=1.0 / 128)
            S.op('dve', lambda e: e.reciprocal(RST[:, 4:4 + ntt], SSQ[:, 4:4 + ntt]), ['RSV'], ['RSV2'])
            yield
            for hp in range(4):
                ps, pk = psum()
                mm(ps[:, 0:ntok], WUKV[:, hp * 128:(hp + 1) * 128], CKV[:, xs, 0:ntok], True, True, ['WUKV', ('CKV', xs)], [pk])
                tt('dve', KNS[:, hp, 0:ntok], ps[:, 0:ntok], RBC[:, 0:ntok], ALU.mult, [pk, 'RBC'], [('KNS', hp)])
                dma(dap(KN[s], hp * 128 * TK(s) + col0, [[TK(s), 128], [1, ntok]]), KNS[:, hp, 0:ntok],
                    [('KNS', hp)], [('KN', s)])
                yield
            for t in range(ntt):
                ps, pk = psum()
                mm(ps[:, 0:512], CKV[:, xs, t * 128:(t + 1) * 128], WUKV[:, 512:1024], True, True, ['WUKV', ('CKV', xs)], [pk])
                if is_meta:
                    ts('dve', RST[:, 4:5], RST[:, 4:5], vmeta[:, 0:1], None, ALU.mult, None, ['RSV2', 'vmeta'], ['RSV2'])
                ts('dve', VST[:, t, :, 0:64], ps[:, 0:512].rearrange("p (h d) -> p h d", h=8), RST[:, 4 + t:5 + t],
                   None, ALU.mult, None, [pk, 'RSV2'], [('VST', t)])
                if is_meta:
                    cp('pool', VST[:, t, :, 64:65], vmeta[:, 0:1].unsqueeze(1).to_broadcast([128, 8, 1]), ['vmeta'], [('VST', t)])
                else:
                    S.op('pool', lambda e, t=t: e.memset(VST[:, t, :, 64:65], 1.0), [], [('VST', t)])
                yield
            for h in range(NH):
                nb = TK(s) // 128
                dma(dap(VS[s], (h * 128 * nb + kb0) * 65, [[nb * 65, 128], [65, ntt], [1, 65]]),
                    VST[:, 0:ntt, h, :], [('VST', t) for t in range(ntt)], [('VS', s)])
                if h % 2 == 1:
                    yield


        def A_step(step):
            is_meta = (step == 0)
            ntt = 1 if is_meta else 4
            ntok = ntt * 128
            row0 = 0 if is_meta else 128 + (step - 1) * 512
            kb0 = 0 if is_meta else 1 + 4 * (step - 1)
            col0 = kb0 * 128
            local = (1 <= step <= nl)
            lcol0 = (step - 1) * 512
            xs = step % 2
            rs = step % 3
            xk = ('XNT', xs)
            if step >= 1:
                pending.append(tail_gen(step - 1))
            if step >= 2:
                pending.append(M_step(step - 2))

            def proj_fm(c0, m, n0=0, nn=None):
                nn_ = ntok if nn is None else nn
                ps, pk = psum()
                for kt in range(8):
                    mm(ps[0:m, 0:nn_], WA[:, kt, c0:c0 + m], XNT[:, xs, kt, n0:n0 + nn_], kt == 0, kt == 7,
                       ['WA', xk], [pk])
                return ps, pk

            need_q = is_meta or local or step == nl + 1
            for h in range(8 if need_q else 4):
                if h < 4:
                    ps, pk = proj_fm(192 + h * 128, 128)
                    bias_ap = bAm[:, h:h + 1]; bk = 'bAm'
                else:
                    ps, pk = psum()
                    for kt in range(8):
                        mm(ps[:, 0:ntok], WL[:, kt, 256 + (h - 4) * 128:256 + (h - 3) * 128], XNT[:, xs, kt, 0:ntok],
                           kt == 0, kt == 7, ['WL', xk], [pk])
                    bias_ap = bLq[:, 2 + h - 4:3 + h - 4]; bk = 'bLq'
                act(MKr[:, rs, h, 1:1 + ntok], ps[:, 0:ntok], AF.Identity, [pk, bk], [('MK', rs)], bias=bias_ap)
                pump(2)
            nh_ = 8 if need_q else 4
            if is_meta:
                cp('dve', SM[:, 16:24], MKr[:, rs, :, 1], [('MK', rs)], ['pre'])
                cp('dve', SM[:, 8:16], MKr[:, rs, :, 1 + 126], [('MK', rs)], ['meta15'])
                S.op('pool', lambda e: e.memset(MKr[:, rs, :, 0:1 + META_LO], 0.0), ['pre'], [('MK', rs)])
            else:
                if step == 1:
                    cp('dve', MKr[:, rs, :, 0], SM[:, 16:24], ['pre'], [('MK', rs)])
                    cp('dve', SM[:, 24:32], MKr[:, rs, :, 1], [('MK', rs)], ['first'])
                else:
                    po = (step - 1) % 3
                    ts('dve', MW[:, 0:1], mcol(step, 2), -1.0, 1.0, ALU.mult, ALU.add, ['MSK'], ['MW0'])
                    ts('dve', GT[:, 0, 0:8], MKr[:, po, :, 512], mcol(step, 2), None, ALU.mult, None,
                       [('MK', po), 'MSK'], ['GT0'])
                    stt(MKr[:, rs, 0:nh_, 0], SM[:, 8:8 + nh_], MW[:, 0:1], GT[:, 0, 0:nh_], ALU.mult, ALU.add,
                        ['meta15', 'MW0', 'GT0'], [('MK', rs)])
            ps, pk = proj_fm(0, 128)
            act(CKV[:, xs, 0:ntok], ps[:, 0:ntok], AF.Identity, [pk, 'bA'], [('CKV', xs)], bias=bA[:, 0:1])
            act(SQ[:, xs, 0:ntok], ps[:, 0:ntok], AF.Square, [pk, 'bA'], [('SQ', xs)], bias=bA[:, 0:1])
            pump(3)
            ps, pk = proj_fm(128, 64)
            stt(KT1[:, 0:ntok], ps[0:32, 0:ntok], bA[0:32, 1:2], CSK[0:32, xs, 0:ntok], ALU.add, ALU.mult,
                [pk, 'bA', ('CSKc', xs)], ['KT1'])
            stt(KT2[:, 0:ntok], ps[32:64, 0:ntok], bA[32:64, 1:2], CSK[32:64, xs, 0:ntok], ALU.add, ALU.mult,
                [pk, 'bA', ('CSKs', xs)], ['KT2'])
            tt('dve', KRS[0:32, 0:ntok], KT1[:, 0:ntok], KT2[:, 0:ntok], ALU.add, ['KT1', 'KT2'], ['KRS'])
            dma(dap(KR[s], col0, [[TK(s), 33], [1, ntok]]), KRS[:, 0:ntok], ['KRS', 'KRS1'], [('KR', s)])
            if step + 2 < nsteps:
                load_x(step + 2, 'cs')
            pump(4)
            for t in range(ntt):
                ps, pk = psum()
                for kt in range(8):
                    mm(ps[:, 0:512], XNT[:, xs, kt, t * 128:(t + 1) * 128], WA[:, kt, 704:1216], kt == 0, kt == 7,
                       ['WA', xk], [pk])
                tt('dve', VR[:, rs, t, :], ps[:, 0:512], bbc[:, 0:512], ALU.add, [pk, 'bbc'], [('VR', rs)])
                pump(5)
            ps, pk = psum()
            for t in range(ntt):
                for kt in range(8):
                    mm(ps[:, t * 16:(t + 1) * 16], XNT[:, xs, kt, t * 128:(t + 1) * 128], WA[:, kt, 1216:1232],
                       kt == 0, kt == 7, ['WA', xk], [pk])
            tt('dve', GR[:, rs, 0:ntt, :], ps[:, 0:ntt * 16].rearrange("p (t g) -> p t g", t=ntt),
               bbc[:, 512:528].unsqueeze(1).to_broadcast([128, ntt, 16]), ALU.add, [pk, 'bbc'], [('GR', rs)])
            pump(5)
            while pending:
                pump()
            if local:
                dma(dap(LG[s], (step - 1) * 4 * 16, [[NLOC[s] * 4 * 16, 128], [1, 64]]),
                    GR[:, rs, :, :].rearrange("p t g -> p (t g)"), [('GR', rs)], [('LG', s)])
                A_local(step, xs, rs, xk, lcol0, col0)

        def A_local(step, xs, rs, xk, lcol0, col0):
            nq = NLOC[s] * 512
            ps2, pk2 = psum()
            for j in range(2):
                ps, pk = psum()
                for kt in range(8):
                    mm(ps[:, 0:512], WL[:, kt, j * 128:(j + 1) * 128], XNT[:, xs, kt, :], kt == 0, kt == 7, ['WL', xk], [pk])
                act(CQT[:, j, :], ps[:, 0:512], AF.Identity, [pk, 'bLq'], [('CQT', j)], bias=bLq[:, j:j + 1])
                act(SQ[:, 1 - xs, :], ps[:, 0:512], AF.Square, [pk, 'bLq'], [('SQ', 1 - xs)], bias=bLq[:, j:j + 1])
                mm(ps2[:, 0:512], ones_b, SQ[:, 1 - xs, :], j == 0, j == 1, [('SQ', 1 - xs), 'Cb'], [pk2])
            act(LNT[:, :], ps2[:, 0:512], AF.Ln, [pk2, 'epst'], ['LNT'], bias=epst[:, 0:1], scale=1.0 / 256)
            act(RBC[:, :], LNT[:, :], AF.Exp, ['LNT'], ['RBC'], scale=-0.5)
            dma(CSQ[64:96, 0, :], dap(cosd[s], col0, [[TK(s), 32], [1, 512]]), [], ['CSQc'])
            dma(CSQ[64:96, 1, :], dap(sind[s], col0, [[TK(s), 32], [1, 512]]), [], ['CSQs'])
            tt('pool', CSQ[64:96, 0, :], CSQ[64:96, 0, :], RBC[64:96, :], ALU.mult, ['CSQc', 'RBC'], ['CSQc'])
            tt('pool', CSQ[64:96, 1, :], CSQ[64:96, 1, :], RBC[64:96, :], ALU.mult, ['CSQs', 'RBC'], ['CSQs'])
            S.op('pool', lambda e: e.memset(QTS[96:97, :, :], 0.0), [], ['QTS96'])
            for h in range(NH):
                ps, pk = psum()
                for j in range(2):
                    mm(ps[:, 0:512], WUQ[:, j, h * 128:(h + 1) * 128], CQT[:, j, :], j == 0, j == 1,
                       ['WUQ', ('CQT', 0), ('CQT', 1)], [pk])
                tt('dve', QTS[0:64, h % 2, :], ps[0:64, 0:512], RBC[0:64, :], ALU.mult, [pk, 'RBC'], [('QTS', h % 2)])
                tt('dve', KT1[:, :], ps[64:96, 0:512], CSQ[64:96, 0, :], ALU.mult, [pk, 'CSQc'], ['KT1'])
                tt('dve', KT2[:, :], ps[96:128, 0:512], CSQ[64:96, 1, :], ALU.mult, [pk, 'CSQs'], ['KT2'])
                tt('pool', QTS[64:96, h % 2, :], KT1[:, :], KT2[:, :], ALU.add, ['KT1', 'KT2'], [('QTS', h % 2)])
                dma(dap(QT[s], h * 97 * nq + lcol0, [[nq, 97], [1, 512]]), QTS[:, h % 2, :], [('QTS', h % 2), 'QTS96'], [('QT', s)])
            for t in range(4):
                ps, pk = psum()
                for kt in range(8):
                    mm(ps[:, 0:512], XNT[:, xs, kt, t * 128:(t + 1) * 128], WL[:, kt, 768:1280], kt == 0, kt == 7,
                       ['WL', xk], [pk])
                tt('dve', CV[:, 0, :], ps[:, 0:512], bbc[:, 528:1040], ALU.add, [pk, 'bbc'], [('CV', 0)])
                act(MOS[:, t % 2, :], CV[:, 0, :], AF.Sigmoid, [('CV', 0)], [('MOS', t % 2)])
                dma(dap(LMO[s], ((step - 1) * 4 + t) * 512, [[NLOC[s] * 4 * 512, 128], [1, 512]]),
                    MOS[:, t % 2, :], [('MOS', t % 2)], [('LMO', s)])

        def M_step(step, deferred_meta=False):
            if False:
                yield
            is_meta = (step == 0)
            ntt = 1 if is_meta else 4
            ntok = ntt * 128
            xs = step % 3
            local = (1 <= step <= nl)
            nh_ = 8 if (is_meta or local) else 4
            lcol0 = (step - 1) * 512
            if not is_meta:
                if step == nsup:
                    src = SM[:, 24:24 + nh_]; sk = 'first'
                else:
                    src = MKr[:, (step + 1) % 3, 0:nh_, 1]; sk = ('MK', (step + 1) % 3)
                ts('dve', MKr[:, xs, 0:nh_, 513], src, mcol(step, 3), None, ALU.mult, None, [sk, 'MSK'], [('MK', xs)])
            if not is_meta or not deferred_meta:
                for h in range(nh_):
                    x0 = MKr[:, xs, h, 0:ntok]; x1 = MKr[:, xs, h, 1:1 + ntok]; x2 = MKr[:, xs, h, 2:2 + ntok]
                    acc = CV[:, h % 2, 0:ntok]
                    ck = ('CV', h % 2)
                    ts('dve', acc, x1, cw[:, h * 3 + 1:h * 3 + 2], cb[:, h:h + 1], ALU.mult, ALU.add,
                       [('MK', xs), 'cw', 'cb'], [ck])
                    stt(acc, x0, cw[:, h * 3:h * 3 + 1], acc, ALU.mult, ALU.add, [('MK', xs), 'cw', ck], [ck])
                    stt(acc, x2, cw[:, h * 3 + 2:h * 3 + 3], acc, ALU.mult, ALU.add, [('MK', xs), 'cw', ck], [ck])
                    act(KTb[:, h, 0:ntok], acc, AF.Silu, [ck], [('KTb', h)])
                    yield
                for t in range(ntt):
                    ps, pk = psum()
                    psb = ps.bitcast(BF)
                    for h in range(4):
                        tr(psb[:, h * 128:(h + 1) * 128], KTb[:, h, t * 128:(t + 1) * 128], [('KTb', h), 'Cb'], [pk])
                    kdst = KTK[:, t, :] if not is_meta else KTKm[:, :]
                    cp('act', kdst, psb[:, 0:512], [pk], [('KTK', t) if not is_meta else 'KTKm'])
                    yield
            if local:
                nq = NLOC[s] * 512
                for h in range(4):
                    dma(dap(LK[s], h * 128 * nq + lcol0, [[nq, 128], [1, 512]]), KTb[:, h, :], [('KTb', h)], [('LK', s)])
                    dma(dap(LQ[s], h * 128 * nq + lcol0, [[nq, 128], [1, 512]]), KTb[:, 4 + h, :], [('KTb', 4 + h)], [('LQ', s)])
                dma(dap(LKT[s], (step - 1) * 4 * 512, [[NLOC[s] * 4 * 512, 128], [1, 2048]]),
                    KTK[:, :, :].rearrange("p t c -> p (t c)"), [('KTK', t) for t in range(4)], [('LKT', s)])
                for t in range(4):
                    cp('pool', VAL[:, t, :].rearrange("p (h d) -> p h d", h=4)[:, :, 0:128],
                       VR[:, xs, t, :].rearrange("p (h d) -> p h d", h=4), [('VR', xs)], [('VAL', t)])
                    S.op('pool', lambda e, t=t: e.memset(VAL[:, t, :].rearrange("p (h d) -> p h d", h=4)[:, :, 128:129], 1.0),
                         [], [('VAL', t)])
                dma(dap(LV[s], (step - 1) * 4 * 516, [[NLOC[s] * 4 * 516, 128], [1, 4 * 516]]),
                    VAL[:, :, :].rearrange("p t c -> p (t c)"), [('VAL', t) for t in range(4)], [('LV', s)])
                return
            if is_meta and not deferred_meta:
                cp('pool', VRm[:, :], VR[:, xs, 0, :], [('VR', xs)], ['VRm'])
                cp('pool', GRm[:, :], GR[:, xs, 0, :], [('GR', xs)], ['GRm'])
                return
            if is_meta:
                G = GRm[:, :].unsqueeze(1)
                gk = 'GRm'
            else:
                G = GR[:, xs, :, :]
                gk = ('GR', xs)
            nt = ntt
            G4 = G.rearrange("p t (d j h) -> p t d j h", d=2, j=2)
            fpre = G4[:, :, :, 1, :]
            ipre = G4[:, :, :, 0, :]
            E1 = GT[:, 0, :].rearrange("p (a b) -> p a b", a=2)[:, :, :]
            A_ = GT[:, 1:1 + nt, 0:8].rearrange("p t (d h) -> p t d h", d=2)
            T_ = GT[:, 5:5 + nt, 0:8].rearrange("p t (d h) -> p t d h", d=2)
            act(T_, fpre, AF.Exp, [gk], ['T_'], scale=-1.0)
            act(T_, T_, AF.Ln, ['T_', 'onet'], ['T_'], bias=onet[:, 0:1])
            if is_meta:
                ts('dve', MW[:, 2:3], vmeta[:, 0:1], -1.0, None, ALU.mult, None, ['vmeta'], ['MW2'])
                ts('dve', MW[:, 3:4], vmeta[:, 0:1], 0.0, None, ALU.mult, None, ['vmeta'], ['MW3'])
                wm0 = vmeta[:, 0:1]
            else:
                ts('dve', MW[:, 2:3], mcol(step, 0), -1.0, None, ALU.mult, None, ['MSK'], ['MW2'])
                ts('dve', MW[:, 3:4], mcol(step, 1), -1.0, None, ALU.mult, None, ['MSK'], ['MW3'])
            for d_ in range(2):
                ts('dve', A_[:, :, d_, :], T_[:, :, d_, :], MW[:, 2 + d_:3 + d_], None, ALU.mult, None,
                   ['T_', 'MW2', 'MW3'], ['A_'])
            AH = AHL[:, 0, 0:nt, :]; AL = AHL[:, 1, 0:nt, :]
            cp('dve', AH, GT[:, 1:1 + nt, 0:8], ['A_'], ['AH'])
            tt('dve', AL, GT[:, 1:1 + nt, 0:8], AH, ALU.subtract, ['A_', 'AH'], ['AL'])
            ps, pk = psum()
            for t in range(nt):
                for (pp, kk, first) in ((AH, 'AH', True), (AL, 'AL', False)):
                    mm(ps[:, t * 16:t * 16 + 4], triU_b, pp[:, t, 0:4], first, not first, ['Cb', kk], [pk])
                for (pp, kk, first) in ((AH, 'AH', True), (AL, 'AL', False)):
                    mm(ps[:, t * 16 + 4:t * 16 + 8], triL_b, pp[:, t, 4:8], first, not first, ['Cb', kk], [pk])
                for (pp, kk, first) in ((AH, 'AH', True), (AL, 'AL', False)):
                    mm(ps[:, t * 16 + 8:t * 16 + 16], ones_b, pp[:, t, 0:8], first, not first, ['Cb', kk], [pk])
            CS = GT[:, 9:9 + nt, :]
            cp('dve', CS, ps[:, 0:nt * 16].rearrange("p (t c) -> p t c", t=nt), [pk], ['CS'])
            yield
            SFX = GT[:, 13, :].rearrange("p (t h) -> p t h", t=4)
            PFX = GT[:, 14, :].rearrange("p (t h) -> p t h", t=4)
            S.op('dve', lambda e: e.memset(GT[:, 13:15, :], 0.0), [], ['SFX', 'PFX'])
            if is_meta:
                cp('dve', SFX[:, 0, :], SM[:, 4:8], ['Hrun'], ['SFX'])
            else:
                for t in range(nt - 2, -1, -1):
                    tt('dve', SFX[:, t, :], SFX[:, t + 1, :], CS[:, t + 1, 8:12], ALU.add, ['SFX', 'CS'], ['SFX'])
                cp('dve', PFX[:, 0, :], SM[:, 0:4], ['Grun'], ['PFX'])
                for t in range(1, nt):
                    tt('dve', PFX[:, t, :], PFX[:, t - 1, :], CS[:, t - 1, 12:16], ALU.add, ['PFX', 'CS'], ['PFX'])
                tt('dve', SM[:, 0:4], PFX[:, nt - 1, :], CS[:, nt - 1, 12:16], ALU.add, ['PFX', 'CS'], ['Grun'])
                tt('dve', GT[:, 15, 0:4], SFX[:, 0, :], CS[:, 0, 8:12], ALU.add, ['SFX', 'CS'], ['FLS'])
                tt('dve', SM[:, 4:8], SM[:, 4:8], GT[:, 15, 0:4], ALU.add, ['Hrun', 'FLS'], ['Hrun'])
            ARG = GT[:, 5:5 + nt, 8:16].rearrange("p t (d h) -> p t d h", d=2)
            tt('dve', ARG[:, :, 0, :], ipre[:, :, 0, :], CS[:, :, 0:4], ALU.subtract, [gk, 'CS'], ['ARG'])
            tt('dve', ARG[:, :, 1, :], ipre[:, :, 1, :], CS[:, :, 4:8], ALU.subtract, [gk, 'CS'], ['ARG'])
            tt('dve', ARG[:, :, 0, :], ARG[:, :, 0, :], CS[:, :, 8:12], ALU.add, ['ARG', 'CS'], ['ARG'])
            tt('dve', ARG[:, :, 1, :], ARG[:, :, 1, :], CS[:, :, 12:16], ALU.add, ['ARG', 'CS'], ['ARG'])
            tt('dve', ARG[:, :, 0, :], ARG[:, :, 0, :], SFX[:, 0:nt, :], ALU.add, ['ARG', 'SFX'], ['ARG'])
            tt('dve', ARG[:, :, 1, :], ARG[:, :, 1, :], PFX[:, 0:nt, :], ALU.add, ['ARG', 'PFX'], ['ARG'])
            Wt = GT[:, 1:1 + nt, 8:16].rearrange("p t (d h) -> p t d h", d=2)
            yield
            act(Wt, ARG, AF.Exp, ['ARG', 'lnks'], ['Wt'], bias=lnks[:, 0:1])
            if is_meta:
                ts('dve', Wt[:, :, 0, :], Wt[:, :, 0, :], vmeta[:, 0:1], None, ALU.mult, None, ['Wt', 'vmeta'], ['Wt'])
            else:
                for d_ in range(2):
                    ts('dve', Wt[:, :, d_, :], Wt[:, :, d_, :], mcol(step, d_), None, ALU.mult, None, ['Wt', 'MSK'], ['Wt'])
            if not is_meta:
                act(GT[:, 15, 4:8], GT[:, 15, 0:4], AF.Exp, ['FLS'], ['EFL'])
            ndir = 1 if is_meta else 2
            for t in range(nt):
                for d_ in range(ndir):
                    vsrc = (VRm[:, :] if is_meta else VR[:, xs, t, :]).rearrange("p (h d) -> p h d", h=4)
                    vk = 'VRm' if is_meta else ('VR', xs)
                    wv = WV[:, t, d_, :].rearrange("p (h d) -> p h d", h=4)
                    tt('dve', wv[:, :, 0:128], vsrc,
                       Wt[:, t, d_, :].unsqueeze(2).to_broadcast([128, 4, 128]), ALU.mult, [vk, 'Wt'], [('WV', t, d_)])
                    cp('pool', wv[:, :, 128:129], Wt[:, t, d_, :].unsqueeze(2), ['Wt'], [('WV', t, d_)])
                    yield
            for d_ in range(ndir):
                for h in range(4):
                    ps, pk = psum()
                    for t in range(nt):
                        ksrc = KTKm[:, h * 128:(h + 1) * 128] if is_meta else KTK[:, t, h * 128:(h + 1) * 128]
                        kk = 'KTKm' if is_meta else ('KTK', t)
                        mm(ps[:, 0:129], ksrc, WV[:, t, d_, h * 129:(h + 1) * 129], t == 0, t == nt - 1,
                           [kk, ('WV', t, d_)], [pk])
                    if d_ == 0 and not is_meta:
                        stt(ST_[:, 0, h * 129:(h + 1) * 129], ST_[:, 0, h * 129:(h + 1) * 129], GT[:, 15, 4 + h:5 + h],
                            ps[:, 0:129], ALU.mult, ALU.add, ['Cf', 'EFL', pk], ['Cf'])
                    else:
                        key = 'Cf' if d_ == 0 else 'Bb'
                        tt('dve', ST_[:, d_, h * 129:(h + 1) * 129], ST_[:, d_, h * 129:(h + 1) * 129], ps[:, 0:129],
                           ALU.add, [key, pk], [key])
                    yield

        oo = o
        KTKm, oo = carve(oo, [128, 512], BF)
        VRm, oo = carve(oo, [128, 512], BF)
        GRm, oo = carve(oo, [128, 16], F32)
        assert oo <= 34 * 1024 + 3072

        def ntiles(step):
            return 1 if step == 0 else 4

        load_x(0)
        for t in range(ntiles(0)):
            N_tile(0, t)
        load_x(1)
        for step in range(nsteps):
            if step + 1 < nsteps:
                for t in range(ntiles(step + 1)):
                    pending.append(once(lambda st_=step + 1, t=t: N_tile(st_, t)))
                if step + 2 < nsteps:
                    pending.append(once(lambda st_=step + 2: load_x(st_, 'x')))
            A_step(step)
            while pending:
                pump()
        for _ in tail_gen(nsteps - 1):
            pass
        for _ in M_step(nsteps - 2):
            pass
        for _ in M_step(nsteps - 1):
            pass
        for _ in M_step(0, deferred_meta=True):
            pass
        if DEBUG and s == 1:
            dma(dbg['kn'].ap(), KN[s].ap(), [('KN', s)], ['dbg1'])
            dma(dbg['kr'].ap(), KR[s].ap(), [('KR', s)], ['dbg2'])
            dma(dbg['vs'].ap().rearrange("h p n c -> (h p) (n c)"), VS[s].ap().rearrange("h p n c -> (h p) (n c)"), [('VS', s)], ['dbg3'])
            dma(dbg['qt'].ap().rearrange("h r n -> (h r) n"), QT[s].ap().rearrange("h r n -> (h r) n"), [('QT', s)], ['dbg4'])
            dma(dbg['lq'].ap(), LQ[s].ap(), [('LQ', s)], ['dbg5'])
            dma(dbg['lk'].ap(), LK[s].ap(), [('LK', s)], ['dbg6'])
        if DEBUG:
            dma(dap(dbg['cf'], s * 128 * 516, [[516, 128], [1, 516]]), ST_[:, 0, :], ['Cf'], ['dbgcf'])
            dma(dap(dbg['bb'], s * 128 * 516, [[516, 128], [1, 516]]), ST_[:, 1, :], ['Bb'], ['dbgbb'])
        if stage < 2:
            continue
        S.barrier()
        mlstm_local(s)
        S.barrier()

    if stage >= 3:
        S.barrier()
        for s in range(NSEQ):
            attention(s)
    if stage >= 4:
        S.barrier()
        ffn_phases()

    S.emit(nc, st)
    st.close()
    return nc


def _rope_tables(pos):
    half = QK_ROPE // 2
    freqs = (10000.0 ** (-np.arange(half, dtype=np.float32) / half)).astype(np.float32)
    ang = pos.astype(np.float32)[None, :] * freqs[:, None]
    c = np.cos(ang).astype(np.float32)
    s_ = np.sin(ang).astype(np.float32)
    return np.concatenate([c, c], 0), np.concatenate([-s_, s_], 0)


def _consts():
    i = np.arange(128)
    ident = np.eye(128, dtype=np.float32)
    triU = (i[:, None] <= i[None, :]).astype(np.float32)
    triL = (i[:, None] >= i[None, :]).astype(np.float32)
    ones = np.ones((128, 128), np.float32)
    return np.concatenate([ident, triU, triL, ones, triU, triL], 1)


def prep_inputs(inp):
    f = lambda a: np.ascontiguousarray(np.asarray(a, dtype=np.float32))
    xs = [f(inp["x_prompt"])[0], f(inp["x_sample"])[0], f(inp["x_sample"])[1]]
    meta = f(inp["meta_tokens"])
    w_in = f(inp["w_in"])[0]; b_in = f(inp["b_in"])[0]
    o_cq, o_ckv, o_kr, o_mq, o_mk, o_mv, o_mo, o_g, o_ga, o_gb = np.cumsum([0, 256, 128, 32, 512, 512, 512, 512, 16, 1024])
    rot = np.concatenate([np.arange(16, 32), np.arange(0, 16)])
    cols_a = np.concatenate([np.arange(o_ckv, o_ckv + 128), o_kr + np.arange(32), o_kr + rot,
                             np.arange(o_mk, o_mk + 512), np.arange(o_mv, o_mv + 512), np.arange(o_g, o_g + 16)])
    cols_l = np.concatenate([np.arange(o_cq, o_cq + 256), np.arange(o_mq, o_mq + 512), np.arange(o_mo, o_mo + 512)])
    cols_g = np.arange(o_ga, o_ga + 2048)
    w_uq = f(inp["w_uq"])[0]
    cu = []
    for h in range(NH):
        b = h * QK_DIM
        cu += [b + np.arange(64), b + 64 + np.arange(32), b + 64 + rot]
    w_uq_e = np.ascontiguousarray(w_uq[:, np.concatenate(cu)])
    w_ukv = f(inp["w_ukv"])[0]
    ck = np.concatenate([h * 128 + np.arange(64) for h in range(NH)])
    cv = np.concatenate([h * 128 + 64 + np.arange(64) for h in range(NH)])
    w_ukv_e = np.ascontiguousarray(w_ukv[:, np.concatenate([ck, cv])])
    conv_w = f(inp["conv_w"])[0]; conv_b = f(inp["conv_b"])[0]
    cwt = np.zeros((128, 8, 3), np.float32); cbt = np.zeros((128, 8), np.float32)
    for h in range(4):
        cwt[:, h, :] = conv_w[:, 512 + h * 128:512 + (h + 1) * 128].T
        cwt[:, 4 + h, :] = conv_w[:, h * 128:(h + 1) * 128].T
        cbt[:, h] = conv_b[512 + h * 128:512 + (h + 1) * 128]
        cbt[:, 4 + h] = conv_b[h * 128:(h + 1) * 128]
    colmaj = lambda v, n: np.ascontiguousarray(f(v).reshape(n, 128).T)
    shared = {
        "cst": _consts(),
        "vmeta": ((np.arange(128) >= META_LO) & (np.arange(128) < META_HI)).astype(np.float32)[:, None].copy(),
        "w_a": np.ascontiguousarray(w_in[:, cols_a]), "b_a": np.ascontiguousarray(b_in[cols_a])[None],
        "w_l": np.ascontiguousarray(w_in[:, cols_l]), "b_l": np.ascontiguousarray(b_in[cols_l])[None],
        "w_g": np.ascontiguousarray(w_in[:, cols_g]), "b_g": np.ascontiguousarray(b_in[cols_g])[None],
        "w_uq": w_uq_e, "w_ukv": w_ukv_e,
        "w_pa": f(inp["w_pa"])[0], "w_pb": f(inp["w_pb"])[0], "w_o": f(inp["w_o"])[0],
        "w_gu": np.ascontiguousarray(np.concatenate([f(inp["w_ffn_gate"])[0], f(inp["w_ffn_up"])[0]], 1)),
        "w_dn": f(inp["w_ffn_down"])[0],
        "g1": colmaj(inp["norm1_g"], 8), "g2": colmaj(inp["norm2_g"], 8),
        "gq": colmaj(inp["q_norm_g"], 2), "gkv": colmaj(inp["kv_norm_g"], 1), "gm": colmaj(inp["m_norm_g"], 4),
        "gfin": f(inp["final_norm_g"])[None],
        "cw": cwt.reshape(128, 24), "cb": cbt,
    }
    in_maps = []
    for c in range(NCORE):
        m = dict(shared)
        for s in range(NSEQ):
            nsup, nl = NSUP[s], NLOC[s]
            L0 = c * nl
            order = [(L0 + i) % nsup for i in range(nsup)]
            x = xs[s]
            mt = np.zeros((128, D), np.float32)
            mt[0] = meta[15] if L0 == 0 else x[L0 * 512 - 1]
            mt[META_LO:META_HI] = meta
            mt[127] = x[0]
            xr = x.reshape(nsup, 512, D)[order].reshape(nsup * 512, D)
            m["xin%d" % s] = np.concatenate([mt, xr], 0)
            m["xloc%d" % s] = np.ascontiguousarray(x[L0 * 512:(L0 + nl) * 512])
            pos = np.zeros(TK(s), np.float32)
            pos[META_LO:META_HI] = np.arange(16)
            for i, su in enumerate(order):
                pos[128 + i * 512:128 + (i + 1) * 512] = 16 + su * 512 + np.arange(512)
            ct, sn = _rope_tables(pos)
            m["cos%d" % s] = ct; m["sin%d" % s] = sn
            mk = np.zeros((1 + nsup, 4), np.float32)
            mk[0] = [1, 0, 0, 0]
            for i, su in enumerate(order):
                st_ = i + 1
                before = su < L0
                after = su >= L0 + nl
                wl = 0.0 if su == 0 else 1.0
                wr = 0.0 if su == nsup - 1 else 1.0
                mk[st_] = [float(before), float(after), wl, wr]
            m["msk%d" % s] = np.ascontiguousarray(np.broadcast_to(mk.reshape(1, -1), (128, (1 + nsup) * 4)))
        in_maps.append(m)
    return in_maps


_NC_CACHE = {}


def kernel(**inputs):
    in_maps = prep_inputs(inputs)
    if 'nc' not in _NC_CACHE:
        _NC_CACHE['nc'] = build()
    nc = _NC_CACHE['nc']
    res = run_bass_kernel_spmd(nc, in_maps, core_ids=list(range(NCORE)))
    outs = []
    yp = np.concatenate([res.results[c]["y0"] for c in range(NCORE)], 0)[None]
    ys = np.stack([np.concatenate([res.results[c]["y%d" % s] for c in range(NCORE)], 0) for s in (1, 2)], 0)
    return (np.ascontiguousarray(yp.astype(np.float32)), np.ascontiguousarray(ys.astype(np.float32)))
```

```python
import numpy as np
from contextlib import ExitStack
import concourse.bass as bass
import concourse.mybir as mybir
from concourse.bass_utils import run_bass_kernel_spmd

F32 = mybir.dt.float32
BF = mybir.dt.bfloat16
AF = mybir.ActivationFunctionType
ALU = mybir.AluOpType
AX = mybir.AxisListType

NCORE = 8
D = 1024
P = 128
NSUP = (32, 16, 16)
NLOC = (4, 2, 2)
NSEQ = 3
QK_NOPE, QK_ROPE, V_HEAD, NH = 64, 32, 64, 8
QK_DIM = 96
MH, MD = 4, 128
D_FF = 2816
EPS = 1e-6
SM_SCALE = QK_DIM ** -0.5
K_SCALE = MD ** -0.5
META_LO, META_HI = 111, 127
NA = 128 + 64 + 512 + 512 + 16
NL_ = 256 + 512 + 512
DEBUG = False


def TK(s):
    return (1 + 4 * NSUP[s]) * 128


class Sched:
    LIMIT = 20000
    R = 24
    DMAQ = ('sp',)

    def __init__(self):
        self.ops = []
        self.lw = {}
        self.rd = {}
        self.bar = None

    def _last_ops(self):
        last = {}
        dl = {}
        for i, o in enumerate(self.ops):
            if o[0] in self.DMAQ:
                dl.setdefault(o[0], []).append(i)
            else:
                last[o[0]] = i
        s = set(last.values())
        for e, l in dl.items():
            s.update(l[-self.R:])
        return s

    def barrier(self):
        self.bar = (self._last_ops(), set())
        self.lw = {}
        self.rd = {}

    def op(self, eng, fn, r=(), w=()):
        i = len(self.ops)
        hard, soft = set(), set()
        if self.bar is not None and eng not in self.bar[1]:
            hard.update(self.bar[0])
            self.bar[1].add(eng)
        isd = eng in self.DMAQ
        for k in r:
            hard.update(self.lw.get(k, ()))
        for k in w:
            hard.update(self.lw.get(k, ()))
            for kk, v in self.rd.get(k, {}).items():
                if isinstance(v, list):
                    soft.update(v)
                else:
                    soft.add(v)
        self.ops.append([eng, fn, hard, soft])
        for k in r:
            d = self.rd.setdefault(k, {})
            if isd:
                d.setdefault(('dma', eng), []).append(i)
            else:
                d[eng] = i
        for k in w:
            if isd and not self.rd.get(k) and k in self.lw and all(self.ops[x][0] in self.DMAQ for x in self.lw[k]):
                self.lw[k] = set(self.lw[k]) | {i}
            else:
                self.lw[k] = {i}
            self.rd[k] = {}
        return i

    def emit(self, nc, stack):
        ops = self.ops
        n = len(ops)
        dmaq = self.DMAQ
        need = [False] * n
        deps = [None] * n
        for i, (eng, fn, hard, soft) in enumerate(ops):
            dl = {}
            dd = set()
            for d, is_hard in [(x, True) for x in hard] + [(x, False) for x in soft]:
                if d >= i:
                    continue
                e2 = ops[d][0]
                if e2 in dmaq:
                    dd.add(d)
                    continue
                if e2 == eng:
                    if eng == 'pe' or not is_hard:
                        continue
                if e2 not in dl or dl[e2] < d:
                    dl[e2] = d
            deps[i] = (dl, sorted(dd))
            for d in dl.values():
                need[d] = True
        sig = [None] * n
        cnt, ep = {}, {}
        dcount = {}
        for i, o in enumerate(ops):
            e = o[0]
            if e in dmaq:
                j = dcount.get(e, 0)
                dcount[e] = j + 1
                sig[i] = (e, 'd%d' % (j % self.R), j // self.R + 1)
                continue
            if not need[i]:
                continue
            c = cnt.get(e, 0) + 1
            if c > self.LIMIT:
                ep[e] = ep.get(e, 0) + 1
                c = 1
            cnt[e] = c
            sig[i] = (e, ep.get(e, 0), c)
        sems = {}
        for s in sig:
            if s is not None and (s[0], s[1]) not in sems:
                sems[(s[0], s[1])] = stack.enter_context(nc.semaphore("s_%s_%s" % (s[0], s[1])))
        per_eng = {}
        for i, o in enumerate(ops):
            per_eng.setdefault(o[0], []).append(i)
        final = self._last_ops()

        def run(eng_name, handle):
            waited = {}

            def wait_for(d):
                s = sig[d]
                key = (s[0], s[1])
                if waited.get(key, 0) >= s[2]:
                    return
                handle.wait_ge(sems[key], s[2] * (16 if s[0] in dmaq else 1))
                waited[key] = s[2]

            for i in per_eng.get(eng_name, []):
                dl, dd = deps[i]
                for e2, d in dl.items():
                    wait_for(d)
                for d in dd:
                    wait_for(d)
                if eng_name in dmaq:
                    s = sig[i]
                    if s[2] > 1:
                        key = (s[0], s[1])
                        if waited.get(key, 0) < s[2] - 1:
                            handle.wait_ge(sems[key], (s[2] - 1) * 16)
                            waited[key] = s[2] - 1
                ins = ops[i][1](handle)
                if sig[i] is not None:
                    s = sig[i]
                    ins.then_inc(sems[(s[0], s[1])], 16 if eng_name in dmaq else 1)
            if eng_name in dmaq:
                for d in sorted(final):
                    if ops[d][0] in dmaq:
                        wait_for(d)

        with nc.Block() as block:
            @block.tensor
            def _(t):
                run('pe', t)

            @block.scalar
            def _(t):
                run('act', t)

            @block.vector
            def _(t):
                run('dve', t)

            @block.gpsimd
            def _(t):
                run('pool', t)

            @block.sync
            def _(t):
                run('sp', t)


def dap(t, off, pat):
    return bass.AP(t, off, [list(p) for p in pat])


def build(stage=99):
    nc = bass.Bass("TRN2", target_bir_lowering=False)
    S = Sched()
    st = ExitStack()
    di = {}

    def din(name, shape, dt=F32):
        di[name] = nc.dram_tensor(name, list(shape), dt, kind="ExternalInput")
        return di[name]

    def dscr(name, shape, dt=BF):
        return nc.dram_tensor(name, list(shape), dt, kind="Internal")

    def dout(name, shape, dt=F32):
        return nc.dram_tensor(name, list(shape), dt, kind="ExternalOutput")

    def sb(name, shape, dt=F32):
        return st.enter_context(nc.sbuf_tensor("sb_" + name, list(shape), dt))

    xin = [din("xin%d" % s, [128 + NSUP[s] * 512, D]) for s in range(NSEQ)]
    xloc = [din("xloc%d" % s, [NLOC[s] * 512, D]) for s in range(NSEQ)]
    cosd = [din("cos%d" % s, [32, TK(s)]) for s in range(NSEQ)]
    sind = [din("sin%d" % s, [32, TK(s)]) for s in range(NSEQ)]
    mskd = [din("msk%d" % s, [128, (1 + NSUP[s]) * 4]) for s in range(NSEQ)]
    cst = din("cst", [128, 6 * 128])
    vmeta_d = din("vmeta", [128, 1])
    w_a_d = din("w_a", [D, NA]); b_a_d = din("b_a", [1, NA])
    w_l_d = din("w_l", [D, NL_]); b_l_d = din("b_l", [1, NL_])
    w_g_d = din("w_g", [D, 2048]); b_g_d = din("b_g", [1, 2048])
    w_uq_d = din("w_uq", [256, 1024])
    w_ukv_d = din("w_ukv", [128, 1024])
    w_pa_d = din("w_pa", [512, D]); w_pb_d = din("w_pb", [512, D]); w_o_d = din("w_o", [D, D])
    w_gu_d = din("w_gu", [D, 2 * D_FF]); w_dn_d = din("w_dn", [D_FF, D])
    g1_d = din("g1", [128, 8]); g2_d = din("g2", [128, 8])
    gq_d = din("gq", [128, 2]); gkv_d = din("gkv", [128, 1]); gm_d = din("gm", [128, 4])
    gfin_d = din("gfin", [1, D])
    cw_d = din("cw", [128, 8 * 3]); cb_d = din("cb", [128, 8])

    y = [dout("y%d" % s, [NLOC[s] * 512, D]) for s in range(NSEQ)]

    KN = [dscr("kn%d" % s, [NH * 64, TK(s)]) for s in range(NSEQ)]
    KR = [dscr("kr%d" % s, [33, TK(s)]) for s in range(NSEQ)]
    VS = [dscr("vs%d" % s, [NH, 128, TK(s) // 128, 65]) for s in range(NSEQ)]
    QT = [dscr("qt%d" % s, [NH, 97, NLOC[s] * 512]) for s in range(NSEQ)]
    LQ = [dscr("lq%d" % s, [MH * 128, NLOC[s] * 512]) for s in range(NSEQ)]
    LK = [dscr("lk%d" % s, [MH * 128, NLOC[s] * 512]) for s in range(NSEQ)]
    LKT = [dscr("lkt%d" % s, [128, NLOC[s] * 4, 512]) for s in range(NSEQ)]
    LV = [dscr("lv%d" % s, [128, NLOC[s] * 4, MH * 129]) for s in range(NSEQ)]
    LG = [dscr("lg%d" % s, [128, NLOC[s] * 4, 16], F32) for s in range(NSEQ)]
    LMO = [dscr("lmo%d" % s, [128, NLOC[s] * 4, 512]) for s in range(NSEQ)]
    AOT = [dscr("aot%d" % s, [512, NLOC[s] * 512]) for s in range(NSEQ)]
    RIV = [dscr("riv%d" % s, [NH, NLOC[s] * 512], F32) for s in range(NSEQ)]
    MOT = [dscr("mot%d" % s, [512, NLOC[s] * 512]) for s in range(NSEQ)]
    H1 = [dscr("h1_%d" % s, [NLOC[s] * 512, D], F32) for s in range(NSEQ)]
    X2T = [dscr("x2t%d" % s, [D, NLOC[s] * 512]) for s in range(NSEQ)]
    HT = [dscr("ht%d" % s, [D_FF, NLOC[s] * 512]) for s in range(NSEQ)]
    dbg = {}
    if DEBUG:
        dbg['cf'] = dout("dbg_cf", [NSEQ, 128, MH * 129])
        dbg['bb'] = dout("dbg_bb", [NSEQ, 128, MH * 129])
        dbg['kn'] = dout("dbg_kn", [NH * 64, TK(1)], BF)
        dbg['kr'] = dout("dbg_kr", [33, TK(1)], BF)
        dbg['vs'] = dout("dbg_vs", [NH, 128, TK(1) // 128, 65], BF)
        dbg['qt'] = dout("dbg_qt", [NH, 97, NLOC[1] * 512], BF)
        dbg['lq'] = dout("dbg_lq", [MH * 128, NLOC[1] * 512], BF)
        dbg['lk'] = dout("dbg_lk", [MH * 128, NLOC[1] * 512], BF)
        dbg['aot'] = dout("dbg_aot", [512, NLOC[1] * 512], BF)
        dbg['mot'] = dout("dbg_mot", [512, NLOC[1] * 512], BF)

    C = sb("cst", [128, 6 * 128])
    Cb = sb("cstb", [128, 6 * 128], BF)
    ident_b = Cb[:, 0:128]
    ones_b = Cb[:, 3 * 128:4 * 128]
    triU_b = Cb[:, 128:256]; triL_b = Cb[:, 256:384]
    triU = C[:, 128:256]; triL = C[:, 256:384]; onesF = C[:, 384:512]
    maskF = C[:, 512:640]; maskB = C[:, 640:768]
    vmeta = sb("vmeta", [128, 1])
    epst = sb("epst", [128, 1]); onet = sb("onet", [128, 1]); lnks = sb("lnks", [128, 1])
    g1 = sb("g1", [128, 8]); g2 = sb("g2", [128, 8]); gq = sb("gq", [128, 2]); gkv = sb("gkv", [128, 1])
    gm = sb("gm", [128, 4])
    cw = sb("cw", [128, 24]); cb = sb("cb", [128, 8])
    WA = sb("WA", [128, 8, NA], BF)
    WL = sb("WL", [128, 8, NL_], BF)
    WUQ = sb("WUQ", [128, 2, 1024], BF)
    WUKV = sb("WUKV", [128, 1024], BF)
    bA = sb("bA", [128, 2])
    bAm = sb("bAm", [128, 4])
    bLq = sb("bLq", [128, 2 + 4])
    bbc = sb("bbc", [128, 512 + 16 + 512])
    stage_t = sb("stage", [128, 1024])
    bG_t = sb("bG", [128, 16])
    S4_t = sb("S4", [128, 16])
    BIG = sb("BIG", [128, 34 * 1024 + 3072])

    def carve(off_words, shape, dt):
        n = int(np.prod(shape[1:]))
        words = n if dt == F32 else (n + 1) // 2
        v = BIG[0:shape[0], off_words:off_words + words]
        if dt == BF:
            v = v.bitcast(BF)[:, 0:n]
        if len(shape) == 3:
            v = v.rearrange("p (a b) -> p a b", a=shape[1])
        elif len(shape) == 4:
            v = v.rearrange("p (a b c) -> p a b c", a=shape[1], b=shape[2])
        return v, off_words + words

    PSA = st.enter_context(nc.psum_tensor("psa", [128, 8 * 512], F32))
    PSAb = PSA.bitcast(BF)

    class _Bank:
        def __init__(self, i):
            self.i = i

        def __getitem__(self, idx):
            p, c = idx
            c0 = 0 if c.start is None else c.start
            c1 = 512 if c.stop is None else c.stop
            return PSA[p, self.i * 512 + c0:self.i * 512 + c1]

        def bitcast(self, dt):
            b = self

            class _B:
                def __getitem__(self, idx):
                    p, c = idx
                    c0 = 0 if c.start is None else c.start
                    c1 = 1024 if c.stop is None else c.stop
                    return PSAb[p, b.i * 1024 + c0:b.i * 1024 + c1]
            return _B()
    PS = [_Bank(i) for i in range(8)]
    psn = [0]

    def psum():
        i = psn[0] % 8
        psn[0] += 1
        return PS[i], ('ps', i)

    def dma(out, in_, r, w, q='sp'):
        S.op(q, lambda e: e.dma_start(out=out, in_=in_, allow_slow_non_contiguous=True), r, w)

    def act(out, in_, func, r, w, bias=None, scale=None, accum=None):
        kw = {}
        if bias is not None:
            kw['bias'] = bias
        if scale is not None:
            kw['scale'] = scale
        if accum is not None:
            kw['accum_out'] = accum
        S.op('act', lambda e: e.activation(out, in_, func, **kw), r, w)

    def ts(eng, out, in0, s1, s2, op0, op1, r, w):
        if op1 is None:
            S.op(eng, lambda e: e.tensor_scalar(out, in0, s1, None, op0), r, w)
        else:
            S.op(eng, lambda e: e.tensor_scalar(out, in0, s1, s2, op0, op1), r, w)

    def tt(eng, out, in0, in1, op, r, w):
        S.op(eng, lambda e: e.tensor_tensor(out, in0, in1, op), r, w)

    def stt(out, in0, sc, in1, op0, op1, r, w):
        S.op('dve', lambda e: e.scalar_tensor_tensor(out, in0, sc, in1, op0, op1), r, w)

    def cp(eng, out, in_, r, w):
        if eng == 'act':
            S.op('act', lambda e: e.copy(out, in_), r, w)
        else:
            S.op(eng, lambda e: e.tensor_copy(out, in_), r, w)

    def mm(out, lhsT, rhs, start, stop, r, w):
        S.op('pe', lambda e: e.matmul(out, lhsT, rhs, start=start, stop=stop), r, w)

    def tr(out, in_, r, w):
        S.op('pe', lambda e: e.transpose(out, in_, ident_b), r, w)

    S.op('pool', lambda e: e.memset(BIG[:, :], 0.0), [], ['BIGZ'])
    S.barrier()
    dma(C[:], cst.ap(), [], ['C'])
    cp('dve', Cb[:], C[:], ['C'], ['Cb'])
    dma(vmeta[:], vmeta_d.ap(), [], ['vmeta'])
    S.op('dve', lambda e: e.memset(epst[:], EPS), [], ['epst'])
    S.op('dve', lambda e: e.memset(onet[:], 1.0), [], ['onet'])
    S.op('dve', lambda e: e.memset(lnks[:], float(np.log(K_SCALE))), [], ['lnks'])
    for t_, d_, k_ in ((g1, g1_d, 'g1'), (g2, g2_d, 'g2'), (gq, gq_d, 'gq'), (gkv, gkv_d, 'gkv'), (gm, gm_d, 'gm'),
                       (cw, cw_d, 'cw'), (cb, cb_d, 'cb')):
        dma(t_[:], d_.ap(), [], [k_])
    dma(bA[:, 0:1], dap(b_a_d, 0, [[1, 128], [1, 1]]), [], ['bA'])
    dma(bA[0:64, 1:2], dap(b_a_d, 128, [[1, 64], [1, 1]]), [], ['bA'])
    dma(bAm[:], dap(b_a_d, 192, [[1, 128], [128, 4]]), [], ['bAm'])
    dma(bLq[:, 0:2], dap(b_l_d, 0, [[1, 128], [128, 2]]), [], ['bLq'])
    dma(bLq[:, 2:6], dap(b_l_d, 256, [[1, 128], [128, 4]]), [], ['bLq'])
    dma(bbc[:, 0:528], dap(b_a_d, 704, [[0, 128], [1, 528]]), [], ['bbc'])
    dma(bbc[:, 528:1040], dap(b_l_d, 768, [[0, 128], [1, 512]]), [], ['bbc'])

    def load_weight(dst, src, rows, cols, gain, gkey, wkey, engines=('dve', 'act'), jobs=None):
        nk = rows // 128

        def one_chunk(kt, c0, ci):
            cn = min(1024, cols - c0)
            dma(stage_t[:, 0:cn], dap(src, kt * 128 * cols + c0, [[cols, 128], [1, cn]]), [], ['stage'])
            o = dst[:, kt, c0:c0 + cn] if nk > 1 or len(dst.shape) == 3 else dst[:, c0:c0 + cn]
            eng = engines[ci % len(engines)]
            if gain is None:
                cp(eng, o, stage_t[:, 0:cn], ['stage'], [wkey])
            elif eng == 'dve':
                ts(eng, o, stage_t[:, 0:cn], gain[:, kt:kt + 1], None, ALU.mult, None,
                   ['stage', gkey], [wkey])
            else:
                act(o, stage_t[:, 0:cn], AF.Copy, ['stage', gkey], [wkey], scale=gain[:, kt:kt + 1])

        ci = 0
        for kt in range(nk):
            for c0 in range(0, cols, 1024):
                if jobs is not None:
                    jobs.append(lambda kt=kt, c0=c0, ci=ci: one_chunk(kt, c0, ci))
                else:
                    one_chunk(kt, c0, ci)
                ci += 1

    load_weight(WA, w_a_d, D, NA, g1, 'g1', 'WA')
    load_weight(WL, w_l_d, D, NL_, g1, 'g1', 'WL')
    load_weight(WUQ, w_uq_d, 256, 1024, gq, 'gq', 'WUQ')
    load_weight(WUKV, w_ukv_d, 128, 1024, gkv, 'gkv', 'WUKV')

    o = 0
    ST_, o = carve(o, [128, 2, 4 * 129], F32)
    O_PERSIST = o
    XT, o = carve(o, [128, 4, 1024], F32)
    XN, o = carve(o, [128, 2, 1024], BF)
    XNT, o = carve(o, [128, 2, 8, 512], BF)
    MKr, o = carve(o, [128, 3, 8, 514], BF)
    VR, o = carve(o, [128, 3, 4, 512], BF)
    GR, o = carve(o, [128, 3, 4, 16], F32)
    CKV, o = carve(o, [128, 2, 512], BF)
    SQ, o = carve(o, [128, 2, 512], BF)
    LNT, o = carve(o, [128, 512], F32)
    RBC, o = carve(o, [128, 512], F32)
    KNS, o = carve(o, [128, 4, 512], BF)
    KRS, o = carve(o, [33, 512], BF)
    KT1, o = carve(o, [32, 512], F32)
    KT2, o = carve(o, [32, 512], F32)
    VST, o = carve(o, [128, 4, 8, 65], BF)
    CSK, o = carve(o, [64, 2, 512], F32)
    SSQ, o = carve(o, [128, 8], F32)
    RST, o = carve(o, [128, 8], F32)
    CV, o = carve(o, [128, 2, 512], F32)
    KTb, o = carve(o, [128, 8, 512], BF)
    KTK, o = carve(o, [128, 4, 512], BF)
    WV, o = carve(o, [128, 4, 2, 4 * 129], BF)
    GT, o = carve(o, [128, 16, 16], F32)
    AHL, o = carve(o, [128, 2, 4, 8], BF)
    CQT, o = carve(o, [128, 2, 512], BF)
    QTS, o = carve(o, [97, 2, 512], BF)
    CSQ, o = carve(o, [128, 2, 512], F32)
    MOS, o = carve(o, [128, 2, 512], BF)
    VAL, o = carve(o, [128, 4, 4 * 129], BF)
    SM, o = carve(o, [128, 64], F32)
    MSK, o = carve(o, [128, 33 * 4], F32)
    MW, o = carve(o, [128, 8], F32)
    assert o <= 34 * 1024 + 3072, o


    def mlstm_local(s):
        nl = NLOC[s]
        nch = nl * 4
        nq = nl * 512
        o2 = O_PERSIST

        def two(shape, dt):
            nonlocal o2
            a, o2 = carve(o2, shape, dt)
            b, o2 = carve(o2, shape, dt)
            return (a, b)
        QTl2 = two([128, 4, 512], BF)
        KTl2 = two([128, 4, 512], BF)
        KKl2 = two([128, 4, 512], BF)
        VAl2 = two([128, 4, 516], BF)
        GRl2 = two([128, 4, 16], F32)
        G22 = two([128, 16, 16], F32)
        AH22 = two([128, 2, 4, 4], BF)
        SST2 = two([128, 2, 128], BF)
        WVl2 = two([128, 2, 130], BF)
        RC2_ = two([128, 4], F32)
        MOl, o2 = carve(o2, [128, 4, 512], BF)
        HF, o2 = carve(o2, [128, nch, 512], F32)
        HB, o2 = carve(o2, [128, nch, 512], F32)
        CBF, o2 = carve(o2, [128, 2, 4 * 130], BF)
        RC, o2 = carve(o2, [128, 8], F32)
        HG, o2 = carve(o2, [128, 512], F32)
        HSQ, o2 = carve(o2, [128, 128], F32)
        MOb, o2 = carve(o2, [128, 512], BF)
        MTs, o2 = carve(o2, [128, 4, 128], BF)
        assert o2 <= 34 * 1024 + 3072, o2

        def dir_gen(d_):
            QTl, KTl, KKl, VAl, GRl = QTl2[d_], KTl2[d_], KKl2[d_], VAl2[d_], GRl2[d_]
            G2, AH2, SST, WVl, RCd = G22[d_], AH22[d_], SST2[d_], WVl2[d_], RC2_[d_]
            HD = HF if d_ == 0 else HB
            K = lambda name: (name, d_)
            skey = 'Cf' if d_ == 0 else 'Bb'
            for h in range(4):
                cp('pool', CBF[:, d_, h * 130:h * 130 + 129], ST_[:, d_, h * 129:(h + 1) * 129], [skey], [('CBF', d_)])
            sups = range(nl) if d_ == 0 else range(nl - 1, -1, -1)
            for i in sups:
                for h in range(4):
                    dma(QTl[:, h, :], dap(LQ[s], h * 128 * nq + i * 512, [[nq, 128], [1, 512]]), [('LQ', s)], [K('QTl')])
                    dma(KTl[:, h, :], dap(LK[s], h * 128 * nq + i * 512, [[nq, 128], [1, 512]]), [('LK', s)], [K('KTl')])
                dma(KKl[:, :, :].rearrange("p t c -> p (t c)"), dap(LKT[s], i * 4 * 512, [[nch * 512, 128], [1, 2048]]), [('LKT', s)], [K('KKl')])
                dma(VAl[:, :, :].rearrange("p t c -> p (t c)"), dap(LV[s], i * 4 * 516, [[nch * 516, 128], [1, 4 * 516]]), [('LV', s)], [K('VAl')])
                dma(GRl[:, :, :].rearrange("p t c -> p (t c)"), dap(LG[s], i * 64, [[nch * 16, 128], [1, 64]]), [('LG', s)], [K('GRl')])
                G4 = GRl[:, :, :].rearrange("p t (d j h) -> p t d j h", d=2, j=2)
                fpre = G4[:, :, d_, 1, :]; ipre = G4[:, :, d_, 0, :]
                T_ = G2[:, 0:4, 0:4]; A_ = G2[:, 0:4, 4:8]; CS = G2[:, 4:8, 0:8]
                ARG = G2[:, 0:4, 8:12]; DD = G2[:, 8:12, 0:4]; WW = G2[:, 8:12, 4:8]; EF = G2[:, 8:12, 8:12]
                FLO = G2[:, 8:12, 12:16]
                act(T_, fpre, AF.Exp, [K('GRl')], [K('T2')], scale=-1.0)
                act(T_, T_, AF.Ln, [K('T2'), 'onet'], [K('T2')], bias=onet[:, 0:1])
                ts('dve', A_, T_, -1.0, None, ALU.mult, None, [K('T2')], [K('A2')])
                cp('dve', AH2[:, 0, :, :], A_, [K('A2')], [K('AH2')])
                tt('dve', AH2[:, 1, :, :], A_, AH2[:, 0, :, :], ALU.subtract, [K('A2'), K('AH2')], [K('AL2')])
                ps, pk = psum()
                tri = triU_b if d_ == 0 else triL_b
                for t in range(4):
                    for j_, kk in ((0, K('AH2')), (1, K('AL2'))):
                        mm(ps[:, t * 8:t * 8 + 4], tri, AH2[:, j_, t, :], j_ == 0, j_ == 1, ['Cb', kk], [pk])
                    for j_, kk in ((0, K('AH2')), (1, K('AL2'))):
                        mm(ps[:, t * 8 + 4:t * 8 + 8], ones_b, AH2[:, j_, t, :], j_ == 0, j_ == 1, ['Cb', kk], [pk])
                cp('dve', CS, ps[:, 0:32].rearrange("p (t c) -> p t c", t=4), [pk], [K('CS2')])
                tt('dve', ARG, ipre, CS[:, :, 0:4], ALU.subtract, [K('GRl'), K('CS2')], [K('ARG2')])
                act(DD, ARG, AF.Exp, [K('ARG2'), 'lnks'], [K('DD')], bias=lnks[:, 0:1])
                tt('dve', ARG, ARG, CS[:, :, 4:8], ALU.add, [K('ARG2'), K('CS2')], [K('ARG2')])
                act(WW, ARG, AF.Exp, [K('ARG2'), 'lnks'], [K('WW')], bias=lnks[:, 0:1])
                act(EF, CS[:, :, 4:8], AF.Exp, [K('CS2')], [K('EF')])
                act(FLO, CS[:, :, 0:4], AF.Exp, [K('CS2')], [K('FLO')], scale=-1.0)
                yield
                chunks = range(4) if d_ == 0 else range(3, -1, -1)
                msk = maskF if d_ == 0 else maskB
                for t in chunks:
                    c = i * 4 + t
                    for h in range(4):
                        sl = h % 2
                        ps1, pk1 = psum()
                        mm(ps1[:, 0:128], KTl[:, h, t * 128:(t + 1) * 128], QTl[:, h, t * 128:(t + 1) * 128], True, True,
                           [K('KTl'), K('QTl')], [pk1])
                        stt(SST[:, sl, :], ps1[:, 0:128], DD[:, t, h:h + 1], msk, ALU.mult, ALU.mult, [pk1, K('DD'), 'C'], [K(('SST', sl))])
                        ps2, pk2 = psum()
                        mm(ps2[:, 0:129], QTl[:, h, t * 128:(t + 1) * 128], CBF[:, d_, h * 130:h * 130 + 129], True, False,
                           [K('QTl'), ('CBF', d_)], [pk2])
                        mm(ps2[:, 0:129], SST[:, sl, :], VAl[:, t, h * 129:(h + 1) * 129], False, True, [K(('SST', sl)), K('VAl')], [pk2])
                        rk = K(('RC', sl))
                        ts('dve', RCd[:, sl:sl + 1], ps2[:, 128:129], FLO[:, t, h:h + 1], None, ALU.max, None, [pk2, K('FLO')], [rk])
                        stt(RCd[:, sl:sl + 1], ps2[:, 128:129], -1.0, RCd[:, sl:sl + 1], ALU.mult, ALU.max, [pk2, rk], [rk])
                        S.op('dve', lambda e, sl=sl: e.reciprocal(RCd[:, sl:sl + 1], RCd[:, sl:sl + 1]), [rk], [rk])
                        hk = ('HD', d_, c)
                        ts('dve', HD[:, c, h * 128:(h + 1) * 128], ps2[:, 0:128], RCd[:, sl:sl + 1], None, ALU.mult, None, [pk2, rk], [hk])
                        ts('dve', WVl[:, sl, 0:129], VAl[:, t, h * 129:(h + 1) * 129], WW[:, t, h:h + 1], None, ALU.mult, None,
                           [K('VAl'), K('WW')], [K(('WVl', sl))])
                        ps3, pk3 = psum()
                        mm(ps3[:, 0:129], KKl[:, t, h * 128:(h + 1) * 128], WVl[:, sl, 0:129], True, True, [K('KKl'), K(('WVl', sl))], [pk3])
                        stt(ST_[:, d_, h * 129:(h + 1) * 129], ST_[:, d_, h * 129:(h + 1) * 129], EF[:, t, h:h + 1], ps3[:, 0:129],
                            ALU.mult, ALU.add, [skey, K('EF'), pk3], [skey])
                        cp('pool', CBF[:, d_, h * 130:h * 130 + 129], ST_[:, d_, h * 129:(h + 1) * 129], [skey], [('CBF', d_)])
                        yield

        gens = [dir_gen(0), dir_gen(1)]
        while gens:
            for g in list(gens):
                try:
                    next(g)
                except StopIteration:
                    gens.remove(g)
        for i in range(nl):
            dma(MOl[:, :, :].rearrange("p t c -> p (t c)"), dap(LMO[s], i * 4 * 512, [[nch * 512, 128], [1, 2048]]), [('LMO', s)], ['MOl'])
            for t in range(4):
                c = i * 4 + t
                tt('dve', HG[:, :], HF[:, c, :], HB[:, c, :], ALU.add, [('HD', 0, c), ('HD', 1, c)], ['HG'])
                tt('dve', HG[:, :], HG[:, :], MOl[:, t, :], ALU.mult, ['HG', 'MOl'], ['HG'])
                for h in range(4):
                    act(HSQ[:, :], HG[:, h * 128:(h + 1) * 128], AF.Square, ['HG'], ['HSQ', 'RC2'], accum=RC[:, 4 + h:5 + h])
                act(RC[:, 4:8], RC[:, 4:8], AF.Sqrt, ['RC2', 'epst'], ['RC2'], bias=epst[:, 0:1], scale=1.0 / 128)
                S.op('dve', lambda e: e.reciprocal(RC[:, 4:8], RC[:, 4:8]), ['RC2'], ['RC2'])
                tt('dve', MOb[:, :].rearrange("p (h d) -> p h d", h=4), HG[:, :].rearrange("p (h d) -> p h d", h=4),
                   RC[:, 4:8].unsqueeze(2).to_broadcast([128, 4, 128]), ALU.mult, ['HG', 'RC2'], ['MOb'])
                ps, pk = psum()
                psb = ps.bitcast(BF)
                for h in range(4):
                    tr(psb[:, h * 128:(h + 1) * 128], MOb[:, h * 128:(h + 1) * 128], ['MOb', 'Cb'], [pk])
                cp('act', MTs[:, :, :], psb[:, 0:512].rearrange("p (h t) -> p h t", h=4), [pk], ['MTs'])
                dma(dap(MOT[s], c * 128, [[nq, 128], [128 * nq, 4], [1, 128]]), MTs[:, :, :], ['MTs'], [('MOT', s)])
        if DEBUG and s == 1:
            dma(dbg['mot'].ap(), MOT[s].ap(), [('MOT', s)], ['dbg8'])

    ARENA_END = 34 * 1024 + 3072
    O_B4W = ARENA_END - 16384
    b4a_jobs = []

    def b4a_weights():
        o3 = O_B4W
        WG_, o3 = carve(o3, [128, 8, 2048], BF)
        WPA_, o3 = carve(o3, [128, 4, 1024], BF)
        WPB_, o3 = carve(o3, [128, 4, 1024], BF)
        WO_, o3 = carve(o3, [128, 8, 1024], BF)
        assert o3 <= ARENA_END
        return WG_, WPA_, WPB_, WO_

    def prefetch_b4a_weights():
        WG_, WPA_, WPB_, WO_ = b4a_weights()
        load_weight(WG_, w_g_d, D, 2048, g1, 'g1', 'WG', engines=('dve',), jobs=b4a_jobs)
        load_weight(WPA_, w_pa_d, 512, D, None, None, 'WPA', engines=('dve',), jobs=b4a_jobs)
        load_weight(WPB_, w_pb_d, 512, D, gm, 'gm', 'WPB', engines=('dve',), jobs=b4a_jobs)
        load_weight(WO_, w_o_d, D, D, None, None, 'WO', engines=('dve',), jobs=b4a_jobs)
        b4a_jobs.append(lambda: dma(bG_t[:, :], dap(b_g_d, 0, [[1, 128], [128, 16]]), [], ['bG']))

    def attention(s):
        nq = NLOC[s] * 512
        nb = TK(s) // 128
        nqb = nq // 512
        o2 = O_PERSIST
        KTt, o2 = carve(o2, [97, TK(s)], BF)
        Vh, o2 = carve(o2, [128, nb, 65], BF)
        if o2 % 2:
            o2 += 1
        QTh, o2 = carve(o2, [97, nq], BF)
        PT, o2 = carve(o2, [128, 3, 1024], BF)
        RR, o2 = carve(o2, [128, 2, 512], F32)
        AO, o2 = carve(o2, [64, 2, 512], BF)
        assert o2 <= O_B4W, o2
        NSEG = 8
        segb = [(nb * g) // NSEG for g in range(NSEG + 1)]
        seg_of = {}
        for g in range(NSEG):
            for kb in range(segb[g], segb[g + 1]):
                seg_of[kb] = g
        for g in range(NSEG):
            c0, c1 = segb[g] * 128, segb[g + 1] * 128
            dma(KTt[64:97, c0:c1], dap(KR[s], c0, [[TK(s), 33], [1, c1 - c0]]), [('KR', s)], [('KTr', g)])
        for h in range(NH):
            for g in range(NSEG):
                c0, c1 = segb[g] * 128, segb[g + 1] * 128
                dma(KTt[0:64, c0:c1], dap(KN[s], h * 64 * TK(s) + c0, [[TK(s), 64], [1, c1 - c0]]), [('KN', s)], [('KTn', g)])
                b0, b1 = segb[g], segb[g + 1]
                dma(Vh[:, b0:b1, :], dap(VS[s], (h * 128 * nb + b0) * 65, [[nb * 65, 128], [65, b1 - b0], [1, 65]]),
                    [('VS', s)], [('Vh', g)])
            dma(QTh[:, :], dap(QT[s], h * 97 * nq, [[nq, 97], [1, nq]]), [('QT', s)], ['QTh'])
            if s == 0 and h == 1:
                prefetch_b4a_weights()
            for qb in range(nqb):
                pob = 6 + (qb % 2)
                po = PS[pob]; pok = ('ps', pob)
                groups = [(kb0, min(2, nb - kb0)) for kb0 in range(0, nb, 2)]

                def emit_S(gi):
                    kb0, n2 = groups[gi]
                    sb0 = (gi % 3) * 2
                    for j in range(n2):
                        kb = kb0 + j
                        g = seg_of[kb]
                        mm(PS[sb0 + j][:, 0:512], KTt[:, kb * 128:(kb + 1) * 128], QTh[:, qb * 512:(qb + 1) * 512], True, True,
                           [('KTr', g), ('KTn', g), 'QTh'], [('ps', sb0 + j)])
                    act(PT[:, gi % 3, 0:n2 * 512], PSA[:, sb0 * 512:(sb0 + n2) * 512], AF.Exp,
                        [('ps', sb0 + j) for j in range(n2)], [('PT', gi % 3)], scale=SM_SCALE)

                def emit_PV(gi):
                    kb0, n2 = groups[gi]
                    for j in range(n2):
                        kb = kb0 + j
                        g = seg_of[kb]
                        mm(po[0:65, 0:512], Vh[:, kb, :], PT[:, gi % 3, j * 512:(j + 1) * 512], kb == 0, kb == nb - 1,
                           [('Vh', g), ('PT', gi % 3)], [pok])

                emit_S(0)
                emit_S(1)
                for gi in range(len(groups)):
                    if gi + 2 < len(groups):
                        emit_S(gi + 2)
                    emit_PV(gi)
                asl = qb % 2
                S.op('dve', lambda e, po=po, asl=asl: e.reciprocal(RR[64:65, asl, :], po[64:65, 0:512]), [pok], [('RR', asl)])
                cp('dve', AO[:, asl, :], po[0:64, 0:512], [pok], [('AO', asl)])
                dma(dap(AOT[s], h * 64 * nq + qb * 512, [[nq, 64], [1, 512]]), AO[:, asl, :], [('AO', asl)], [('AOT', s)])
                dma(dap(RIV[s], h * nq + qb * 512, [[nq, 1], [1, 512]]), RR[64:65, asl, :], [('RR', asl)], [('RIV', s)])
                for _ in range(3):
                    if b4a_jobs:
                        b4a_jobs.pop(0)()
        if DEBUG and s == 1:
            dma(dbg['aot'].ap(), AOT[s].ap(), [('AOT', s)], ['dbg7'])


    def ffn_phases():
        o2 = O_PERSIST
        WG, WPA, WPB, WO = b4a_weights()
        XT4, o2 = carve(o2, [128, 4, 1024], F32)
        XN4, o2 = carve(o2, [128, 2, 1024], BF)
        XNT4, o2 = carve(o2, [128, 8, 512], BF)
        AOl, o2 = carve(o2, [128, 4, 512], BF)
        RBl, o2 = carve(o2, [128, 4, 512], F32)
        MOl4, o2 = carve(o2, [128, 4, 512], BF)
        MRG, o2 = carve(o2, [128, 8, 512], BF)
        H1t, o2 = carve(o2, [128, 1, 1024], F32)
        X2N, o2 = carve(o2, [128, 2, 1024], BF)
        X2Ts, o2 = carve(o2, [128, 1, 8, 128], BF)
        SGA, o2 = carve(o2, [128, 512], F32)
        SGB, o2 = carve(o2, [128, 512], F32)
        T1, o2 = carve(o2, [128, 512], F32)
        assert o2 <= O_B4W, o2
        assert not b4a_jobs
        bG = bG_t
        S4 = S4_t
        assert o2 <= 34 * 1024 + 3072, o2
        for s in range(NSEQ):
            nq = NLOC[s] * 512
            for i in range(NLOC[s]):
                for t in range(4):
                    dma(XT4[:, t, :], dap(xloc[s], (i * 512 + t * 128) * D, [[D, 128], [1, D]]), [], [('XT4', t)])
                    act(XN4[:, t % 2, :], XT4[:, t, :], AF.Square, [('XT4', t)], [('XN4', t % 2), ('S4', t)], accum=S4[:, t:t + 1])
                    act(S4[:, 4 + t:5 + t], S4[:, t:t + 1], AF.Sqrt, [('S4', t), 'epst'], [('S4b', t)], bias=epst[:, 0:1], scale=1.0 / D)
                    S.op('dve', lambda e, t=t: e.reciprocal(S4[:, 4 + t:5 + t], S4[:, 4 + t:5 + t]), [('S4b', t)], [('S4b', t)])
                    ts('dve', XN4[:, t % 2, :], XT4[:, t, :], S4[:, 4 + t:5 + t], None, ALU.mult, None,
                       [('XT4', t), ('S4b', t)], [('XN4', t % 2)])
                    ps, pk = psum()
                    psb = ps.bitcast(BF)
                    for kt in range(8):
                        tr(psb[:, kt * 128:(kt + 1) * 128], XN4[:, t % 2, kt * 128:(kt + 1) * 128], [('XN4', t % 2), 'Cb'], [pk])
                    cp('act', XNT4[:, :, t * 128:(t + 1) * 128], psb[:, 0:1024].rearrange("p (k t) -> p k t", k=8), [pk], ['XNT4'])
                for j in range(4):
                    dma(AOl[:, j, :], dap(AOT[s], j * 128 * nq + i * 512, [[nq, 128], [1, 512]]), [('AOT', s)], ['AOl'])
                    for hh in range(2):
                        dma(RBl[hh * 64:(hh + 1) * 64, j, :], dap(RIV[s], (2 * j + hh) * nq + i * 512, [[0, 64], [1, 512]]),
                            [('RIV', s)], ['RBl'])
                    dma(MOl4[:, j, :], dap(MOT[s], j * 128 * nq + i * 512, [[nq, 128], [1, 512]]), [('MOT', s)], ['MOl4'])
                tt('dve', AOl[:, :, :], AOl[:, :, :], RBl[:, :, :], ALU.mult, ['AOl', 'RBl'], ['AOl'])
                for c in range(8):
                    pa, pka = psum()
                    for k in range(4):
                        mm(pa[:, 0:512], WPA[:, k, c * 128:(c + 1) * 128], AOl[:, k, :], k == 0, k == 3, ['WPA', 'AOl'], [pka])
                    pb, pkb = psum()
                    for k in range(4):
                        mm(pb[:, 0:512], WPB[:, k, c * 128:(c + 1) * 128], MOl4[:, k, :], k == 0, k == 3, ['WPB', 'MOl4'], [pkb])
                    ga, pkga = psum()
                    for k in range(8):
                        mm(ga[:, 0:512], WG[:, k, c * 128:(c + 1) * 128], XNT4[:, k, :], k == 0, k == 7, ['WG', 'XNT4'], [pkga])
                    gb, pkgb = psum()
                    for k in range(8):
                        mm(gb[:, 0:512], WG[:, k, 1024 + c * 128:1024 + (c + 1) * 128], XNT4[:, k, :], k == 0, k == 7, ['WG', 'XNT4'], [pkgb])
                    act(SGA[:, :], ga[:, 0:512], AF.Sigmoid, [pkga, 'bG'], ['SGA'], bias=bG[:, c:c + 1])
                    act(SGB[:, :], gb[:, 0:512], AF.Sigmoid, [pkgb, 'bG'], ['SGB'], bias=bG[:, 8 + c:9 + c])
                    tt('dve', T1[:, :], pa[:, 0:512], SGA[:, :], ALU.mult, [pka, 'SGA'], ['T1'])
                    tt('dve', SGB[:, :], pb[:, 0:512], SGB[:, :], ALU.mult, [pkb, 'SGB'], ['SGB'])
                    tt('pool', MRG[:, c, :], T1[:, :], SGB[:, :], ALU.add, ['T1', 'SGB'], ['MRG'])
                for t in range(4):
                    hs = t % 2
                    for half in range(2):
                        ps, pk = psum()
                        for k in range(8):
                            mm(ps[:, 0:512], MRG[:, k, t * 128:(t + 1) * 128], WO[:, k, half * 512:(half + 1) * 512], k == 0, k == 7,
                               ['MRG', 'WO'], [pk])
                        tt('dve', H1t[:, 0, half * 512:(half + 1) * 512], ps[:, 0:512], XT4[:, t, half * 512:(half + 1) * 512], ALU.add,
                           [pk, ('XT4', t)], ['H1t'])
                    dma(dap(H1[s], (i * 512 + t * 128) * D, [[D, 128], [1, D]]), H1t[:, 0, :], ['H1t'], [('H1', s)])
                    act(X2N[:, hs, :], H1t[:, 0, :], AF.Square, ['H1t'], [('X2N', hs), ('S4c', hs)], accum=S4[:, 8 + hs:9 + hs])
                    act(S4[:, 10 + hs:11 + hs], S4[:, 8 + hs:9 + hs], AF.Sqrt, [('S4c', hs), 'epst'], [('S4d', hs)], bias=epst[:, 0:1], scale=1.0 / D)
                    S.op('dve', lambda e, hs=hs: e.reciprocal(S4[:, 10 + hs:11 + hs], S4[:, 10 + hs:11 + hs]), [('S4d', hs)], [('S4d', hs)])
                    ts('dve', X2N[:, hs, :], H1t[:, 0, :], S4[:, 10 + hs:11 + hs], None, ALU.mult, None, ['H1t', ('S4d', hs)], [('X2N', hs)])
                    ps, pk = psum()
                    psb = ps.bitcast(BF)
                    for kt in range(8):
                        tr(psb[:, kt * 128:(kt + 1) * 128], X2N[:, hs, kt * 128:(kt + 1) * 128], [('X2N', hs), 'Cb'], [pk])
                    cp('act', X2Ts[:, 0, :, :], psb[:, 0:1024].rearrange("p (k t) -> p k t", k=8), [pk], ['X2Ts'])
                    dma(dap(X2T[s], i * 512 + t * 128, [[nq, 128], [128 * nq, 8], [1, 128]]), X2Ts[:, 0, :, :], ['X2Ts'], [('X2T', s)])
        S.barrier()
        o2 = O_PERSIST
        WGU, o2 = carve(o2, [128, 8, 2 * D_FF], BF)
        X2l, o2 = carve(o2, [128, 2, 8, 512], BF)
        SIL, o2 = carve(o2, [128, 2, 512], F32)
        HTc, o2 = carve(o2, [128, 2, 512], BF)
        assert o2 <= 34 * 1024 + 3072, o2
        load_weight(WGU, w_gu_d, D, 2 * D_FF, g2, 'g2', 'WGU')
        it = 0
        for s in range(NSEQ):
            nq = NLOC[s] * 512
            for i in range(NLOC[s]):
                xsl = it % 2
                it += 1
                dma(X2l[:, xsl, :, :], dap(X2T[s], i * 512, [[nq, 128], [128 * nq, 8], [1, 512]]), [('X2T', s)], [('X2l', xsl)])
                for c in range(D_FF // 128):
                    sl = c % 2
                    pg, pkg = psum()
                    for k in range(8):
                        mm(pg[:, 0:512], WGU[:, k, c * 128:(c + 1) * 128], X2l[:, xsl, k, :], k == 0, k == 7, ['WGU', ('X2l', xsl)], [pkg])
                    pu, pku = psum()
                    for k in range(8):
                        mm(pu[:, 0:512], WGU[:, k, D_FF + c * 128:D_FF + (c + 1) * 128], X2l[:, xsl, k, :], k == 0, k == 7,
                           ['WGU', ('X2l', xsl)], [pku])
                    act(SIL[:, sl, :], pg[:, 0:512], AF.Silu, [pkg], [('SIL', sl)])
                    tt('dve', HTc[:, sl, :], pu[:, 0:512], SIL[:, sl, :], ALU.mult, [pku, ('SIL', sl)], [('HTc', sl)])
                    dma(dap(HT[s], c * 128 * nq + i * 512, [[nq, 128], [1, 512]]), HTc[:, sl, :], [('HTc', sl)], [('HT', s)])
        S.barrier()
        o2 = O_PERSIST
        WDN, o2 = carve(o2, [128, 22, 1024], BF)
        HTl, o2 = carve(o2, [128, 22, 512], BF)
        H1l, o2 = carve(o2, [128, 2, 1024], F32)
        YT, o2 = carve(o2, [128, 2, 1024], F32)
        YSQ, o2 = carve(o2, [128, 1024], F32)
        GF, o2 = carve(o2, [128, 1024], F32)
        S5, o2 = carve(o2, [128, 8], F32)
        assert o2 <= 34 * 1024 + 3072, o2
        load_weight(WDN, w_dn_d, D_FF, D, None, None, 'WDN')
        dma(GF[:, :], dap(gfin_d, 0, [[0, 128], [1, D]]), [], ['GF'])
        for s in range(NSEQ):
            nq = NLOC[s] * 512
            for i in range(NLOC[s]):
                dma(HTl[:, :, :], dap(HT[s], i * 512, [[nq, 128], [128 * nq, 22], [1, 512]]), [('HT', s)], ['HTl'])
                for t in range(4):
                    hs = t % 2
                    dma(H1l[:, hs, :], dap(H1[s], (i * 512 + t * 128) * D, [[D, 128], [1, D]]), [('H1', s)], [('H1l', hs)])
                    for half in range(2):
                        ps, pk = psum()
                        for k in range(22):
                            mm(ps[:, 0:512], HTl[:, k, t * 128:(t + 1) * 128], WDN[:, k, half * 512:(half + 1) * 512], k == 0, k == 21,
                               ['HTl', 'WDN'], [pk])
                        tt('dve', YT[:, hs, half * 512:(half + 1) * 512], ps[:, 0:512], H1l[:, hs, half * 512:(half + 1) * 512], ALU.add,
                           [pk, ('H1l', hs)], [('YT', hs)])
                    act(YSQ[:, :], YT[:, hs, :], AF.Square, [('YT', hs)], ['YSQ', ('S5', hs)], accum=S5[:, hs:hs + 1])
                    act(S5[:, 2 + hs:3 + hs], S5[:, hs:hs + 1], AF.Sqrt, [('S5', hs), 'epst'], [('S5b', hs)], bias=epst[:, 0:1], scale=1.0 / D)
                    S.op('dve', lambda e, hs=hs: e.reciprocal(S5[:, 2 + hs:3 + hs], S5[:, 2 + hs:3 + hs]), [('S5b', hs)], [('S5b', hs)])
                    stt(YT[:, hs, :], YT[:, hs, :], S5[:, 2 + hs:3 + hs], GF[:, :], ALU.mult, ALU.mult, [('YT', hs), ('S5b', hs), 'GF'], [('YT', hs)])
                    dma(dap(y[s], (i * 512 + t * 128) * D, [[D, 128], [1, D]]), YT[:, hs, :], [('YT', hs)], [('y', s)])

    kscale_ln = float(np.log(K_SCALE))

    for s in range(NSEQ):
        if stage < 1:
            break
        nsup = NSUP[s]
        nl = NLOC[s]
        nsteps = 1 + nsup
        dma(MSK[:, 0:nsteps * 4], mskd[s].ap(), [], ['MSK'])
        S.op('dve', lambda e: e.memset(ST_[:], 0.0), [], ['Cf', 'Bb'])
        S.op('dve', lambda e: e.memset(SM[:, 0:8], 0.0), [], ['Grun', 'Hrun'])
        S.op('pool', lambda e: e.memset(KRS[32:33, :], 1.0), [], ['KRS1'])

        def mcol(step, j):
            return MSK[:, step * 4 + j:step * 4 + j + 1]

        def load_x(step, what='both'):
            is_meta = (step == 0)
            ntt = 1 if is_meta else 4
            ntok = ntt * 128
            row0 = 0 if is_meta else 128 + (step - 1) * 512
            col0 = 0 if is_meta else (1 + 4 * (step - 1)) * 128
            xs = step % 2
            if what in ('x', 'both'):
                for t in range(ntt):
                    dma(XT[:, t, :], dap(xin[s], (row0 + t * 128) * D, [[D, 128], [1, D]]), [], [('XT', t)])
            if what in ('cs', 'both'):
                dma(CSK[0:32, xs, 0:ntok], dap(cosd[s], col0, [[TK(s), 32], [1, ntok]]), [], [('CSKc', xs)])
                dma(CSK[32:64, xs, 0:ntok], dap(sind[s], col0, [[TK(s), 32], [1, ntok]]), [], [('CSKs', xs)])

        def N_tile(step, t):
            xs = step % 2
            act(XN[:, t % 2, :], XT[:, t, :], AF.Square, [('XT', t)], [('XN', t % 2), ('SSQ', t)],
                accum=SSQ[:, t:t + 1])
            act(RST[:, t:t + 1], SSQ[:, t:t + 1], AF.Sqrt, [('SSQ', t), 'epst'], [('RST', t)],
                bias=epst[:, 0:1], scale=1.0 / D)
            S.op('dve', lambda e, t=t: e.reciprocal(RST[:, t:t + 1], RST[:, t:t + 1]),
                 [('RST', t)], [('RST', t)])
            if t % 2 == 0:
                ts('dve', XN[:, t % 2, :], XT[:, t, :], RST[:, t:t + 1], None,
                   ALU.mult, None, [('XT', t), ('RST', t)], [('XN', t % 2)])
            else:
                act(XN[:, t % 2, :], XT[:, t, :], AF.Copy, [('XT', t), ('RST', t)], [('XN', t % 2)], scale=RST[:, t:t + 1])
            ps, pk = psum()
            psb = ps.bitcast(BF)
            for kt in range(8):
                tr(psb[:, kt * 128:(kt + 1) * 128], XN[:, t % 2, kt * 128:(kt + 1) * 128], [('XN', t % 2), 'Cb'], [pk])
            cp('act', XNT[:, xs, :, t * 128:(t + 1) * 128],
               psb[:, 0:1024].rearrange("p (k t) -> p k t", k=8), [pk], [('XNT', xs)])

        pending = []

        def once(f):
            f()
            if False:
                yield

        def pump(n=1):
            for _ in range(n):
                if not pending:
                    return
                g = pending.pop(0)
                try:
                    next(g)
                    pending.append(g)
                except StopIteration:
                    pass

        def tail_gen(step):
            is_meta = (step == 0)
            ntt = 1 if is_meta else 4
            ntok = ntt * 128
            kb0 = 0 if is_meta else 1 + 4 * (step - 1)
            col0 = kb0 * 128
            xs = step % 2
            ps2, pk2 = psum()
            mm(ps2[:, 0:ntok], ones_b, SQ[:, xs, 0:ntok], True, True, [('SQ', xs), 'Cb'], [pk2])
            act(LNT[:, 0:ntok], ps2[:, 0:ntok], AF.Ln, [pk2, 'epst'], ['LNT'], bias=epst[:, 0:1], scale=1.0 / 128)
            act(RBC[:, 0:ntok], LNT[:, 0:ntok], AF.Exp, ['LNT'], ['RBC'], scale=-0.5)
            yield
            ps3, pk3 = psum()
            for t in range(ntt):
                mm(ps3[:, t:t + 1], SQ[:, xs, t * 128:(t + 1) * 128], ones_b[:, 0:1], True, True, [('SQ', xs), 'Cb'], [pk3])
            act(SSQ[:, 4:4 + ntt], ps3[:, 0:ntt], AF.Sqrt, [pk3, 'epst'], ['RSV'], bias=epst[:, 0:1], scale=1.0 / 128)
            S.op('dve', lambda e: e.reciprocal(RST[:, 4:4 + ntt], SSQ[:, 4:4 + ntt]), ['RSV'], ['RSV2'])
            yield
            for hp in range(4):
                ps, pk = psum()
                mm(ps[:, 0:ntok], WUKV[:, hp * 128:(hp + 1) * 128], CKV[:, xs, 0:ntok], True, True, ['WUKV', ('CKV', xs)], [pk])
                tt('dve', KNS[:, hp, 0:ntok], ps[:, 0:ntok], RBC[:, 0:ntok], ALU.mult, [pk, 'RBC'], [('KNS', hp)])
                dma(dap(KN[s], hp * 128 * TK(s) + col0, [[TK(s), 128], [1, ntok]]), KNS[:, hp, 0:ntok],
                    [('KNS', hp)], [('KN', s)])
                yield
            for t in range(ntt):
                ps, pk = psum()
                mm(ps[:, 0:512], CKV[:, xs, t * 128:(t + 1) * 128], WUKV[:, 512:1024], True, True, ['WUKV', ('CKV', xs)], [pk])
                if is_meta:
                    ts('dve', RST[:, 4:5], RST[:, 4:5], vmeta[:, 0:1], None, ALU.mult, None, ['RSV2', 'vmeta'], ['RSV2'])
                ts('dve', VST[:, t, :, 0:64], ps[:, 0:512].rearrange("p (h d) -> p h d", h=8), RST[:, 4 + t:5 + t],
                   None, ALU.mult, None, [pk, 'RSV2'], [('VST', t)])
                if is_meta:
                    cp('pool', VST[:, t, :, 64:65], vmeta[:, 0:1].unsqueeze(1).to_broadcast([128, 8, 1]), ['vmeta'], [('VST', t)])
                else:
                    S.op('pool', lambda e, t=t: e.memset(VST[:, t, :, 64:65], 1.0), [], [('VST', t)])
                yield
            for h in range(NH):
                nb = TK(s) // 128
                dma(dap(VS[s], (h * 128 * nb + kb0) * 65, [[nb * 65, 128], [65, ntt], [1, 65]]),
                    VST[:, 0:ntt, h, :], [('VST', t) for t in range(ntt)], [('VS', s)])
                if h % 2 == 1:
                    yield


        def A_step(step):
            is_meta = (step == 0)
            ntt = 1 if is_meta else 4
            ntok = ntt * 128
            row0 = 0 if is_meta else 128 + (step - 1) * 512
            kb0 = 0 if is_meta else 1 + 4 * (step - 1)
            col0 = kb0 * 128
            local = (1 <= step <= nl)
            lcol0 = (step - 1) * 512
            xs = step % 2
            rs = step % 3
            xk = ('XNT', xs)
            if step >= 1:
                pending.append(tail_gen(step - 1))
            if step >= 2:
                pending.append(M_step(step - 2))

            def proj_fm(c0, m, n0=0, nn=None):
                nn_ = ntok if nn is None else nn
                ps, pk = psum()
                for kt in range(8):
                    mm(ps[0:m, 0:nn_], WA[:, kt, c0:c0 + m], XNT[:, xs, kt, n0:n0 + nn_], kt == 0, kt == 7,
                       ['WA', xk], [pk])
                return ps, pk

            need_q = is_meta or local or step == nl + 1
            for h in range(8 if need_q else 4):
                if h < 4:
                    ps, pk = proj_fm(192 + h * 128, 128)
                    bias_ap = bAm[:, h:h + 1]; bk = 'bAm'
                else:
                    ps, pk = psum()
                    for kt in range(8):
                        mm(ps[:, 0:ntok], WL[:, kt, 256 + (h - 4) * 128:256 + (h - 3) * 128], XNT[:, xs, kt, 0:ntok],
                           kt == 0, kt == 7, ['WL', xk], [pk])
                    bias_ap = bLq[:, 2 + h - 4:3 + h - 4]; bk = 'bLq'
                act(MKr[:, rs, h, 1:1 + ntok], ps[:, 0:ntok], AF.Identity, [pk, bk], [('MK', rs)], bias=bias_ap)
                pump(2)
            nh_ = 8 if need_q else 4
            if is_meta:
                cp('dve', SM[:, 16:24], MKr[:, rs, :, 1], [('MK', rs)], ['pre'])
                cp('dve', SM[:, 8:16], MKr[:, rs, :, 1 + 126], [('MK', rs)], ['meta15'])
                S.op('pool', lambda e: e.memset(MKr[:, rs, :, 0:1 + META_LO], 0.0), ['pre'], [('MK', rs)])
            else:
                if step == 1:
                    cp('dve', MKr[:, rs, :, 0], SM[:, 16:24], ['pre'], [('MK', rs)])
                    cp('dve', SM[:, 24:32], MKr[:, rs, :, 1], [('MK', rs)], ['first'])
                else:
                    po = (step - 1) % 3
                    ts('dve', MW[:, 0:1], mcol(step, 2), -1.0, 1.0, ALU.mult, ALU.add, ['MSK'], ['MW0'])
                    ts('dve', GT[:, 0, 0:8], MKr[:, po, :, 512], mcol(step, 2), None, ALU.mult, None,
                       [('MK', po), 'MSK'], ['GT0'])
                    stt(MKr[:, rs, 0:nh_, 0], SM[:, 8:8 + nh_], MW[:, 0:1], GT[:, 0, 0:nh_], ALU.mult, ALU.add,
                        ['meta15', 'MW0', 'GT0'], [('MK', rs)])
            ps, pk = proj_fm(0, 128)
            act(CKV[:, xs, 0:ntok], ps[:, 0:ntok], AF.Identity, [pk, 'bA'], [('CKV', xs)], bias=bA[:, 0:1])
            act(SQ[:, xs, 0:ntok], ps[:, 0:ntok], AF.Square, [pk, 'bA'], [('SQ', xs)], bias=bA[:, 0:1])
            pump(3)
            ps, pk = proj_fm(128, 64)
            stt(KT1[:, 0:ntok], ps[0:32, 0:ntok], bA[0:32, 1:2], CSK[0:32, xs, 0:ntok], ALU.add, ALU.mult,
                [pk, 'bA', ('CSKc', xs)], ['KT1'])
            stt(KT2[:, 0:ntok], ps[32:64, 0:ntok], bA[32:64, 1:2], CSK[32:64, xs, 0:ntok], ALU.add, ALU.mult,
                [pk, 'bA', ('CSKs', xs)], ['KT2'])
            tt('dve', KRS[0:32, 0:ntok], KT1[:, 0:ntok], KT2[:, 0:ntok], ALU.add, ['KT1', 'KT2'], ['KRS'])
            dma(dap(KR[s], col0, [[TK(s), 33], [1, ntok]]), KRS[:, 0:ntok], ['KRS', 'KRS1'], [('KR', s)])
            if step + 2 < nsteps:
                load_x(step + 2, 'cs')
            pump(4)
            for t in range(ntt):
                ps, pk = psum()
                for kt in range(8):
                    mm(ps[:, 0:512], XNT[:, xs, kt, t * 128:(t + 1) * 128], WA[:, kt, 704:1216], kt == 0, kt == 7,
                       ['WA', xk], [pk])
                tt('dve', VR[:, rs, t, :], ps[:, 0:512], bbc[:, 0:512], ALU.add, [pk, 'bbc'], [('VR', rs)])
                pump(5)
            ps, pk = psum()
            for t in range(ntt):
                for kt in range(8):
                    mm(ps[:, t * 16:(t + 1) * 16], XNT[:, xs, kt, t * 128:(t + 1) * 128], WA[:, kt, 1216:1232],
                       kt == 0, kt == 7, ['WA', xk], [pk])
            tt('dve', GR[:, rs, 0:ntt, :], ps[:, 0:ntt * 16].rearrange("p (t g) -> p t g", t=ntt),
               bbc[:, 512:528].unsqueeze(1).to_broadcast([128, ntt, 16]), ALU.add, [pk, 'bbc'], [('GR', rs)])
            pump(5)
            while pending:
                pump()
            if local:
                dma(dap(LG[s], (step - 1) * 4 * 16, [[NLOC[s] * 4 * 16, 128], [1, 64]]),
                    GR[:, rs, :, :].rearrange("p t g -> p (t g)"), [('GR', rs)], [('LG', s)])
                A_local(step, xs, rs, xk, lcol0, col0)

        def A_local(step, xs, rs, xk, lcol0, col0):
            nq = NLOC[s] * 512
            ps2, pk2 = psum()
            for j in range(2):
                ps, pk = psum()
                for kt in range(8):
                    mm(ps[:, 0:512], WL[:, kt, j * 128:(j + 1) * 128], XNT[:, xs, kt, :], kt == 0, kt == 7, ['WL', xk], [pk])
                act(CQT[:, j, :], ps[:, 0:512], AF.Identity, [pk, 'bLq'], [('CQT', j)], bias=bLq[:, j:j + 1])
                act(SQ[:, 1 - xs, :], ps[:, 0:512], AF.Square, [pk, 'bLq'], [('SQ', 1 - xs)], bias=bLq[:, j:j + 1])
                mm(ps2[:, 0:512], ones_b, SQ[:, 1 - xs, :], j == 0, j == 1, [('SQ', 1 - xs), 'Cb'], [pk2])
            act(LNT[:, :], ps2[:, 0:512], AF.Ln, [pk2, 'epst'], ['LNT'], bias=epst[:, 0:1], scale=1.0 / 256)
            act(RBC[:, :], LNT[:, :], AF.Exp, ['LNT'], ['RBC'], scale=-0.5)
            dma(CSQ[64:96, 0, :], dap(cosd[s], col0, [[TK(s), 32], [1, 512]]), [], ['CSQc'])
            dma(CSQ[64:96, 1, :], dap(sind[s], col0, [[TK(s), 32], [1, 512]]), [], ['CSQs'])
            tt('pool', CSQ[64:96, 0, :], CSQ[64:96, 0, :], RBC[64:96, :], ALU.mult, ['CSQc', 'RBC'], ['CSQc'])
            tt('pool', CSQ[64:96, 1, :], CSQ[64:96, 1, :], RBC[64:96, :], ALU.mult, ['CSQs', 'RBC'], ['CSQs'])
            S.op('pool', lambda e: e.memset(QTS[96:97, :, :], 0.0), [], ['QTS96'])
            for h in range(NH):
                ps, pk = psum()
                for j in range(2):
                    mm(ps[:, 0:512], WUQ[:, j, h * 128:(h + 1) * 128], CQT[:, j, :], j == 0, j == 1,
                       ['WUQ', ('CQT', 0), ('CQT', 1)], [pk])
                tt('dve', QTS[0:64, h % 2, :], ps[0:64, 0:512], RBC[0:64, :], ALU.mult, [pk, 'RBC'], [('QTS', h % 2)])
                tt('dve', KT1[:, :], ps[64:96, 0:512], CSQ[64:96, 0, :], ALU.mult, [pk, 'CSQc'], ['KT1'])
                tt('dve', KT2[:, :], ps[96:128, 0:512], CSQ[64:96, 1, :], ALU.mult, [pk, 'CSQs'], ['KT2'])
                tt('pool', QTS[64:96, h % 2, :], KT1[:, :], KT2[:, :], ALU.add, ['KT1', 'KT2'], [('QTS', h % 2)])
                dma(dap(QT[s], h * 97 * nq + lcol0, [[nq, 97], [1, 512]]), QTS[:, h % 2, :], [('QTS', h % 2), 'QTS96'], [('QT', s)])
            for t in range(4):
                ps, pk = psum()
                for kt in range(8):
                    mm(ps[:, 0:512], XNT[:, xs, kt, t * 128:(t + 1) * 128], WL[:, kt, 768:1280], kt == 0, kt == 7,
                       ['WL', xk], [pk])
                tt('dve', CV[:, 0, :], ps[:, 0:512], bbc[:, 528:1040], ALU.add, [pk, 'bbc'], [('CV', 0)])
                act(MOS[:, t % 2, :], CV[:, 0, :], AF.Sigmoid, [('CV', 0)], [('MOS', t % 2)])
                dma(dap(LMO[s], ((step - 1) * 4 + t) * 512, [[NLOC[s] * 4 * 512, 128], [1, 512]]),
                    MOS[:, t % 2, :], [('MOS', t % 2)], [('LMO', s)])

        def M_step(step, deferred_meta=False):
            if False:
                yield
            is_meta = (step == 0)
            ntt = 1 if is_meta else 4
            ntok = ntt * 128
            xs = step % 3
            local = (1 <= step <= nl)
            nh_ = 8 if (is_meta or local) else 4
            lcol0 = (step - 1) * 512
            if not is_meta:
                if step == nsup:
                    src = SM[:, 24:24 + nh_]; sk = 'first'
                else:
                    src = MKr[:, (step + 1) % 3, 0:nh_, 1]; sk = ('MK', (step + 1) % 3)
                ts('dve', MKr[:, xs, 0:nh_, 513], src, mcol(step, 3), None, ALU.mult, None, [sk, 'MSK'], [('MK', xs)])
            if not is_meta or not deferred_meta:
                for h in range(nh_):
                    x0 = MKr[:, xs, h, 0:ntok]; x1 = MKr[:, xs, h, 1:1 + ntok]; x2 = MKr[:, xs, h, 2:2 + ntok]
                    acc = CV[:, h % 2, 0:ntok]
                    ck = ('CV', h % 2)
                    ts('dve', acc, x1, cw[:, h * 3 + 1:h * 3 + 2], cb[:, h:h + 1], ALU.mult, ALU.add,
                       [('MK', xs), 'cw', 'cb'], [ck])
                    stt(acc, x0, cw[:, h * 3:h * 3 + 1], acc, ALU.mult, ALU.add, [('MK', xs), 'cw', ck], [ck])
                    stt(acc, x2, cw[:, h * 3 + 2:h * 3 + 3], acc, ALU.mult, ALU.add, [('MK', xs), 'cw', ck], [ck])
                    act(KTb[:, h, 0:ntok], acc, AF.Silu, [ck], [('KTb', h)])
                    yield
                for t in range(ntt):
                    ps, pk = psum()
                    psb = ps.bitcast(BF)
                    for h in range(4):
                        tr(psb[:, h * 128:(h + 1) * 128], KTb[:, h, t * 128:(t + 1) * 128], [('KTb', h), 'Cb'], [pk])
                    kdst = KTK[:, t, :] if not is_meta else KTKm[:, :]
                    cp('act', kdst, psb[:, 0:512], [pk], [('KTK', t) if not is_meta else 'KTKm'])
                    yield
            if local:
                nq = NLOC[s] * 512
                for h in range(4):
                    dma(dap(LK[s], h * 128 * nq + lcol0, [[nq, 128], [1, 512]]), KTb[:, h, :], [('KTb', h)], [('LK', s)])
                    dma(dap(LQ[s], h * 128 * nq + lcol0, [[nq, 128], [1, 512]]), KTb[:, 4 + h, :], [('KTb', 4 + h)], [('LQ', s)])
                dma(dap(LKT[s], (step - 1) * 4 * 512, [[NLOC[s] * 4 * 512, 128], [1, 2048]]),
                    KTK[:, :, :].rearrange("p t c -> p (t c)"), [('KTK', t) for t in range(4)], [('LKT', s)])
                for t in range(4):
                    cp('pool', VAL[:, t, :].rearrange("p (h d) -> p h d", h=4)[:, :, 0:128],
                       VR[:, xs, t, :].rearrange("p (h d) -> p h d", h=4), [('VR', xs)], [('VAL', t)])
                    S.op('pool', lambda e, t=t: e.memset(VAL[:, t, :].rearrange("p (h d) -> p h d", h=4)[:, :, 128:129], 1.0),
                         [], [('VAL', t)])
                dma(dap(LV[s], (step - 1) * 4 * 516, [[NLOC[s] * 4 * 516, 128], [1, 4 * 516]]),
                    VAL[:, :, :].rearrange("p t c -> p (t c)"), [('VAL', t) for t in range(4)], [('LV', s)])
                return
            if is_meta and not deferred_meta:
                cp('pool', VRm[:, :], VR[:, xs, 0, :], [('VR', xs)], ['VRm'])
                cp('pool', GRm[:, :], GR[:, xs, 0, :], [('GR', xs)], ['GRm'])
                return
            if is_meta:
                G = GRm[:, :].unsqueeze(1)
                gk = 'GRm'
            else:
                G = GR[:, xs, :, :]
                gk = ('GR', xs)
            nt = ntt
            G4 = G.rearrange("p t (d j h) -> p t d j h", d=2, j=2)
            fpre = G4[:, :, :, 1, :]
            ipre = G4[:, :, :, 0, :]
            E1 = GT[:, 0, :].rearrange("p (a b) -> p a b", a=2)[:, :, :]
            A_ = GT[:, 1:1 + nt, 0:8].rearrange("p t (d h) -> p t d h", d=2)
            T_ = GT[:, 5:5 + nt, 0:8].rearrange("p t (d h) -> p t d h", d=2)
            act(T_, fpre, AF.Exp, [gk], ['T_'], scale=-1.0)
            act(T_, T_, AF.Ln, ['T_', 'onet'], ['T_'], bias=onet[:, 0:1])
            if is_meta:
                ts('dve', MW[:, 2:3], vmeta[:, 0:1], -1.0, None, ALU.mult, None, ['vmeta'], ['MW2'])
                ts('dve', MW[:, 3:4], vmeta[:, 0:1], 0.0, None, ALU.mult, None, ['vmeta'], ['MW3'])
                wm0 = vmeta[:, 0:1]
            else:
                ts('dve', MW[:, 2:3], mcol(step, 0), -1.0, None, ALU.mult, None, ['MSK'], ['MW2'])
                ts('dve', MW[:, 3:4], mcol(step, 1), -1.0, None, ALU.mult, None, ['MSK'], ['MW3'])
            for d_ in range(2):
                ts('dve', A_[:, :, d_, :], T_[:, :, d_, :], MW[:, 2 + d_:3 + d_], None, ALU.mult, None,
                   ['T_', 'MW2', 'MW3'], ['A_'])
            AH = AHL[:, 0, 0:nt, :]; AL = AHL[:, 1, 0:nt, :]
            cp('dve', AH, GT[:, 1:1 + nt, 0:8], ['A_'], ['AH'])
            tt('dve', AL, GT[:, 1:1 + nt, 0:8], AH, ALU.subtract, ['A_', 'AH'], ['AL'])
            ps, pk = psum()
            for t in range(nt):
                for (pp, kk, first) in ((AH, 'AH', True), (AL, 'AL', False)):
                    mm(ps[:, t * 16:t * 16 + 4], triU_b, pp[:, t, 0:4], first, not first, ['Cb', kk], [pk])
                for (pp, kk, first) in ((AH, 'AH', True), (AL, 'AL', False)):
                    mm(ps[:, t * 16 + 4:t * 16 + 8], triL_b, pp[:, t, 4:8], first, not first, ['Cb', kk], [pk])
                for (pp, kk, first) in ((AH, 'AH', True), (AL, 'AL', False)):
                    mm(ps[:, t * 16 + 8:t * 16 + 16], ones_b, pp[:, t, 0:8], first, not first, ['Cb', kk], [pk])
            CS = GT[:, 9:9 + nt, :]
            cp('dve', CS, ps[:, 0:nt * 16].rearrange("p (t c) -> p t c", t=nt), [pk], ['CS'])
            yield
            SFX = GT[:, 13, :].rearrange("p (t h) -> p t h", t=4)
            PFX = GT[:, 14, :].rearrange("p (t h) -> p t h", t=4)
            S.op('dve', lambda e: e.memset(GT[:, 13:15, :], 0.0), [], ['SFX', 'PFX'])
            if is_meta:
                cp('dve', SFX[:, 0, :], SM[:, 4:8], ['Hrun'], ['SFX'])
            else:
                for t in range(nt - 2, -1, -1):
                    tt('dve', SFX[:, t, :], SFX[:, t + 1, :], CS[:, t + 1, 8:12], ALU.add, ['SFX', 'CS'], ['SFX'])
                cp('dve', PFX[:, 0, :], SM[:, 0:4], ['Grun'], ['PFX'])
                for t in range(1, nt):
                    tt('dve', PFX[:, t, :], PFX[:, t - 1, :], CS[:, t - 1, 12:16], ALU.add, ['PFX', 'CS'], ['PFX'])
                tt('dve', SM[:, 0:4], PFX[:, nt - 1, :], CS[:, nt - 1, 12:16], ALU.add, ['PFX', 'CS'], ['Grun'])
                tt('dve', GT[:, 15, 0:4], SFX[:, 0, :], CS[:, 0, 8:12], ALU.add, ['SFX', 'CS'], ['FLS'])
                tt('dve', SM[:, 4:8], SM[:, 4:8], GT[:, 15, 0:4], ALU.add, ['Hrun', 'FLS'], ['Hrun'])
            ARG = GT[:, 5:5 + nt, 8:16].rearrange("p t (d h) -> p t d h", d=2)
            tt('dve', ARG[:, :, 0, :], ipre[:, :, 0, :], CS[:, :, 0:4], ALU.subtract, [gk, 'CS'], ['ARG'])
            tt('dve', ARG[:, :, 1, :], ipre[:, :, 1, :], CS[:, :, 4:8], ALU.subtract, [gk, 'CS'], ['ARG'])
            tt('dve', ARG[:, :, 0, :], ARG[:, :, 0, :], CS[:, :, 8:12], ALU.add, ['ARG', 'CS'], ['ARG'])
            tt('dve', ARG[:, :, 1, :], ARG[:, :, 1, :], CS[:, :, 12:16], ALU.add, ['ARG', 'CS'], ['ARG'])
            tt('dve', ARG[:, :, 0, :], ARG[:, :, 0, :], SFX[:, 0:nt, :], ALU.add, ['ARG', 'SFX'], ['ARG'])
            tt('dve', ARG[:, :, 1, :], ARG[:, :, 1, :], PFX[:, 0:nt, :], ALU.add, ['ARG', 'PFX'], ['ARG'])
            Wt = GT[:, 1:1 + nt, 8:16].rearrange("p t (d h) -> p t d h", d=2)
            yield
            act(Wt, ARG, AF.Exp, ['ARG', 'lnks'], ['Wt'], bias=lnks[:, 0:1])
            if is_meta:
                ts('dve', Wt[:, :, 0, :], Wt[:, :, 0, :], vmeta[:, 0:1], None, ALU.mult, None, ['Wt', 'vmeta'], ['Wt'])
            else:
                for d_ in range(2):
                    ts('dve', Wt[:, :, d_, :], Wt[:, :, d_, :], mcol(step, d_), None, ALU.mult, None, ['Wt', 'MSK'], ['Wt'])
            if not is_meta:
                act(GT[:, 15, 4:8], GT[:, 15, 0:4], AF.Exp, ['FLS'], ['EFL'])
            ndir = 1 if is_meta else 2
            for t in range(nt):
                for d_ in range(ndir):
                    vsrc = (VRm[:, :] if is_meta else VR[:, xs, t, :]).rearrange("p (h d) -> p h d", h=4)
                    vk = 'VRm' if is_meta else ('VR', xs)
                    wv = WV[:, t, d_, :].rearrange("p (h d) -> p h d", h=4)
                    tt('dve', wv[:, :, 0:128], vsrc,
                       Wt[:, t, d_, :].unsqueeze(2).to_broadcast([128, 4, 128]), ALU.mult, [vk, 'Wt'], [('WV', t, d_)])
                    cp('pool', wv[:, :, 128:129], Wt[:, t, d_, :].unsqueeze(2), ['Wt'], [('WV', t, d_)])
                    yield
            for d_ in range(ndir):
                for h in range(4):
                    ps, pk = psum()
                    for t in range(nt):
                        ksrc = KTKm[:, h * 128:(h + 1) * 128] if is_meta else KTK[:, t, h * 128:(h + 1) * 128]
                        kk = 'KTKm' if is_meta else ('KTK', t)
                        mm(ps[:, 0:129], ksrc, WV[:, t, d_, h * 129:(h + 1) * 129], t == 0, t == nt - 1,
                           [kk, ('WV', t, d_)], [pk])
                    if d_ == 0 and not is_meta:
                        stt(ST_[:, 0, h * 129:(h + 1) * 129], ST_[:, 0, h * 129:(h + 1) * 129], GT[:, 15, 4 + h:5 + h],
                            ps[:, 0:129], ALU.mult, ALU.add, ['Cf', 'EFL', pk], ['Cf'])
                    else:
                        key = 'Cf' if d_ == 0 else 'Bb'
                        tt('dve', ST_[:, d_, h * 129:(h + 1) * 129], ST_[:, d_, h * 129:(h + 1) * 129], ps[:, 0:129],
                           ALU.add, [key, pk], [key])
                    yield

        oo = o
        KTKm, oo = carve(oo, [128, 512], BF)
        VRm, oo = carve(oo, [128, 512], BF)
        GRm, oo = carve(oo, [128, 16], F32)
        assert oo <= 34 * 1024 + 3072

        def ntiles(step):
            return 1 if step == 0 else 4

        load_x(0)
        for t in range(ntiles(0)):
            N_tile(0, t)
        load_x(1)
        for step in range(nsteps):
            if step + 1 < nsteps:
                for t in range(ntiles(step + 1)):
                    pending.append(once(lambda st_=step + 1, t=t: N_tile(st_, t)))
                if step + 2 < nsteps:
                    pending.append(once(lambda st_=step + 2: load_x(st_, 'x')))
            A_step(step)
            while pending:
                pump()
        for _ in tail_gen(nsteps - 1):
            pass
        for _ in M_step(nsteps - 2):
            pass
        for _ in M_step(nsteps - 1):
            pass
        for _ in M_step(0, deferred_meta=True):
            pass
        if DEBUG and s == 1:
            dma(dbg['kn'].ap(), KN[s].ap(), [('KN', s)], ['dbg1'])
            dma(dbg['kr'].ap(), KR[s].ap(), [('KR', s)], ['dbg2'])
            dma(dbg['vs'].ap().rearrange("h p n c -> (h p) (n c)"), VS[s].ap().rearrange("h p n c -> (h p) (n c)"), [('VS', s)], ['dbg3'])
            dma(dbg['qt'].ap().rearrange("h r n -> (h r) n"), QT[s].ap().rearrange("h r n -> (h r) n"), [('QT', s)], ['dbg4'])
            dma(dbg['lq'].ap(), LQ[s].ap(), [('LQ', s)], ['dbg5'])
            dma(dbg['lk'].ap(), LK[s].ap(), [('LK', s)], ['dbg6'])
        if DEBUG:
            dma(dap(dbg['cf'], s * 128 * 516, [[516, 128], [1, 516]]), ST_[:, 0, :], ['Cf'], ['dbgcf'])
            dma(dap(dbg['bb'], s * 128 * 516, [[516, 128], [1, 516]]), ST_[:, 1, :], ['Bb'], ['dbgbb'])
        if stage < 2:
            continue
        S.barrier()
        mlstm_local(s)
        S.barrier()

    if stage >= 3:
        S.barrier()
        for s in range(NSEQ):
            attention(s)
    if stage >= 4:
        S.barrier()
        ffn_phases()

    S.emit(nc, st)
    st.close()
    return nc


def _rope_tables(pos):
    half = QK_ROPE // 2
    freqs = (10000.0 ** (-np.arange(half, dtype=np.float32) / half)).astype(np.float32)
    ang = pos.astype(np.float32)[None, :] * freqs[:, None]
    c = np.cos(ang).astype(np.float32)
    s_ = np.sin(ang).astype(np.float32)
    return np.concatenate([c, c], 0), np.concatenate([-s_, s_], 0)


def _consts():
    i = np.arange(128)
    ident = np.eye(128, dtype=np.float32)
    triU = (i[:, None] <= i[None, :]).astype(np.float32)
    triL = (i[:, None] >= i[None, :]).astype(np.float32)
    ones = np.ones((128, 128), np.float32)
    return np.concatenate([ident, triU, triL, ones, triU, triL], 1)


def prep_inputs(inp):
    f = lambda a: np.ascontiguousarray(np.asarray(a, dtype=np.float32))
    xs = [f(inp["x_prompt"])[0], f(inp["x_sample"])[0], f(inp["x_sample"])[1]]
    meta = f(inp["meta_tokens"])
    w_in = f(inp["w_in"])[0]; b_in = f(inp["b_in"])[0]
    o_cq, o_ckv, o_kr, o_mq, o_mk, o_mv, o_mo, o_g, o_ga, o_gb = np.cumsum([0, 256, 128, 32, 512, 512, 512, 512, 16, 1024])
    rot = np.concatenate([np.arange(16, 32), np.arange(0, 16)])
    cols_a = np.concatenate([np.arange(o_ckv, o_ckv + 128), o_kr + np.arange(32), o_kr + rot,
                             np.arange(o_mk, o_mk + 512), np.arange(o_mv, o_mv + 512), np.arange(o_g, o_g + 16)])
    cols_l = np.concatenate([np.arange(o_cq, o_cq + 256), np.arange(o_mq, o_mq + 512), np.arange(o_mo, o_mo + 512)])
    cols_g = np.arange(o_ga, o_ga + 2048)
    w_uq = f(inp["w_uq"])[0]
    cu = []
    for h in range(NH):
        b = h * QK_DIM
        cu += [b + np.arange(64), b + 64 + np.arange(32), b + 64 + rot]
    w_uq_e = np.ascontiguousarray(w_uq[:, np.concatenate(cu)])
    w_ukv = f(inp["w_ukv"])[0]
    ck = np.concatenate([h * 128 + np.arange(64) for h in range(NH)])
    cv = np.concatenate([h * 128 + 64 + np.arange(64) for h in range(NH)])
    w_ukv_e = np.ascontiguousarray(w_ukv[:, np.concatenate([ck, cv])])
    conv_w = f(inp["conv_w"])[0]; conv_b = f(inp["conv_b"])[0]
    cwt = np.zeros((128, 8, 3), np.float32); cbt = np.zeros((128, 8), np.float32)
    for h in range(4):
        cwt[:, h, :] = conv_w[:, 512 + h * 128:512 + (h + 1) * 128].T
        cwt[:, 4 + h, :] = conv_w[:, h * 128:(h + 1) * 128].T
        cbt[:, h] = conv_b[512 + h * 128:512 + (h + 1) * 128]
        cbt[:, 4 + h] = conv_b[h * 128:(h + 1) * 128]
    colmaj = lambda v, n: np.ascontiguousarray(f(v).reshape(n, 128).T)
    shared = {
        "cst": _consts(),
        "vmeta": ((np.arange(128) >= META_LO) & (np.arange(128) < META_HI)).astype(np.float32)[:, None].copy(),
        "w_a": np.ascontiguousarray(w_in[:, cols_a]), "b_a": np.ascontiguousarray(b_in[cols_a])[None],
        "w_l": np.ascontiguousarray(w_in[:, cols_l]), "b_l": np.ascontiguousarray(b_in[cols_l])[None],
        "w_g": np.ascontiguousarray(w_in[:, cols_g]), "b_g": np.ascontiguousarray(b_in[cols_g])[None],
        "w_uq": w_uq_e, "w_ukv": w_ukv_e,
        "w_pa": f(inp["w_pa"])[0], "w_pb": f(inp["w_pb"])[0], "w_o": f(inp["w_o"])[0],
        "w_gu": np.ascontiguousarray(np.concatenate([f(inp["w_ffn_gate"])[0], f(inp["w_ffn_up"])[0]], 1)),
        "w_dn": f(inp["w_ffn_down"])[0],
        "g1": colmaj(inp["norm1_g"], 8), "g2": colmaj(inp["norm2_g"], 8),
        "gq": colmaj(inp["q_norm_g"], 2), "gkv": colmaj(inp["kv_norm_g"], 1), "gm": colmaj(inp["m_norm_g"], 4),
        "gfin": f(inp["final_norm_g"])[None],
        "cw": cwt.reshape(128, 24), "cb": cbt,
    }
    in_maps = []
    for c in range(NCORE):
        m = dict(shared)
        for s in range(NSEQ):
            nsup, nl = NSUP[s], NLOC[s]
            L0 = c * nl
            order = [(L0 + i) % nsup for i in range(nsup)]
            x = xs[s]
            mt = np.zeros((128, D), np.float32)
            mt[0] = meta[15] if L0 == 0 else x[L0 * 512 - 1]
            mt[META_LO:META_HI] = meta
            mt[127] = x[0]
            xr = x.reshape(nsup, 512, D)[order].reshape(nsup * 512, D)
            m["xin%d" % s] = np.concatenate([mt, xr], 0)
            m["xloc%d" % s] = np.ascontiguousarray(x[L0 * 512:(L0 + nl) * 512])
            pos = np.zeros(TK(s), np.float32)
            pos[META_LO:META_HI] = np.arange(16)
            for i, su in enumerate(order):
                pos[128 + i * 512:128 + (i + 1) * 512] = 16 + su * 512 + np.arange(512)
            ct, sn = _rope_tables(pos)
            m["cos%d" % s] = ct; m["sin%d" % s] = sn
            mk = np.zeros((1 + nsup, 4), np.float32)
            mk[0] = [1, 0, 0, 0]
            for i, su in enumerate(order):
                st_ = i + 1
                before = su < L0
                after = su >= L0 + nl
                wl = 0.0 if su == 0 else 1.0
                wr = 0.0 if su == nsup - 1 else 1.0
                mk[st_] = [float(before), float(after), wl, wr]
            m["msk%d" % s] = np.ascontiguousarray(np.broadcast_to(mk.reshape(1, -1), (128, (1 + nsup) * 4)))
        in_maps.append(m)
    return in_maps


_NC_CACHE = {}


def kernel(**inputs):
    in_maps = prep_inputs(inputs)
    if 'nc' not in _NC_CACHE:
        _NC_CACHE['nc'] = build()
    nc = _NC_CACHE['nc']
    res = run_bass_kernel_spmd(nc, in_maps, core_ids=list(range(NCORE)))
    outs = []
    yp = np.concatenate([res.results[c]["y0"] for c in range(NCORE)], 0)[None]
    ys = np.stack([np.concatenate([res.results[c]["y%d" % s] for c in range(NCORE)], 0) for s in (1, 2)], 0)
    return (np.ascontiguousarray(yp.astype(np.float32)), np.ascontiguousarray(ys.astype(np.float32)))
```

```python
import numpy as np
from contextlib import ExitStack
import concourse.bass as bass
import concourse.mybir as mybir
from concourse.bass_utils import run_bass_kernel_spmd

F32 = mybir.dt.float32
BF = mybir.dt.bfloat16
AF = mybir.ActivationFunctionType
ALU = mybir.AluOpType
AX = mybir.AxisListType

NCORE = 8
D = 1024
P = 128
NSUP = (32, 16, 16)
NLOC = (4, 2, 2)
NSEQ = 3
QK_NOPE, QK_ROPE, V_HEAD, NH = 64, 32, 64, 8
QK_DIM = 96
MH, MD = 4, 128
D_FF = 2816
EPS = 1e-6
SM_SCALE = QK_DIM ** -0.5
K_SCALE = MD ** -0.5
META_LO, META_HI = 111, 127
NA = 128 + 64 + 512 + 512 + 16
NL_ = 256 + 512 + 512
DEBUG = False


def TK(s):
    return (1 + 4 * NSUP[s]) * 128


class Sched:
    LIMIT = 20000
    R = 24
    DMAQ = ('sp',)

    def __init__(self):
        self.ops = []
        self.lw = {}
        self.rd = {}
        self.bar = None

    def _last_ops(self):
        last = {}
        dl = {}
        for i, o in enumerate(self.ops):
            if o[0] in self.DMAQ:
                dl.setdefault(o[0], []).append(i)
            else:
                last[o[0]] = i
        s = set(last.values())
        for e, l in dl.items():
            s.update(l[-self.R:])
        return s

    def barrier(self):
        self.bar = (self._last_ops(), set())
        self.lw = {}
        self.rd = {}

    def op(self, eng, fn, r=(), w=()):
        i = len(self.ops)
        hard, soft = set(), set()
        if self.bar is not None and eng not in self.bar[1]:
            hard.update(self.bar[0])
            self.bar[1].add(eng)
        isd = eng in self.DMAQ
        for k in r:
            hard.update(self.lw.get(k, ()))
        for k in w:
            hard.update(self.lw.get(k, ()))
            for kk, v in self.rd.get(k, {}).items():
                if isinstance(v, list):
                    soft.update(v)
                else:
                    soft.add(v)
        self.ops.append([eng, fn, hard, soft])
        for k in r:
            d = self.rd.setdefault(k, {})
            if isd:
                d.setdefault(('dma', eng), []).append(i)
            else:
                d[eng] = i
        for k in w:
            if isd and not self.rd.get(k) and k in self.lw and all(self.ops[x][0] in self.DMAQ for x in self.lw[k]):
                self.lw[k] = set(self.lw[k]) | {i}
            else:
                self.lw[k] = {i}
            self.rd[k] = {}
        return i

    def emit(self, nc, stack):
        ops = self.ops
        n = len(ops)
        dmaq = self.DMAQ
        need = [False] * n
        deps = [None] * n
        for i, (eng, fn, hard, soft) in enumerate(ops):
            dl = {}
            dd = set()
            for d, is_hard in [(x, True) for x in hard] + [(x, False) for x in soft]:
                if d >= i:
                    continue
                e2 = ops[d][0]
                if e2 in dmaq:
                    dd.add(d)
                    continue
                if e2 == eng:
                    if eng == 'pe' or not is_hard:
                        continue
                if e2 not in dl or dl[e2] < d:
                    dl[e2] = d
            deps[i] = (dl, sorted(dd))
            for d in dl.values():
                need[d] = True
        sig = [None] * n
        cnt, ep = {}, {}
        dcount = {}
        for i, o in enumerate(ops):
            e = o[0]
            if e in dmaq:
                j = dcount.get(e, 0)
                dcount[e] = j + 1
                sig[i] = (e, 'd%d' % (j % self.R), j // self.R + 1)
                continue
            if not need[i]:
                continue
            c = cnt.get(e, 0) + 1
            if c > self.LIMIT:
                ep[e] = ep.get(e, 0) + 1
                c = 1
            cnt[e] = c
            sig[i] = (e, ep.get(e, 0), c)
        sems = {}
        for s in sig:
            if s is not None and (s[0], s[1]) not in sems:
                sems[(s[0], s[1])] = stack.enter_context(nc.semaphore("s_%s_%s" % (s[0], s[1])))
        per_eng = {}
        for i, o in enumerate(ops):
            per_eng.setdefault(o[0], []).append(i)
        final = self._last_ops()

        def run(eng_name, handle):
            waited = {}

            def wait_for(d):
                s = sig[d]
                key = (s[0], s[1])
                if waited.get(key, 0) >= s[2]:
                    return
                handle.wait_ge(sems[key], s[2] * (16 if s[0] in dmaq else 1))
                waited[key] = s[2]

            for i in per_eng.get(eng_name, []):
                dl, dd = deps[i]
                for e2, d in dl.items():
                    wait_for(d)
                for d in dd:
                    wait_for(d)
                if eng_name in dmaq:
                    s = sig[i]
                    if s[2] > 1:
                        key = (s[0], s[1])
                        if waited.get(key, 0) < s[2] - 1:
                            handle.wait_ge(sems[key], (s[2] - 1) * 16)
                            waited[key] = s[2] - 1
                ins = ops[i][1](handle)
                if sig[i] is not None:
                    s = sig[i]
                    ins.then_inc(sems[(s[0], s[1])], 16 if eng_name in dmaq else 1)
            if eng_name in dmaq:
                for d in sorted(final):
                    if ops[d][0] in dmaq:
                        wait_for(d)

        with nc.Block() as block:
            @block.tensor
            def _(t):
                run('pe', t)

            @block.scalar
            def _(t):
                run('act', t)

            @block.vector
            def _(t):
                run('dve', t)

            @block.gpsimd
            def _(t):
                run('pool', t)

            @block.sync
            def _(t):
                run('sp', t)


def dap(t, off, pat):
    return bass.AP(t, off, [list(p) for p in pat])


def build(stage=99):
    nc = bass.Bass("TRN2", target_bir_lowering=False)
    S = Sched()
    st = ExitStack()
    di = {}

    def din(name, shape, dt=F32):
        di[name] = nc.dram_tensor(name, list(shape), dt, kind="ExternalInput")
        return di[name]

    def dscr(name, shape, dt=BF):
        return nc.dram_tensor(name, list(shape), dt, kind="Internal")

    def dout(name, shape, dt=F32):
        return nc.dram_tensor(name, list(shape), dt, kind="ExternalOutput")

    def sb(name, shape, dt=F32):
        return st.enter_context(nc.sbuf_tensor("sb_" + name, list(shape), dt))

    xin = [din("xin%d" % s, [128 + NSUP[s] * 512, D]) for s in range(NSEQ)]
    xloc = [din("xloc%d" % s, [NLOC[s] * 512, D]) for s in range(NSEQ)]
    cosd = [din("cos%d" % s, [32, TK(s)]) for s in range(NSEQ)]
    sind = [din("sin%d" % s, [32, TK(s)]) for s in range(NSEQ)]
    mskd = [din("msk%d" % s, [128, (1 + NSUP[s]) * 4]) for s in range(NSEQ)]
    cst = din("cst", [128, 6 * 128])
    vmeta_d = din("vmeta", [128, 1])
    w_a_d = din("w_a", [D, NA]); b_a_d = din("b_a", [1, NA])
    w_l_d = din("w_l", [D, NL_]); b_l_d = din("b_l", [1, NL_])
    w_g_d = din("w_g", [D, 2048]); b_g_d = din("b_g", [1, 2048])
    w_uq_d = din("w_uq", [256, 1024])
    w_ukv_d = din("w_ukv", [128, 1024])
    w_pa_d = din("w_pa", [512, D]); w_pb_d = din("w_pb", [512, D]); w_o_d = din("w_o", [D, D])
    w_gu_d = din("w_gu", [D, 2 * D_FF]); w_dn_d = din("w_dn", [D_FF, D])
    g1_d = din("g1", [128, 8]); g2_d = din("g2", [128, 8])
    gq_d = din("gq", [128, 2]); gkv_d = din("gkv", [128, 1]); gm_d = din("gm", [128, 4])
    gfin_d = din("gfin", [1, D])
    cw_d = din("cw", [128, 8 * 3]); cb_d = din("cb", [128, 8])

    y = [dout("y%d" % s, [NLOC[s] * 512, D]) for s in range(NSEQ)]

    KN = [dscr("kn%d" % s, [NH * 64, TK(s)]) for s in range(NSEQ)]
    KR = [dscr("kr%d" % s, [33, TK(s)]) for s in range(NSEQ)]
    VS = [dscr("vs%d" % s, [NH, 128, TK(s) // 128, 65]) for s in range(NSEQ)]
    QT = [dscr("qt%d" % s, [NH, 97, NLOC[s] * 512]) for s in range(NSEQ)]
    LQ = [dscr("lq%d" % s, [MH * 128, NLOC[s] * 512]) for s in range(NSEQ)]
    LK = [dscr("lk%d" % s, [MH * 128, NLOC[s] * 512]) for s in range(NSEQ)]
    LKT = [dscr("lkt%d" % s, [128, NLOC[s] * 4, 512]) for s in range(NSEQ)]
    LV = [dscr("lv%d" % s, [128, NLOC[s] * 4, MH * 129]) for s in range(NSEQ)]
    LG = [dscr("lg%d" % s, [128, NLOC[s] * 4, 16], F32) for s in range(NSEQ)]
    LMO = [dscr("lmo%d" % s, [128, NLOC[s] * 4, 512]) for s in range(NSEQ)]
    AOT = [dscr("aot%d" % s, [512, NLOC[s] * 512]) for s in range(NSEQ)]
    RIV = [dscr("riv%d" % s, [NH, NLOC[s] * 512], F32) for s in range(NSEQ)]
    MOT = [dscr("mot%d" % s, [512, NLOC[s] * 512]) for s in range(NSEQ)]
    H1 = [dscr("h1_%d" % s, [NLOC[s] * 512, D], F32) for s in range(NSEQ)]
    X2T = [dscr("x2t%d" % s, [D, NLOC[s] * 512]) for s in range(NSEQ)]
    HT = [dscr("ht%d" % s, [D_FF, NLOC[s] * 512]) for s in range(NSEQ)]
    dbg = {}
    if DEBUG:
        dbg['cf'] = dout("dbg_cf", [NSEQ, 128, MH * 129])
        dbg['bb'] = dout("dbg_bb", [NSEQ, 128, MH * 129])
        dbg['kn'] = dout("dbg_kn", [NH * 64, TK(1)], BF)
        dbg['kr'] = dout("dbg_kr", [33, TK(1)], BF)
        dbg['vs'] = dout("dbg_vs", [NH, 128, TK(1) // 128, 65], BF)
        dbg['qt'] = dout("dbg_qt", [NH, 97, NLOC[1] * 512], BF)
        dbg['lq'] = dout("dbg_lq", [MH * 128, NLOC[1] * 512], BF)
        dbg['lk'] = dout("dbg_lk", [MH * 128, NLOC[1] * 512], BF)
        dbg['aot'] = dout("dbg_aot", [512, NLOC[1] * 512], BF)
        dbg['mot'] = dout("dbg_mot", [512, NLOC[1] * 512], BF)

    C = sb("cst", [128, 6 * 128])
    Cb = sb("cstb", [128, 6 * 128], BF)
    ident_b = Cb[:, 0:128]
    ones_b = Cb[:, 3 * 128:4 * 128]
    triU_b = Cb[:, 128:256]; triL_b = Cb[:, 256:384]
    triU = C[:, 128:256]; triL = C[:, 256:384]; onesF = C[:, 384:512]
    maskF = C[:, 512:640]; maskB = C[:, 640:768]
    vmeta = sb("vmeta", [128, 1])
    epst = sb("epst", [128, 1]); onet = sb("onet", [128, 1]); lnks = sb("lnks", [128, 1])
    g1 = sb("g1", [128, 8]); g2 = sb("g2", [128, 8]); gq = sb("gq", [128, 2]); gkv = sb("gkv", [128, 1])
    gm = sb("gm", [128, 4])
    cw = sb("cw", [128, 24]); cb = sb("cb", [128, 8])
    WA = sb("WA", [128, 8, NA], BF)
    WL = sb("WL", [128, 8, NL_], BF)
    WUQ = sb("WUQ", [128, 2, 1024], BF)
    WUKV = sb("WUKV", [128, 1024], BF)
    bA = sb("bA", [128, 2])
    bAm = sb("bAm", [128, 4])
    bLq = sb("bLq", [128, 2 + 4])
    bbc = sb("bbc", [128, 512 + 16 + 512])
    stage_t = sb("stage", [128, 1024])
    bG_t = sb("bG", [128, 16])
    S4_t = sb("S4", [128, 16])
    BIG = sb("BIG", [128, 34 * 1024 + 3072])

    def carve(off_words, shape, dt):
        n = int(np.prod(shape[1:]))
        words = n if dt == F32 else (n + 1) // 2
        v = BIG[0:shape[0], off_words:off_words + words]
        if dt == BF:
            v = v.bitcast(BF)[:, 0:n]
        if len(shape) == 3:
            v = v.rearrange("p (a b) -> p a b", a=shape[1])
        elif len(shape) == 4:
            v = v.rearrange("p (a b c) -> p a b c", a=shape[1], b=shape[2])
        return v, off_words + words

    PSA = st.enter_context(nc.psum_tensor("psa", [128, 8 * 512], F32))
    PSAb = PSA.bitcast(BF)

    class _Bank:
        def __init__(self, i):
            self.i = i

        def __getitem__(self, idx):
            p, c = idx
            c0 = 0 if c.start is None else c.start
            c1 = 512 if c.stop is None else c.stop
            return PSA[p, self.i * 512 + c0:self.i * 512 + c1]

        def bitcast(self, dt):
            b = self

            class _B:
                def __getitem__(self, idx):
                    p, c = idx
                    c0 = 0 if c.start is None else c.start
                    c1 = 1024 if c.stop is None else c.stop
                    return PSAb[p, b.i * 1024 + c0:b.i * 1024 + c1]
            return _B()
    PS = [_Bank(i) for i in range(8)]
    psn = [0]

    def psum():
        i = psn[0] % 8
        psn[0] += 1
        return PS[i], ('ps', i)

    def dma(out, in_, r, w, q='sp'):
        S.op(q, lambda e: e.dma_start(out=out, in_=in_, allow_slow_non_contiguous=True), r, w)

    def act(out, in_, func, r, w, bias=None, scale=None, accum=None):
        kw = {}
        if bias is not None:
            kw['bias'] = bias
        if scale is not None:
            kw['scale'] = scale
        if accum is not None:
            kw['accum_out'] = accum
        S.op('act', lambda e: e.activation(out, in_, func, **kw), r, w)

    def ts(eng, out, in0, s1, s2, op0, op1, r, w):
        if op1 is None:
            S.op(eng, lambda e: e.tensor_scalar(out, in0, s1, None, op0), r, w)
        else:
            S.op(eng, lambda e: e.tensor_scalar(out, in0, s1, s2, op0, op1), r, w)

    def tt(eng, out, in0, in1, op, r, w):
        S.op(eng, lambda e: e.tensor_tensor(out, in0, in1, op), r, w)

    def stt(out, in0, sc, in1, op0, op1, r, w):
        S.op('dve', lambda e: e.scalar_tensor_tensor(out, in0, sc, in1, op0, op1), r, w)

    def cp(eng, out, in_, r, w):
        if eng == 'act':
            S.op('act', lambda e: e.copy(out, in_), r, w)
        else:
            S.op(eng, lambda e: e.tensor_copy(out, in_), r, w)

    def mm(out, lhsT, rhs, start, stop, r, w):
        S.op('pe', lambda e: e.matmul(out, lhsT, rhs, start=start, stop=stop), r, w)

    def tr(out, in_, r, w):
        S.op('pe', lambda e: e.transpose(out, in_, ident_b), r, w)

    S.op('pool', lambda e: e.memset(BIG[:, :], 0.0), [], ['BIGZ'])
    S.barrier()
    dma(C[:], cst.ap(), [], ['C'])
    cp('dve', Cb[:], C[:], ['C'], ['Cb'])
    dma(vmeta[:], vmeta_d.ap(), [], ['vmeta'])
    S.op('dve', lambda e: e.memset(epst[:], EPS), [], ['epst'])
    S.op('dve', lambda e: e.memset(onet[:], 1.0), [], ['onet'])
    S.op('dve', lambda e: e.memset(lnks[:], float(np.log(K_SCALE))), [], ['lnks'])
    for t_, d_, k_ in ((g1, g1_d, 'g1'), (g2, g2_d, 'g2'), (gq, gq_d, 'gq'), (gkv, gkv_d, 'gkv'), (gm, gm_d, 'gm'),
                       (cw, cw_d, 'cw'), (cb, cb_d, 'cb')):
        dma(t_[:], d_.ap(), [], [k_])
    dma(bA[:, 0:1], dap(b_a_d, 0, [[1, 128], [1, 1]]), [], ['bA'])
    dma(bA[0:64, 1:2], dap(b_a_d, 128, [[1, 64], [1, 1]]), [], ['bA'])
    dma(bAm[:], dap(b_a_d, 192, [[1, 128], [128, 4]]), [], ['bAm'])
    dma(bLq[:, 0:2], dap(b_l_d, 0, [[1, 128], [128, 2]]), [], ['bLq'])
    dma(bLq[:, 2:6], dap(b_l_d, 256, [[1, 128], [128, 4]]), [], ['bLq'])
    dma(bbc[:, 0:528], dap(b_a_d, 704, [[0, 128], [1, 528]]), [], ['bbc'])
    dma(bbc[:, 528:1040], dap(b_l_d, 768, [[0, 128], [1, 512]]), [], ['bbc'])

    def load_weight(dst, src, rows, cols, gain, gkey, wkey, engines=('dve', 'act'), jobs=None):
        nk = rows // 128

        def one_chunk(kt, c0, ci):
            cn = min(1024, cols - c0)
            dma(stage_t[:, 0:cn], dap(src, kt * 128 * cols + c0, [[cols, 128], [1, cn]]), [], ['stage'])
            o = dst[:, kt, c0:c0 + cn] if nk > 1 or len(dst.shape) == 3 else dst[:, c0:c0 + cn]
            eng = engines[ci % len(engines)]
            if gain is None:
                cp(eng, o, stage_t[:, 0:cn], ['stage'], [wkey])
            elif eng == 'dve':
                ts(eng, o, stage_t[:, 0:cn], gain[:, kt:kt + 1], None, ALU.mult, None,
                   ['stage', gkey], [wkey])
            else:
                act(o, stage_t[:, 0:cn], AF.Copy, ['stage', gkey], [wkey], scale=gain[:, kt:kt + 1])

        ci = 0
        for kt in range(nk):
            for c0 in range(0, cols, 1024):
                if jobs is not None:
                    jobs.append(lambda kt=kt, c0=c0, ci=ci: one_chunk(kt, c0, ci))
                else:
                    one_chunk(kt, c0, ci)
                ci += 1

    load_weight(WA, w_a_d, D, NA, g1, 'g1', 'WA')
    load_weight(WL, w_l_d, D, NL_, g1, 'g1', 'WL')
    load_weight(WUQ, w_uq_d, 256, 1024, gq, 'gq', 'WUQ')
    load_weight(WUKV, w_ukv_d, 128, 1024, gkv, 'gkv', 'WUKV')

    o = 0
    ST_, o = carve(o, [128, 2, 4 * 129], F32)
    O_PERSIST = o
    XT, o = carve(o, [128, 4, 1024], F32)
    XN, o = carve(o, [128, 2, 1024], BF)
    XNT, o = carve(o, [128, 2, 8, 512], BF)
    MKr, o = carve(o, [128, 3, 8, 514], BF)
    VR, o = carve(o, [128, 3, 4, 512], BF)
    GR, o = carve(o, [128, 3, 4, 16], F32)
    CKV, o = carve(o, [128, 2, 512], BF)
    SQ, o = carve(o, [128, 2, 512], BF)
    LNT, o = carve(o, [128, 512], F32)
    RBC, o = carve(o, [128, 512], F32)
    KNS, o = carve(o, [128, 4, 512], BF)
    KRS, o = carve(o, [33, 512], BF)
    KT1, o = carve(o, [32, 512], F32)
    KT2, o = carve(o, [32, 512], F32)
    VST, o = carve(o, [128, 4, 8, 65], BF)
    CSK, o = carve(o, [64, 2, 512], F32)
    SSQ, o = carve(o, [128, 8], F32)
    RST, o = carve(o, [128, 8], F32)
    CV, o = carve(o, [128, 2, 512], F32)
    KTb, o = carve(o, [128, 8, 512], BF)
    KTK, o = carve(o, [128, 4, 512], BF)
    WV, o = carve(o, [128, 4, 2, 4 * 129], BF)
    GT, o = carve(o, [128, 16, 16], F32)
    AHL, o = carve(o, [128, 2, 4, 8], BF)
    CQT, o = carve(o, [128, 2, 512], BF)
    QTS, o = carve(o, [97, 2, 512], BF)
    CSQ, o = carve(o, [128, 2, 512], F32)
    MOS, o = carve(o, [128, 2, 512], BF)
    VAL, o = carve(o, [128, 4, 4 * 129], BF)
    SM, o = carve(o, [128, 64], F32)
    MSK, o = carve(o, [128, 33 * 4], F32)
    MW, o = carve(o, [128, 8], F32)
    assert o <= 34 * 1024 + 3072, o


    def mlstm_local(s):
        nl = NLOC[s]
        nch = nl * 4
        nq = nl * 512
        o2 = O_PERSIST

        def two(shape, dt):
            nonlocal o2
            a, o2 = carve(o2, shape, dt)
            b, o2 = carve(o2, shape, dt)
            return (a, b)
        QTl2 = two([128, 4, 512], BF)
        KTl2 = two([128, 4, 512], BF)
        KKl2 = two([128, 4, 512], BF)
        VAl2 = two([128, 4, 516], BF)
        GRl2 = two([128, 4, 16], F32)
        G22 = two([128, 16, 16], F32)
        AH22 = two([128, 2, 4, 4], BF)
        SST2 = two([128, 4, 128], BF)
        WVl2 = two([128, 4, 130], BF)
        RC2_ = two([128, 4], F32)
        MOl, o2 = carve(o2, [128, 4, 512], BF)
        HF, o2 = carve(o2, [128, nch, 512], F32)
        HB, o2 = carve(o2, [128, nch, 512], F32)
        CBF, o2 = carve(o2, [128, 2, 4 * 130], BF)
        RC, o2 = carve(o2, [128, 8], F32)
        HG, o2 = carve(o2, [128, 512], F32)
        HSQ, o2 = carve(o2, [128, 128], F32)
        MOb, o2 = carve(o2, [128, 512], BF)
        MTs, o2 = carve(o2, [128, 4, 128], BF)
        assert o2 <= 34 * 1024 + 3072, o2

        def dir_gen(d_):
            QTl, KTl, KKl, VAl, GRl = QTl2[d_], KTl2[d_], KKl2[d_], VAl2[d_], GRl2[d_]
            G2, AH2, SST, WVl, RCd = G22[d_], AH22[d_], SST2[d_], WVl2[d_], RC2_[d_]
            HD = HF if d_ == 0 else HB
            K = lambda name: (name, d_)
            skey = 'Cf' if d_ == 0 else 'Bb'
            for h in range(4):
                cp('pool', CBF[:, d_, h * 130:h * 130 + 129], ST_[:, d_, h * 129:(h + 1) * 129], [skey], [('CBF', d_, h), (skey, h)])
            sups = range(nl) if d_ == 0 else range(nl - 1, -1, -1)
            for i in sups:
                for h in range(4):
                    dma(QTl[:, h, :], dap(LQ[s], h * 128 * nq + i * 512, [[nq, 128], [1, 512]]), [('LQ', s)], [K('QTl')])
                    dma(KTl[:, h, :], dap(LK[s], h * 128 * nq + i * 512, [[nq, 128], [1, 512]]), [('LK', s)], [K('KTl')])
                dma(KKl[:, :, :].rearrange("p t c -> p (t c)"), dap(LKT[s], i * 4 * 512, [[nch * 512, 128], [1, 2048]]), [('LKT', s)], [K('KKl')])
                dma(VAl[:, :, :].rearrange("p t c -> p (t c)"), dap(LV[s], i * 4 * 516, [[nch * 516, 128], [1, 4 * 516]]), [('LV', s)], [K('VAl')])
                dma(GRl[:, :, :].rearrange("p t c -> p (t c)"), dap(LG[s], i * 64, [[nch * 16, 128], [1, 64]]), [('LG', s)], [K('GRl')])
                G4 = GRl[:, :, :].rearrange("p t (d j h) -> p t d j h", d=2, j=2)
                fpre = G4[:, :, d_, 1, :]; ipre = G4[:, :, d_, 0, :]
                T_ = G2[:, 0:4, 0:4]; A_ = G2[:, 0:4, 4:8]; CS = G2[:, 4:8, 0:8]
                ARG = G2[:, 0:4, 8:12]; DD = G2[:, 8:12, 0:4]; WW = G2[:, 8:12, 4:8]; EF = G2[:, 8:12, 8:12]
                FLO = G2[:, 8:12, 12:16]
                act(T_, fpre, AF.Exp, [K('GRl')], [K('T2')], scale=-1.0)
                act(T_, T_, AF.Ln, [K('T2'), 'onet'], [K('T2')], bias=onet[:, 0:1])
                ts('dve', A_, T_, -1.0, None, ALU.mult, None, [K('T2')], [K('A2')])
                cp('dve', AH2[:, 0, :, :], A_, [K('A2')], [K('AH2')])
                tt('dve', AH2[:, 1, :, :], A_, AH2[:, 0, :, :], ALU.subtract, [K('A2'), K('AH2')], [K('AL2')])
                ps, pk = psum()
                tri = triU_b if d_ == 0 else triL_b
                for t in range(4):
                    for j_, kk in ((0, K('AH2')), (1, K('AL2'))):
                        mm(ps[:, t * 8:t * 8 + 4], tri, AH2[:, j_, t, :], j_ == 0, j_ == 1, ['Cb', kk], [pk])
                    for j_, kk in ((0, K('AH2')), (1, K('AL2'))):
                        mm(ps[:, t * 8 + 4:t * 8 + 8], ones_b, AH2[:, j_, t, :], j_ == 0, j_ == 1, ['Cb', kk], [pk])
                cp('dve', CS, ps[:, 0:32].rearrange("p (t c) -> p t c", t=4), [pk], [K('CS2')])
                tt('dve', ARG, ipre, CS[:, :, 0:4], ALU.subtract, [K('GRl'), K('CS2')], [K('ARG2')])
                act(DD, ARG, AF.Exp, [K('ARG2'), 'lnks'], [K('DD')], bias=lnks[:, 0:1])
                tt('dve', ARG, ARG, CS[:, :, 4:8], ALU.add, [K('ARG2'), K('CS2')], [K('ARG2')])
                act(WW, ARG, AF.Exp, [K('ARG2'), 'lnks'], [K('WW')], bias=lnks[:, 0:1])
                act(EF, CS[:, :, 4:8], AF.Exp, [K('CS2')], [K('EF')])
                act(FLO, CS[:, :, 0:4], AF.Exp, [K('CS2')], [K('FLO')], scale=-1.0)
                yield
                chunks = range(4) if d_ == 0 else range(3, -1, -1)
                msk = maskF if d_ == 0 else maskB

                def unit_gen(h, t, c):
                    sl = h
                    ps1, pk1 = psum()
                    mm(ps1[:, 0:128], KTl[:, h, t * 128:(t + 1) * 128], QTl[:, h, t * 128:(t + 1) * 128], True, True,
                       [K('KTl'), K('QTl')], [pk1])
                    yield
                    stt(SST[:, sl, :], ps1[:, 0:128], DD[:, t, h:h + 1], msk, ALU.mult, ALU.mult, [pk1, K('DD'), 'C'], [K(('SST', sl))])
                    ts('dve', WVl[:, sl, 0:129], VAl[:, t, h * 129:(h + 1) * 129], WW[:, t, h:h + 1], None, ALU.mult, None,
                       [K('VAl'), K('WW')], [K(('WVl', sl))])
                    yield
                    ps2, pk2 = psum()
                    pk3 = pk2
                    mm(ps2[:, 0:129], QTl[:, h, t * 128:(t + 1) * 128], CBF[:, d_, h * 130:h * 130 + 129], True, False,
                       [K('QTl'), ('CBF', d_, h)], [pk2])
                    mm(ps2[:, 0:129], SST[:, sl, :], VAl[:, t, h * 129:(h + 1) * 129], False, True, [K(('SST', sl)), K('VAl')], [pk2])
                    mm(ps2[:, 256:385], KKl[:, t, h * 128:(h + 1) * 128], WVl[:, sl, 0:129], True, True, [K('KKl'), K(('WVl', sl))], [pk3])
                    yield
                    rk = K(('RC', sl))
                    sk_h = (skey, h)
                    stt(ST_[:, d_, h * 129:(h + 1) * 129], ST_[:, d_, h * 129:(h + 1) * 129], EF[:, t, h:h + 1], ps2[:, 256:385],
                        ALU.mult, ALU.add, [sk_h, K('EF'), pk3], [sk_h])
                    ts('dve', RCd[:, sl:sl + 1], ps2[:, 128:129], FLO[:, t, h:h + 1], None, ALU.max, None, [pk2, K('FLO')], [rk])
                    stt(RCd[:, sl:sl + 1], ps2[:, 128:129], -1.0, RCd[:, sl:sl + 1], ALU.mult, ALU.max, [pk2, rk], [rk])
                    yield
                    cp('pool', CBF[:, d_, h * 130:h * 130 + 129], ST_[:, d_, h * 129:(h + 1) * 129], [sk_h], [('CBF', d_, h)])
                    S.op('dve', lambda e, sl=sl: e.reciprocal(RCd[:, sl:sl + 1], RCd[:, sl:sl + 1]), [rk], [rk])
                    yield
                    hk = ('HD', d_, c, h)
                    ts('dve', HD[:, c, h * 128:(h + 1) * 128], ps2[:, 0:128], RCd[:, sl:sl + 1], None, ALU.mult, None, [pk2, rk], [hk])
                    yield

                for t in chunks:
                    c = i * 4 + t
                    units = [unit_gen(h, t, c) for h in range(4)]
                    while units:
                        for u in list(units):
                            try:
                                next(u)
                            except StopIteration:
                                units.remove(u)
                        yield

        gens = [dir_gen(0), dir_gen(1)]
        while gens:
            for g in list(gens):
                try:
                    next(g)
                except StopIteration:
                    gens.remove(g)
        for i in range(nl):
            dma(MOl[:, :, :].rearrange("p t c -> p (t c)"), dap(LMO[s], i * 4 * 512, [[nch * 512, 128], [1, 2048]]), [('LMO', s)], ['MOl'])
            for t in range(4):
                c = i * 4 + t
                tt('dve', HG[:, :], HF[:, c, :], HB[:, c, :], ALU.add, [('HD', d__, c, h__) for d__ in range(2) for h__ in range(4)], ['HG'])
                tt('dve', HG[:, :], HG[:, :], MOl[:, t, :], ALU.mult, ['HG', 'MOl'], ['HG'])
                for h in range(4):
                    act(HSQ[:, :], HG[:, h * 128:(h + 1) * 128], AF.Square, ['HG'], ['HSQ', 'RC2'], accum=RC[:, 4 + h:5 + h])
                act(RC[:, 4:8], RC[:, 4:8], AF.Sqrt, ['RC2', 'epst'], ['RC2'], bias=epst[:, 0:1], scale=1.0 / 128)
                S.op('dve', lambda e: e.reciprocal(RC[:, 4:8], RC[:, 4:8]), ['RC2'], ['RC2'])
                tt('dve', MOb[:, :].rearrange("p (h d) -> p h d", h=4), HG[:, :].rearrange("p (h d) -> p h d", h=4),
                   RC[:, 4:8].unsqueeze(2).to_broadcast([128, 4, 128]), ALU.mult, ['HG', 'RC2'], ['MOb'])
                ps, pk = psum()
                psb = ps.bitcast(BF)
                for h in range(4):
                    tr(psb[:, h * 128:(h + 1) * 128], MOb[:, h * 128:(h + 1) * 128], ['MOb', 'Cb'], [pk])
                cp('act', MTs[:, :, :], psb[:, 0:512].rearrange("p (h t) -> p h t", h=4), [pk], ['MTs'])
                dma(dap(MOT[s], c * 128, [[nq, 128], [128 * nq, 4], [1, 128]]), MTs[:, :, :], ['MTs'], [('MOT', s)])
        if DEBUG and s == 1:
            dma(dbg['mot'].ap(), MOT[s].ap(), [('MOT', s)], ['dbg8'])

    ARENA_END = 34 * 1024 + 3072
    O_B4W = ARENA_END - 16384
    b4a_jobs = []

    def b4a_weights():
        o3 = O_B4W
        WG_, o3 = carve(o3, [128, 8, 2048], BF)
        WPA_, o3 = carve(o3, [128, 4, 1024], BF)
        WPB_, o3 = carve(o3, [128, 4, 1024], BF)
        WO_, o3 = carve(o3, [128, 8, 1024], BF)
        assert o3 <= ARENA_END
        return WG_, WPA_, WPB_, WO_

    def prefetch_b4a_weights():
        WG_, WPA_, WPB_, WO_ = b4a_weights()
        load_weight(WG_, w_g_d, D, 2048, g1, 'g1', 'WG', engines=('dve',), jobs=b4a_jobs)
        load_weight(WPA_, w_pa_d, 512, D, None, None, 'WPA', engines=('dve',), jobs=b4a_jobs)
        load_weight(WPB_, w_pb_d, 512, D, gm, 'gm', 'WPB', engines=('dve',), jobs=b4a_jobs)
        load_weight(WO_, w_o_d, D, D, None, None, 'WO', engines=('dve',), jobs=b4a_jobs)
        b4a_jobs.append(lambda: dma(bG_t[:, :], dap(b_g_d, 0, [[1, 128], [128, 16]]), [], ['bG']))

    def attention(s):
        nq = NLOC[s] * 512
        nb = TK(s) // 128
        nqb = nq // 512
        o2 = O_PERSIST
        KTt, o2 = carve(o2, [97, TK(s)], BF)
        Vh, o2 = carve(o2, [128, nb, 65], BF)
        if o2 % 2:
            o2 += 1
        QTh, o2 = carve(o2, [97, nq], BF)
        PT, o2 = carve(o2, [128, 3, 1024], BF)
        RR, o2 = carve(o2, [128, 2, 512], F32)
        AO, o2 = carve(o2, [64, 2, 512], BF)
        assert o2 <= O_B4W, o2
        NSEG = 8
        segb = [(nb * g) // NSEG for g in range(NSEG + 1)]
        seg_of = {}
        for g in range(NSEG):
            for kb in range(segb[g], segb[g + 1]):
                seg_of[kb] = g
        for g in range(NSEG):
            c0, c1 = segb[g] * 128, segb[g + 1] * 128
            dma(KTt[64:97, c0:c1], dap(KR[s], c0, [[TK(s), 33], [1, c1 - c0]]), [('KR', s)], [('KTr', g)])
        for h in range(NH):
            for g in range(NSEG):
                c0, c1 = segb[g] * 128, segb[g + 1] * 128
                dma(KTt[0:64, c0:c1], dap(KN[s], h * 64 * TK(s) + c0, [[TK(s), 64], [1, c1 - c0]]), [('KN', s)], [('KTn', g)])
                b0, b1 = segb[g], segb[g + 1]
                dma(Vh[:, b0:b1, :], dap(VS[s], (h * 128 * nb + b0) * 65, [[nb * 65, 128], [65, b1 - b0], [1, 65]]),
                    [('VS', s)], [('Vh', g)])
            dma(QTh[:, :], dap(QT[s], h * 97 * nq, [[nq, 97], [1, nq]]), [('QT', s)], ['QTh'])
            if s == 0 and h == 1:
                prefetch_b4a_weights()
            for qb in range(nqb):
                pob = 6 + (qb % 2)
                po = PS[pob]; pok = ('ps', pob)
                groups = [(kb0, min(2, nb - kb0)) for kb0 in range(0, nb, 2)]

                def emit_S(gi):
                    kb0, n2 = groups[gi]
                    sb0 = (gi % 3) * 2
                    for j in range(n2):
                        kb = kb0 + j
                        g = seg_of[kb]
                        mm(PS[sb0 + j][:, 0:512], KTt[:, kb * 128:(kb + 1) * 128], QTh[:, qb * 512:(qb + 1) * 512], True, True,
                           [('KTr', g), ('KTn', g), 'QTh'], [('ps', sb0 + j)])
                    act(PT[:, gi % 3, 0:n2 * 512], PSA[:, sb0 * 512:(sb0 + n2) * 512], AF.Exp,
                        [('ps', sb0 + j) for j in range(n2)], [('PT', gi % 3)], scale=SM_SCALE)

                def emit_PV(gi):
                    kb0, n2 = groups[gi]
                    for j in range(n2):
                        kb = kb0 + j
                        g = seg_of[kb]
                        mm(po[0:65, 0:512], Vh[:, kb, :], PT[:, gi % 3, j * 512:(j + 1) * 512], kb == 0, kb == nb - 1,
                           [('Vh', g), ('PT', gi % 3)], [pok])

                emit_S(0)
                emit_S(1)
                for gi in range(len(groups)):
                    if gi + 2 < len(groups):
                        emit_S(gi + 2)
                    emit_PV(gi)
                asl = qb % 2
                S.op('dve', lambda e, po=po, asl=asl: e.reciprocal(RR[64:65, asl, :], po[64:65, 0:512]), [pok], [('RR', asl)])
                cp('dve', AO[:, asl, :], po[0:64, 0:512], [pok], [('AO', asl)])
                dma(dap(AOT[s], h * 64 * nq + qb * 512, [[nq, 64], [1, 512]]), AO[:, asl, :], [('AO', asl)], [('AOT', s)])
                dma(dap(RIV[s], h * nq + qb * 512, [[nq, 1], [1, 512]]), RR[64:65, asl, :], [('RR', asl)], [('RIV', s)])
                for _ in range(3):
                    if b4a_jobs:
                        b4a_jobs.pop(0)()
        if DEBUG and s == 1:
            dma(dbg['aot'].ap(), AOT[s].ap(), [('AOT', s)], ['dbg7'])


    def ffn_phases():
        o2 = O_PERSIST
        WG, WPA, WPB, WO = b4a_weights()
        XT4, o2 = carve(o2, [128, 4, 1024], F32)
        XN4, o2 = carve(o2, [128, 2, 1024], BF)
        XNT4, o2 = carve(o2, [128, 8, 512], BF)
        AOl, o2 = carve(o2, [128, 4, 512], BF)
        RBl, o2 = carve(o2, [128, 4, 512], F32)
        MOl4, o2 = carve(o2, [128, 4, 512], BF)
        MRG, o2 = carve(o2, [128, 8, 512], BF)
        H1t, o2 = carve(o2, [128, 1, 1024], F32)
        X2N, o2 = carve(o2, [128, 2, 1024], BF)
        X2Ts, o2 = carve(o2, [128, 1, 8, 128], BF)
        SGA, o2 = carve(o2, [128, 512], F32)
        SGB, o2 = carve(o2, [128, 512], F32)
        T1, o2 = carve(o2, [128, 512], F32)
        assert o2 <= O_B4W, o2
        assert not b4a_jobs
        bG = bG_t
        S4 = S4_t
        assert o2 <= 34 * 1024 + 3072, o2
        for s in range(NSEQ):
            nq = NLOC[s] * 512
            for i in range(NLOC[s]):
                for t in range(4):
                    dma(XT4[:, t, :], dap(xloc[s], (i * 512 + t * 128) * D, [[D, 128], [1, D]]), [], [('XT4', t)])
                    act(XN4[:, t % 2, :], XT4[:, t, :], AF.Square, [('XT4', t)], [('XN4', t % 2), ('S4', t)], accum=S4[:, t:t + 1])
                    act(S4[:, 4 + t:5 + t], S4[:, t:t + 1], AF.Sqrt, [('S4', t), 'epst'], [('S4b', t)], bias=epst[:, 0:1], scale=1.0 / D)
                    S.op('dve', lambda e, t=t: e.reciprocal(S4[:, 4 + t:5 + t], S4[:, 4 + t:5 + t]), [('S4b', t)], [('S4b', t)])
                    ts('dve', XN4[:, t % 2, :], XT4[:, t, :], S4[:, 4 + t:5 + t], None, ALU.mult, None,
                       [('XT4', t), ('S4b', t)], [('XN4', t % 2)])
                    ps, pk = psum()
                    psb = ps.bitcast(BF)
                    for kt in range(8):
                        tr(psb[:, kt * 128:(kt + 1) * 128], XN4[:, t % 2, kt * 128:(kt + 1) * 128], [('XN4', t % 2), 'Cb'], [pk])
                    cp('act', XNT4[:, :, t * 128:(t + 1) * 128], psb[:, 0:1024].rearrange("p (k t) -> p k t", k=8), [pk], ['XNT4'])
                for j in range(4):
                    dma(AOl[:, j, :], dap(AOT[s], j * 128 * nq + i * 512, [[nq, 128], [1, 512]]), [('AOT', s)], ['AOl'])
                    for hh in range(2):
                        dma(RBl[hh * 64:(hh + 1) * 64, j, :], dap(RIV[s], (2 * j + hh) * nq + i * 512, [[0, 64], [1, 512]]),
                            [('RIV', s)], ['RBl'])
                    dma(MOl4[:, j, :], dap(MOT[s], j * 128 * nq + i * 512, [[nq, 128], [1, 512]]), [('MOT', s)], ['MOl4'])
                tt('dve', AOl[:, :, :], AOl[:, :, :], RBl[:, :, :], ALU.mult, ['AOl', 'RBl'], ['AOl'])
                for c in range(8):
                    pa, pka = psum()
                    for k in range(4):
                        mm(pa[:, 0:512], WPA[:, k, c * 128:(c + 1) * 128], AOl[:, k, :], k == 0, k == 3, ['WPA', 'AOl'], [pka])
                    pb, pkb = psum()
                    for k in range(4):
                        mm(pb[:, 0:512], WPB[:, k, c * 128:(c + 1) * 128], MOl4[:, k, :], k == 0, k == 3, ['WPB', 'MOl4'], [pkb])
                    ga, pkga = psum()
                    for k in range(8):
                        mm(ga[:, 0:512], WG[:, k, c * 128:(c + 1) * 128], XNT4[:, k, :], k == 0, k == 7, ['WG', 'XNT4'], [pkga])
                    gb, pkgb = psum()
                    for k in range(8):
                        mm(gb[:, 0:512], WG[:, k, 1024 + c * 128:1024 + (c + 1) * 128], XNT4[:, k, :], k == 0, k == 7, ['WG', 'XNT4'], [pkgb])
                    act(SGA[:, :], ga[:, 0:512], AF.Sigmoid, [pkga, 'bG'], ['SGA'], bias=bG[:, c:c + 1])
                    act(SGB[:, :], gb[:, 0:512], AF.Sigmoid, [pkgb, 'bG'], ['SGB'], bias=bG[:, 8 + c:9 + c])
                    tt('dve', T1[:, :], pa[:, 0:512], SGA[:, :], ALU.mult, [pka, 'SGA'], ['T1'])
                    tt('dve', SGB[:, :], pb[:, 0:512], SGB[:, :], ALU.mult, [pkb, 'SGB'], ['SGB'])
                    tt('pool', MRG[:, c, :], T1[:, :], SGB[:, :], ALU.add, ['T1', 'SGB'], ['MRG'])
                for t in range(4):
                    hs = t % 2
                    for half in range(2):
                        ps, pk = psum()
                        for k in range(8):
                            mm(ps[:, 0:512], MRG[:, k, t * 128:(t + 1) * 128], WO[:, k, half * 512:(half + 1) * 512], k == 0, k == 7,
                               ['MRG', 'WO'], [pk])
                        tt('dve', H1t[:, 0, half * 512:(half + 1) * 512], ps[:, 0:512], XT4[:, t, half * 512:(half + 1) * 512], ALU.add,
                           [pk, ('XT4', t)], ['H1t'])
                    dma(dap(H1[s], (i * 512 + t * 128) * D, [[D, 128], [1, D]]), H1t[:, 0, :], ['H1t'], [('H1', s)])
                    act(X2N[:, hs, :], H1t[:, 0, :], AF.Square, ['H1t'], [('X2N', hs), ('S4c', hs)], accum=S4[:, 8 + hs:9 + hs])
                    act(S4[:, 10 + hs:11 + hs], S4[:, 8 + hs:9 + hs], AF.Sqrt, [('S4c', hs), 'epst'], [('S4d', hs)], bias=epst[:, 0:1], scale=1.0 / D)
                    S.op('dve', lambda e, hs=hs: e.reciprocal(S4[:, 10 + hs:11 + hs], S4[:, 10 + hs:11 + hs]), [('S4d', hs)], [('S4d', hs)])
                    ts('dve', X2N[:, hs, :], H1t[:, 0, :], S4[:, 10 + hs:11 + hs], None, ALU.mult, None, ['H1t', ('S4d', hs)], [('X2N', hs)])
                    ps, pk = psum()
                    psb = ps.bitcast(BF)
                    for kt in range(8):
                        tr(psb[:, kt * 128:(kt + 1) * 128], X2N[:, hs, kt * 128:(kt + 1) * 128], [('X2N', hs), 'Cb'], [pk])
                    cp('act', X2Ts[:, 0, :, :], psb[:, 0:1024].rearrange("p (k t) -> p k t", k=8), [pk], ['X2Ts'])
                    dma(dap(X2T[s], i * 512 + t * 128, [[nq, 128], [128 * nq, 8], [1, 128]]), X2Ts[:, 0, :, :], ['X2Ts'], [('X2T', s)])
        S.barrier()
        o2 = O_PERSIST
        WGU, o2 = carve(o2, [128, 8, 2 * D_FF], BF)
        X2l, o2 = carve(o2, [128, 2, 8, 512], BF)
        SIL, o2 = carve(o2, [128, 2, 512], F32)
        HTc, o2 = carve(o2, [128, 2, 512], BF)
        assert o2 <= 34 * 1024 + 3072, o2
        load_weight(WGU, w_gu_d, D, 2 * D_FF, g2, 'g2', 'WGU')
        it = 0
        for s in range(NSEQ):
            nq = NLOC[s] * 512
            for i in range(NLOC[s]):
                xsl = it % 2
                it += 1
                dma(X2l[:, xsl, :, :], dap(X2T[s], i * 512, [[nq, 128], [128 * nq, 8], [1, 512]]), [('X2T', s)], [('X2l', xsl)])
                for c in range(D_FF // 128):
                    sl = c % 2
                    pg, pkg = psum()
                    for k in range(8):
                        mm(pg[:, 0:512], WGU[:, k, c * 128:(c + 1) * 128], X2l[:, xsl, k, :], k == 0, k == 7, ['WGU', ('X2l', xsl)], [pkg])
                    pu, pku = psum()
                    for k in range(8):
                        mm(pu[:, 0:512], WGU[:, k, D_FF + c * 128:D_FF + (c + 1) * 128], X2l[:, xsl, k, :], k == 0, k == 7,
                           ['WGU', ('X2l', xsl)], [pku])
                    act(SIL[:, sl, :], pg[:, 0:512], AF.Silu, [pkg], [('SIL', sl)])
                    tt('dve', HTc[:, sl, :], pu[:, 0:512], SIL[:, sl, :], ALU.mult, [pku, ('SIL', sl)], [('HTc', sl)])
                    dma(dap(HT[s], c * 128 * nq + i * 512, [[nq, 128], [1, 512]]), HTc[:, sl, :], [('HTc', sl)], [('HT', s)])
        S.barrier()
        o2 = O_PERSIST
        WDN, o2 = carve(o2, [128, 22, 1024], BF)
        HTl, o2 = carve(o2, [128, 22, 512], BF)
        H1l, o2 = carve(o2, [128, 2, 1024], F32)
        YT, o2 = carve(o2, [128, 2, 1024], F32)
        YSQ, o2 = carve(o2, [128, 1024], F32)
        GF, o2 = carve(o2, [128, 1024], F32)
        S5, o2 = carve(o2, [128, 8], F32)
        assert o2 <= 34 * 1024 + 3072, o2
        load_weight(WDN, w_dn_d, D_FF, D, None, None, 'WDN')
        dma(GF[:, :], dap(gfin_d, 0, [[0, 128], [1, D]]), [], ['GF'])
        for s in range(NSEQ):
            nq = NLOC[s] * 512
            for i in range(NLOC[s]):
                dma(HTl[:, :, :], dap(HT[s], i * 512, [[nq, 128], [128 * nq, 22], [1, 512]]), [('HT', s)], ['HTl'])
                for t in range(4):
                    hs = t % 2
                    dma(H1l[:, hs, :], dap(H1[s], (i * 512 + t * 128) * D, [[D, 128], [1, D]]), [('H1', s)], [('H1l', hs)])
                    for half in range(2):
                        ps, pk = psum()
                        for k in range(22):
                            mm(ps[:, 0:512], HTl[:, k, t * 128:(t + 1) * 128], WDN[:, k, half * 512:(half + 1) * 512], k == 0, k == 21,
                               ['HTl', 'WDN'], [pk])
                        tt('dve', YT[:, hs, half * 512:(half + 1) * 512], ps[:, 0:512], H1l[:, hs, half * 512:(half + 1) * 512], ALU.add,
                           [pk, ('H1l', hs)], [('YT', hs)])
                    act(YSQ[:, :], YT[:, hs, :], AF.Square, [('YT', hs)], ['YSQ', ('S5', hs)], accum=S5[:, hs:hs + 1])
                    act(S5[:, 2 + hs:3 + hs], S5[:, hs:hs + 1], AF.Sqrt, [('S5', hs), 'epst'], [('S5b', hs)], bias=epst[:, 0:1], scale=1.0 / D)
                    S.op('dve', lambda e, hs=hs: e.reciprocal(S5[:, 2 + hs:3 + hs], S5[:, 2 + hs:3 + hs]), [('S5b', hs)], [('S5b', hs)])
                    stt(YT[:, hs, :], YT[:, hs, :], S5[:, 2 + hs:3 + hs], GF[:, :], ALU.mult, ALU.mult, [('YT', hs), ('S5b', hs), 'GF'], [('YT', hs)])
                    dma(dap(y[s], (i * 512 + t * 128) * D, [[D, 128], [1, D]]), YT[:, hs, :], [('YT', hs)], [('y', s)])

    kscale_ln = float(np.log(K_SCALE))

    for s in range(NSEQ):
        if stage < 1:
            break
        nsup = NSUP[s]
        nl = NLOC[s]
        nsteps = 1 + nsup
        dma(MSK[:, 0:nsteps * 4], mskd[s].ap(), [], ['MSK'])
        S.op('dve', lambda e: e.memset(ST_[:], 0.0), [], ['Cf', 'Bb'])
        S.op('dve', lambda e: e.memset(SM[:, 0:8], 0.0), [], ['Grun', 'Hrun'])
        S.op('pool', lambda e: e.memset(KRS[32:33, :], 1.0), [], ['KRS1'])

        def mcol(step, j):
            return MSK[:, step * 4 + j:step * 4 + j + 1]

        def load_x(step, what='both'):
            is_meta = (step == 0)
            ntt = 1 if is_meta else 4
            ntok = ntt * 128
            row0 = 0 if is_meta else 128 + (step - 1) * 512
            col0 = 0 if is_meta else (1 + 4 * (step - 1)) * 128
            xs = step % 2
            if what in ('x', 'both'):
                for t in range(ntt):
                    dma(XT[:, t, :], dap(xin[s], (row0 + t * 128) * D, [[D, 128], [1, D]]), [], [('XT', t)])
            if what in ('cs', 'both'):
                dma(CSK[0:32, xs, 0:ntok], dap(cosd[s], col0, [[TK(s), 32], [1, ntok]]), [], [('CSKc', xs)])
                dma(CSK[32:64, xs, 0:ntok], dap(sind[s], col0, [[TK(s), 32], [1, ntok]]), [], [('CSKs', xs)])

        def N_tile(step, t):
            xs = step % 2
            act(XN[:, t % 2, :], XT[:, t, :], AF.Square, [('XT', t)], [('XN', t % 2), ('SSQ', t)],
                accum=SSQ[:, t:t + 1])
            act(RST[:, t:t + 1], SSQ[:, t:t + 1], AF.Sqrt, [('SSQ', t), 'epst'], [('RST', t)],
                bias=epst[:, 0:1], scale=1.0 / D)
            S.op('dve', lambda e, t=t: e.reciprocal(RST[:, t:t + 1], RST[:, t:t + 1]),
                 [('RST', t)], [('RST', t)])
            if t % 2 == 0:
                ts('dve', XN[:, t % 2, :], XT[:, t, :], RST[:, t:t + 1], None,
                   ALU.mult, None, [('XT', t), ('RST', t)], [('XN', t % 2)])
            else:
                act(XN[:, t % 2, :], XT[:, t, :], AF.Copy, [('XT', t), ('RST', t)], [('XN', t % 2)], scale=RST[:, t:t + 1])
            ps, pk = psum()
            psb = ps.bitcast(BF)
            for kt in range(8):
                tr(psb[:, kt * 128:(kt + 1) * 128], XN[:, t % 2, kt * 128:(kt + 1) * 128], [('XN', t % 2), 'Cb'], [pk])
            cp('act', XNT[:, xs, :, t * 128:(t + 1) * 128],
               psb[:, 0:1024].rearrange("p (k t) -> p k t", k=8), [pk], [('XNT', xs)])

        pending = []

        def once(f):
            f()
            if False:
                yield

        def pump(n=1):
            for _ in range(n):
                if not pending:
                    return
                g = pending.pop(0)
                try:
                    next(g)
                    pending.append(g)
                except StopIteration:
                    pass

        def tail_gen(step):
            is_meta = (step == 0)
            ntt = 1 if is_meta else 4
            ntok = ntt * 128
            kb0 = 0 if is_meta else 1 + 4 * (step - 1)
            col0 = kb0 * 128
            xs = step % 2
            ps2, pk2 = psum()
            mm(ps2[:, 0:ntok], ones_b, SQ[:, xs, 0:ntok], True, True, [('SQ', xs), 'Cb'], [pk2])
            act(LNT[:, 0:ntok], ps2[:, 0:ntok], AF.Ln, [pk2, 'epst'], ['LNT'], bias=epst[:, 0:1], scale=1.0 / 128)
            act(RBC[:, 0:ntok], LNT[:, 0:ntok], AF.Exp, ['LNT'], ['RBC'], scale=-0.5)
            yield
            ps3, pk3 = psum()
            for t in range(ntt):
                mm(ps3[:, t:t + 1], SQ[:, xs, t * 128:(t + 1) * 128], ones_b[:, 0:1], True, True, [('SQ', xs), 'Cb'], [pk3])
            act(SSQ[:, 4:4 + ntt], ps3[:, 0:ntt], AF.Sqrt, [pk3, 'epst'], ['RSV'], bias=epst[:, 0:1], scale=1.0 / 128)
            S.op('dve', lambda e: e.reciprocal(RST[:, 4:4 + ntt], SSQ[:, 4:4 + ntt]), ['RSV'], ['RSV2'])
            yield
            for hp in range(4):
                ps, pk = psum()
                mm(ps[:, 0:ntok], WUKV[:, hp * 128:(hp + 1) * 128], CKV[:, xs, 0:ntok], True, True, ['WUKV', ('CKV', xs)], [pk])
                tt('dve', KNS[:, hp, 0:ntok], ps[:, 0:ntok], RBC[:, 0:ntok], ALU.mult, [pk, 'RBC'], [('KNS', hp)])
                dma(dap(KN[s], hp * 128 * TK(s) + col0, [[TK(s), 128], [1, ntok]]), KNS[:, hp, 0:ntok],
                    [('KNS', hp)], [('KN', s)])
                yield
            for t in range(ntt):
                ps, pk = psum()
                mm(ps[:, 0:512], CKV[:, xs, t * 128:(t + 1) * 128], WUKV[:, 512:1024], True, True, ['WUKV', ('CKV', xs)], [pk])
                if is_meta:
                    ts('dve', RST[:, 4:5], RST[:, 4:5], vmeta[:, 0:1], None, ALU.mult, None, ['RSV2', 'vmeta'], ['RSV2'])
                ts('dve', VST[:, t, :, 0:64], ps[:, 0:512].rearrange("p (h d) -> p h d", h=8), RST[:, 4 + t:5 + t],
                   None, ALU.mult, None, [pk, 'RSV2'], [('VST', t)])
                if is_meta:
                    cp('pool', VST[:, t, :, 64:65], vmeta[:, 0:1].unsqueeze(1).to_broadcast([128, 8, 1]), ['vmeta'], [('VST', t)])
                else:
                    S.op('pool', lambda e, t=t: e.memset(VST[:, t, :, 64:65], 1.0), [], [('VST', t)])
                yield
            for h in range(NH):
                nb = TK(s) // 128
                dma(dap(VS[s], (h * 128 * nb + kb0) * 65, [[nb * 65, 128], [65, ntt], [1, 65]]),
                    VST[:, 0:ntt, h, :], [('VST', t) for t in range(ntt)], [('VS', s)])
                if h % 2 == 1:
                    yield


        def A_step(step):
            is_meta = (step == 0)
            ntt = 1 if is_meta else 4
            ntok = ntt * 128
            row0 = 0 if is_meta else 128 + (step - 1) * 512
            kb0 = 0 if is_meta else 1 + 4 * (step - 1)
            col0 = kb0 * 128
            local = (1 <= step <= nl)
            lcol0 = (step - 1) * 512
            xs = step % 2
            rs = step % 3
            xk = ('XNT', xs)
            if step >= 1:
                pending.append(tail_gen(step - 1))
            if step >= 2:
                pending.append(M_step(step - 2))

            def proj_fm(c0, m, n0=0, nn=None):
                nn_ = ntok if nn is None else nn
                ps, pk = psum()
                for kt in range(8):
                    mm(ps[0:m, 0:nn_], WA[:, kt, c0:c0 + m], XNT[:, xs, kt, n0:n0 + nn_], kt == 0, kt == 7,
                       ['WA', xk], [pk])
                return ps, pk

            need_q = is_meta or local or step == nl + 1
            for h in range(8 if need_q else 4):
                if h < 4:
                    ps, pk = proj_fm(192 + h * 128, 128)
                    bias_ap = bAm[:, h:h + 1]; bk = 'bAm'
                else:
                    ps, pk = psum()
                    for kt in range(8):
                        mm(ps[:, 0:ntok], WL[:, kt, 256 + (h - 4) * 128:256 + (h - 3) * 128], XNT[:, xs, kt, 0:ntok],
                           kt == 0, kt == 7, ['WL', xk], [pk])
                    bias_ap = bLq[:, 2 + h - 4:3 + h - 4]; bk = 'bLq'
                act(MKr[:, rs, h, 1:1 + ntok], ps[:, 0:ntok], AF.Identity, [pk, bk], [('MK', rs)], bias=bias_ap)
                pump(2)
            nh_ = 8 if need_q else 4
            if is_meta:
                cp('dve', SM[:, 16:24], MKr[:, rs, :, 1], [('MK', rs)], ['pre'])
                cp('dve', SM[:, 8:16], MKr[:, rs, :, 1 + 126], [('MK', rs)], ['meta15'])
                S.op('pool', lambda e: e.memset(MKr[:, rs, :, 0:1 + META_LO], 0.0), ['pre'], [('MK', rs)])
            else:
                if step == 1:
                    cp('dve', MKr[:, rs, :, 0], SM[:, 16:24], ['pre'], [('MK', rs)])
                    cp('dve', SM[:, 24:32], MKr[:, rs, :, 1], [('MK', rs)], ['first'])
                else:
                    po = (step - 1) % 3
                    ts('dve', MW[:, 0:1], mcol(step, 2), -1.0, 1.0, ALU.mult, ALU.add, ['MSK'], ['MW0'])
                    ts('dve', GT[:, 0, 0:8], MKr[:, po, :, 512], mcol(step, 2), None, ALU.mult, None,
                       [('MK', po), 'MSK'], ['GT0'])
                    stt(MKr[:, rs, 0:nh_, 0], SM[:, 8:8 + nh_], MW[:, 0:1], GT[:, 0, 0:nh_], ALU.mult, ALU.add,
                        ['meta15', 'MW0', 'GT0'], [('MK', rs)])
            ps, pk = proj_fm(0, 128)
            act(CKV[:, xs, 0:ntok], ps[:, 0:ntok], AF.Identity, [pk, 'bA'], [('CKV', xs)], bias=bA[:, 0:1])
            act(SQ[:, xs, 0:ntok], ps[:, 0:ntok], AF.Square, [pk, 'bA'], [('SQ', xs)], bias=bA[:, 0:1])
            pump(3)
            ps, pk = proj_fm(128, 64)
            stt(KT1[:, 0:ntok], ps[0:32, 0:ntok], bA[0:32, 1:2], CSK[0:32, xs, 0:ntok], ALU.add, ALU.mult,
                [pk, 'bA', ('CSKc', xs)], ['KT1'])
            stt(KT2[:, 0:ntok], ps[32:64, 0:ntok], bA[32:64, 1:2], CSK[32:64, xs, 0:ntok], ALU.add, ALU.mult,
                [pk, 'bA', ('CSKs', xs)], ['KT2'])
            tt('dve', KRS[0:32, 0:ntok], KT1[:, 0:ntok], KT2[:, 0:ntok], ALU.add, ['KT1', 'KT2'], ['KRS'])
            dma(dap(KR[s], col0, [[TK(s), 33], [1, ntok]]), KRS[:, 0:ntok], ['KRS', 'KRS1'], [('KR', s)])
            if step + 2 < nsteps:
                load_x(step + 2, 'cs')
            pump(4)
            for t in range(ntt):
                ps, pk = psum()
                for kt in range(8):
                    mm(ps[:, 0:512], XNT[:, xs, kt, t * 128:(t + 1) * 128], WA[:, kt, 704:1216], kt == 0, kt == 7,
                       ['WA', xk], [pk])
                tt('dve', VR[:, rs, t, :], ps[:, 0:512], bbc[:, 0:512], ALU.add, [pk, 'bbc'], [('VR', rs)])
                pump(5)
            ps, pk = psum()
            for t in range(ntt):
                for kt in range(8):
                    mm(ps[:, t * 16:(t + 1) * 16], XNT[:, xs, kt, t * 128:(t + 1) * 128], WA[:, kt, 1216:1232],
                       kt == 0, kt == 7, ['WA', xk], [pk])
            tt('dve', GR[:, rs, 0:ntt, :], ps[:, 0:ntt * 16].rearrange("p (t g) -> p t g", t=ntt),
               bbc[:, 512:528].unsqueeze(1).to_broadcast([128, ntt, 16]), ALU.add, [pk, 'bbc'], [('GR', rs)])
            pump(5)
            while pending:
                pump()
            if local:
                dma(dap(LG[s], (step - 1) * 4 * 16, [[NLOC[s] * 4 * 16, 128], [1, 64]]),
                    GR[:, rs, :, :].rearrange("p t g -> p (t g)"), [('GR', rs)], [('LG', s)])
                A_local(step, xs, rs, xk, lcol0, col0)

        def A_local(step, xs, rs, xk, lcol0, col0):
            nq = NLOC[s] * 512
            ps2, pk2 = psum()
            for j in range(2):
                ps, pk = psum()
                for kt in range(8):
                    mm(ps[:, 0:512], WL[:, kt, j * 128:(j + 1) * 128], XNT[:, xs, kt, :], kt == 0, kt == 7, ['WL', xk], [pk])
                act(CQT[:, j, :], ps[:, 0:512], AF.Identity, [pk, 'bLq'], [('CQT', j)], bias=bLq[:, j:j + 1])
                act(SQ[:, 1 - xs, :], ps[:, 0:512], AF.Square, [pk, 'bLq'], [('SQ', 1 - xs)], bias=bLq[:, j:j + 1])
                mm(ps2[:, 0:512], ones_b, SQ[:, 1 - xs, :], j == 0, j == 1, [('SQ', 1 - xs), 'Cb'], [pk2])
            act(LNT[:, :], ps2[:, 0:512], AF.Ln, [pk2, 'epst'], ['LNT'], bias=epst[:, 0:1], scale=1.0 / 256)
            act(RBC[:, :], LNT[:, :], AF.Exp, ['LNT'], ['RBC'], scale=-0.5)
            dma(CSQ[64:96, 0, :], dap(cosd[s], col0, [[TK(s), 32], [1, 512]]), [], ['CSQc'])
            dma(CSQ[64:96, 1, :], dap(sind[s], col0, [[TK(s), 32], [1, 512]]), [], ['CSQs'])
            tt('pool', CSQ[64:96, 0, :], CSQ[64:96, 0, :], RBC[64:96, :], ALU.mult, ['CSQc', 'RBC'], ['CSQc'])
            tt('pool', CSQ[64:96, 1, :], CSQ[64:96, 1, :], RBC[64:96, :], ALU.mult, ['CSQs', 'RBC'], ['CSQs'])
            S.op('pool', lambda e: e.memset(QTS[96:97, :, :], 0.0), [], ['QTS96'])
            for h in range(NH):
                ps, pk = psum()
                for j in range(2):
                    mm(ps[:, 0:512], WUQ[:, j, h * 128:(h + 1) * 128], CQT[:, j, :], j == 0, j == 1,
                       ['WUQ', ('CQT', 0), ('CQT', 1)], [pk])
                tt('dve', QTS[0:64, h % 2, :], ps[0:64, 0:512], RBC[0:64, :], ALU.mult, [pk, 'RBC'], [('QTS', h % 2)])
                tt('dve', KT1[:, :], ps[64:96, 0:512], CSQ[64:96, 0, :], ALU.mult, [pk, 'CSQc'], ['KT1'])
                tt('dve', KT2[:, :], ps[96:128, 0:512], CSQ[64:96, 1, :], ALU.mult, [pk, 'CSQs'], ['KT2'])
                tt('pool', QTS[64:96, h % 2, :], KT1[:, :], KT2[:, :], ALU.add, ['KT1', 'KT2'], [('QTS', h % 2)])
                dma(dap(QT[s], h * 97 * nq + lcol0, [[nq, 97], [1, 512]]), QTS[:, h % 2, :], [('QTS', h % 2), 'QTS96'], [('QT', s)])
            for t in range(4):
                ps, pk = psum()
                for kt in range(8):
                    mm(ps[:, 0:512], XNT[:, xs, kt, t * 128:(t + 1) * 128], WL[:, kt, 768:1280], kt == 0, kt == 7,
                       ['WL', xk], [pk])
                tt('dve', CV[:, 0, :], ps[:, 0:512], bbc[:, 528:1040], ALU.add, [pk, 'bbc'], [('CV', 0)])
                act(MOS[:, t % 2, :], CV[:, 0, :], AF.Sigmoid, [('CV', 0)], [('MOS', t % 2)])
                dma(dap(LMO[s], ((step - 1) * 4 + t) * 512, [[NLOC[s] * 4 * 512, 128], [1, 512]]),
                    MOS[:, t % 2, :], [('MOS', t % 2)], [('LMO', s)])

        def M_step(step, deferred_meta=False):
            if False:
                yield
            is_meta = (step == 0)
            ntt = 1 if is_meta else 4
            ntok = ntt * 128
            xs = step % 3
            local = (1 <= step <= nl)
            nh_ = 8 if (is_meta or local) else 4
            lcol0 = (step - 1) * 512
            if not is_meta:
                if step == nsup:
                    src = SM[:, 24:24 + nh_]; sk = 'first'
                else:
                    src = MKr[:, (step + 1) % 3, 0:nh_, 1]; sk = ('MK', (step + 1) % 3)
                ts('dve', MKr[:, xs, 0:nh_, 513], src, mcol(step, 3), None, ALU.mult, None, [sk, 'MSK'], [('MK', xs)])
            if not is_meta or not deferred_meta:
                for h in range(nh_):
                    x0 = MKr[:, xs, h, 0:ntok]; x1 = MKr[:, xs, h, 1:1 + ntok]; x2 = MKr[:, xs, h, 2:2 + ntok]
                    acc = CV[:, h % 2, 0:ntok]
                    ck = ('CV', h % 2)
                    ts('dve', acc, x1, cw[:, h * 3 + 1:h * 3 + 2], cb[:, h:h + 1], ALU.mult, ALU.add,
                       [('MK', xs), 'cw', 'cb'], [ck])
                    stt(acc, x0, cw[:, h * 3:h * 3 + 1], acc, ALU.mult, ALU.add, [('MK', xs), 'cw', ck], [ck])
                    stt(acc, x2, cw[:, h * 3 + 2:h * 3 + 3], acc, ALU.mult, ALU.add, [('MK', xs), 'cw', ck], [ck])
                    act(KTb[:, h, 0:ntok], acc, AF.Silu, [ck], [('KTb', h)])
                    yield
                for t in range(ntt):
                    ps, pk = psum()
                    psb = ps.bitcast(BF)
                    for h in range(4):
                        tr(psb[:, h * 128:(h + 1) * 128], KTb[:, h, t * 128:(t + 1) * 128], [('KTb', h), 'Cb'], [pk])
                    kdst = KTK[:, t, :] if not is_meta else KTKm[:, :]
                    cp('act', kdst, psb[:, 0:512], [pk], [('KTK', t) if not is_meta else 'KTKm'])
                    yield
            if local:
                nq = NLOC[s] * 512
                for h in range(4):
                    dma(dap(LK[s], h * 128 * nq + lcol0, [[nq, 128], [1, 512]]), KTb[:, h, :], [('KTb', h)], [('LK', s)])
                    dma(dap(LQ[s], h * 128 * nq + lcol0, [[nq, 128], [1, 512]]), KTb[:, 4 + h, :], [('KTb', 4 + h)], [('LQ', s)])
                dma(dap(LKT[s], (step - 1) * 4 * 512, [[NLOC[s] * 4 * 512, 128], [1, 2048]]),
                    KTK[:, :, :].rearrange("p t c -> p (t c)"), [('KTK', t) for t in range(4)], [('LKT', s)])
                for t in range(4):
                    cp('pool', VAL[:, t, :].rearrange("p (h d) -> p h d", h=4)[:, :, 0:128],
                       VR[:, xs, t, :].rearrange("p (h d) -> p h d", h=4), [('VR', xs)], [('VAL', t)])
                    S.op('pool', lambda e, t=t: e.memset(VAL[:, t, :].rearrange("p (h d) -> p h d", h=4)[:, :, 128:129], 1.0),
                         [], [('VAL', t)])
                dma(dap(LV[s], (step - 1) * 4 * 516, [[NLOC[s] * 4 * 516, 128], [1, 4 * 516]]),
                    VAL[:, :, :].rearrange("p t c -> p (t c)"), [('VAL', t) for t in range(4)], [('LV', s)])
                return
            if is_meta and not deferred_meta:
                cp('pool', VRm[:, :], VR[:, xs, 0, :], [('VR', xs)], ['VRm'])
                cp('pool', GRm[:, :], GR[:, xs, 0, :], [('GR', xs)], ['GRm'])
                return
            if is_meta:
                G = GRm[:, :].unsqueeze(1)
                gk = 'GRm'
            else:
                G = GR[:, xs, :, :]
                gk = ('GR', xs)
            nt = ntt
            G4 = G.rearrange("p t (d j h) -> p t d j h", d=2, j=2)
            fpre = G4[:, :, :, 1, :]
            ipre = G4[:, :, :, 0, :]
            E1 = GT[:, 0, :].rearrange("p (a b) -> p a b", a=2)[:, :, :]
            A_ = GT[:, 1:1 + nt, 0:8].rearrange("p t (d h) -> p t d h", d=2)
            T_ = GT[:, 5:5 + nt, 0:8].rearrange("p t (d h) -> p t d h", d=2)
            act(T_, fpre, AF.Exp, [gk], ['T_'], scale=-1.0)
            act(T_, T_, AF.Ln, ['T_', 'onet'], ['T_'], bias=onet[:, 0:1])
            if is_meta:
                ts('dve', MW[:, 2:3], vmeta[:, 0:1], -1.0, None, ALU.mult, None, ['vmeta'], ['MW2'])
                ts('dve', MW[:, 3:4], vmeta[:, 0:1], 0.0, None, ALU.mult, None, ['vmeta'], ['MW3'])
                wm0 = vmeta[:, 0:1]
            else:
                ts('dve', MW[:, 2:3], mcol(step, 0), -1.0, None, ALU.mult, None, ['MSK'], ['MW2'])
                ts('dve', MW[:, 3:4], mcol(step, 1), -1.0, None, ALU.mult, None, ['MSK'], ['MW3'])
            for d_ in range(2):
                ts('dve', A_[:, :, d_, :], T_[:, :, d_, :], MW[:, 2 + d_:3 + d_], None, ALU.mult, None,
                   ['T_', 'MW2', 'MW3'], ['A_'])
            AH = AHL[:, 0, 0:nt, :]; AL = AHL[:, 1, 0:nt, :]
            cp('dve', AH, GT[:, 1:1 + nt, 0:8], ['A_'], ['AH'])
            tt('dve', AL, GT[:, 1:1 + nt, 0:8], AH, ALU.subtract, ['A_', 'AH'], ['AL'])
            ps, pk = psum()
            for t in range(nt):
                for (pp, kk, first) in ((AH, 'AH', True), (AL, 'AL', False)):
                    mm(ps[:, t * 16:t * 16 + 4], triU_b, pp[:, t, 0:4], first, not first, ['Cb', kk], [pk])
                for (pp, kk, first) in ((AH, 'AH', True), (AL, 'AL', False)):
                    mm(ps[:, t * 16 + 4:t * 16 + 8], triL_b, pp[:, t, 4:8], first, not first, ['Cb', kk], [pk])
                for (pp, kk, first) in ((AH, 'AH', True), (AL, 'AL', False)):
                    mm(ps[:, t * 16 + 8:t * 16 + 16], ones_b, pp[:, t, 0:8], first, not first, ['Cb', kk], [pk])
            CS = GT[:, 9:9 + nt, :]
            cp('dve', CS, ps[:, 0:nt * 16].rearrange("p (t c) -> p t c", t=nt), [pk], ['CS'])
            yield
            SFX = GT[:, 13, :].rearrange("p (t h) -> p t h", t=4)
            PFX = GT[:, 14, :].rearrange("p (t h) -> p t h", t=4)
            S.op('dve', lambda e: e.memset(GT[:, 13:15, :], 0.0), [], ['SFX', 'PFX'])
            if is_meta:
                cp('dve', SFX[:, 0, :], SM[:, 4:8], ['Hrun'], ['SFX'])
            else:
                for t in range(nt - 2, -1, -1):
                    tt('dve', SFX[:, t, :], SFX[:, t + 1, :], CS[:, t + 1, 8:12], ALU.add, ['SFX', 'CS'], ['SFX'])
                cp('dve', PFX[:, 0, :], SM[:, 0:4], ['Grun'], ['PFX'])
                for t in range(1, nt):
                    tt('dve', PFX[:, t, :], PFX[:, t - 1, :], CS[:, t - 1, 12:16], ALU.add, ['PFX', 'CS'], ['PFX'])
                tt('dve', SM[:, 0:4], PFX[:, nt - 1, :], CS[:, nt - 1, 12:16], ALU.add, ['PFX', 'CS'], ['Grun'])
                tt('dve', GT[:, 15, 0:4], SFX[:, 0, :], CS[:, 0, 8:12], ALU.add, ['SFX', 'CS'], ['FLS'])
                tt('dve', SM[:, 4:8], SM[:, 4:8], GT[:, 15, 0:4], ALU.add, ['Hrun', 'FLS'], ['Hrun'])
            ARG = GT[:, 5:5 + nt, 8:16].rearrange("p t (d h) -> p t d h", d=2)
            tt('dve', ARG[:, :, 0, :], ipre[:, :, 0, :], CS[:, :, 0:4], ALU.subtract, [gk, 'CS'], ['ARG'])
            tt('dve', ARG[:, :, 1, :], ipre[:, :, 1, :], CS[:, :, 4:8], ALU.subtract, [gk, 'CS'], ['ARG'])
            tt('dve', ARG[:, :, 0, :], ARG[:, :, 0, :], CS[:, :, 8:12], ALU.add, ['ARG', 'CS'], ['ARG'])
            tt('dve', ARG[:, :, 1, :], ARG[:, :, 1, :], CS[:, :, 12:16], ALU.add, ['ARG', 'CS'], ['ARG'])
            tt('dve', ARG[:, :, 0, :], ARG[:, :, 0, :], SFX[:, 0:nt, :], ALU.add, ['ARG', 'SFX'], ['ARG'])
            tt('dve', ARG[:, :, 1, :], ARG[:, :, 1, :], PFX[:, 0:nt, :], ALU.add, ['ARG', 'PFX'], ['ARG'])
            Wt = GT[:, 1:1 + nt, 8:16].rearrange("p t (d h) -> p t d h", d=2)
            yield
            act(Wt, ARG, AF.Exp, ['ARG', 'lnks'], ['Wt'], bias=lnks[:, 0:1])
            if is_meta:
                ts('dve', Wt[:, :, 0, :], Wt[:, :, 0, :], vmeta[:, 0:1], None, ALU.mult, None, ['Wt', 'vmeta'], ['Wt'])
            else:
                for d_ in range(2):
                    ts('dve', Wt[:, :, d_, :], Wt[:, :, d_, :], mcol(step, d_), None, ALU.mult, None, ['Wt', 'MSK'], ['Wt'])
            if not is_meta:
                act(GT[:, 15, 4:8], GT[:, 15, 0:4], AF.Exp, ['FLS'], ['EFL'])
            ndir = 1 if is_meta else 2
            for t in range(nt):
                for d_ in range(ndir):
                    vsrc = (VRm[:, :] if is_meta else VR[:, xs, t, :]).rearrange("p (h d) -> p h d", h=4)
                    vk = 'VRm' if is_meta else ('VR', xs)
                    wv = WV[:, t, d_, :].rearrange("p (h d) -> p h d", h=4)
                    tt('dve', wv[:, :, 0:128], vsrc,
                       Wt[:, t, d_, :].unsqueeze(2).to_broadcast([128, 4, 128]), ALU.mult, [vk, 'Wt'], [('WV', t, d_)])
                    cp('pool', wv[:, :, 128:129], Wt[:, t, d_, :].unsqueeze(2), ['Wt'], [('WV', t, d_)])
                    yield
            for d_ in range(ndir):
                for h in range(4):
                    ps, pk = psum()
                    for t in range(nt):
                        ksrc = KTKm[:, h * 128:(h + 1) * 128] if is_meta else KTK[:, t, h * 128:(h + 1) * 128]
                        kk = 'KTKm' if is_meta else ('KTK', t)
                        mm(ps[:, 0:129], ksrc, WV[:, t, d_, h * 129:(h + 1) * 129], t == 0, t == nt - 1,
                           [kk, ('WV', t, d_)], [pk])
                    if d_ == 0 and not is_meta:
                        stt(ST_[:, 0, h * 129:(h + 1) * 129], ST_[:, 0, h * 129:(h + 1) * 129], GT[:, 15, 4 + h:5 + h],
                            ps[:, 0:129], ALU.mult, ALU.add, ['Cf', 'EFL', pk], ['Cf'])
                    else:
                        key = 'Cf' if d_ == 0 else 'Bb'
                        tt('dve', ST_[:, d_, h * 129:(h + 1) * 129], ST_[:, d_, h * 129:(h + 1) * 129], ps[:, 0:129],
                           ALU.add, [key, pk], [key])
                    yield

        oo = o
        KTKm, oo = carve(oo, [128, 512], BF)
        VRm, oo = carve(oo, [128, 512], BF)
        GRm, oo = carve(oo, [128, 16], F32)
        assert oo <= 34 * 1024 + 3072

        def ntiles(step):
            return 1 if step == 0 else 4

        load_x(0)
        for t in range(ntiles(0)):
            N_tile(0, t)
        load_x(1)
        for step in range(nsteps):
            if step + 1 < nsteps:
                for t in range(ntiles(step + 1)):
                    pending.append(once(lambda st_=step + 1, t=t: N_tile(st_, t)))
                if step + 2 < nsteps:
                    pending.append(once(lambda st_=step + 2: load_x(st_, 'x')))
            A_step(step)
            while pending:
                pump()
        for _ in tail_gen(nsteps - 1):
            pass
        for _ in M_step(nsteps - 2):
            pass
        for _ in M_step(nsteps - 1):
            pass
        for _ in M_step(0, deferred_meta=True):
            pass
        if DEBUG and s == 1:
            dma(dbg['kn'].ap(), KN[s].ap(), [('KN', s)], ['dbg1'])
            dma(dbg['kr'].ap(), KR[s].ap(), [('KR', s)], ['dbg2'])
            dma(dbg['vs'].ap().rearrange("h p n c -> (h p) (n c)"), VS[s].ap().rearrange("h p n c -> (h p) (n c)"), [('VS', s)], ['dbg3'])
            dma(dbg['qt'].ap().rearrange("h r n -> (h r) n"), QT[s].ap().rearrange("h r n -> (h r) n"), [('QT', s)], ['dbg4'])
            dma(dbg['lq'].ap(), LQ[s].ap(), [('LQ', s)], ['dbg5'])
            dma(dbg['lk'].ap(), LK[s].ap(), [('LK', s)], ['dbg6'])
        if DEBUG:
            dma(dap(dbg['cf'], s * 128 * 516, [[516, 128], [1, 516]]), ST_[:, 0, :], ['Cf'], ['dbgcf'])
            dma(dap(dbg['bb'], s * 128 * 516, [[516, 128], [1, 516]]), ST_[:, 1, :], ['Bb'], ['dbgbb'])
        if stage < 2:
            continue
        S.barrier()
        mlstm_local(s)
        S.barrier()

    if stage >= 3:
        S.barrier()
        for s in range(NSEQ):
            attention(s)
    if stage >= 4:
        S.barrier()
        ffn_phases()

    S.emit(nc, st)
    st.close()
    return nc


def _rope_tables(pos):
    half = QK_ROPE // 2
    freqs = (10000.0 ** (-np.arange(half, dtype=np.float32) / half)).astype(np.float32)
    ang = pos.astype(np.float32)[None, :] * freqs[:, None]
    c = np.cos(ang).astype(np.float32)
    s_ = np.sin(ang).astype(np.float32)
    return np.concatenate([c, c], 0), np.concatenate([-s_, s_], 0)


def _consts():
    i = np.arange(128)
    ident = np.eye(128, dtype=np.float32)
    triU = (i[:, None] <= i[None, :]).astype(np.float32)
    triL = (i[:, None] >= i[None, :]).astype(np.float32)
    ones = np.ones((128, 128), np.float32)
    return np.concatenate([ident, triU, triL, ones, triU, triL], 1)


def prep_inputs(inp):
    f = lambda a: np.ascontiguousarray(np.asarray(a, dtype=np.float32))
    xs = [f(inp["x_prompt"])[0], f(inp["x_sample"])[0], f(inp["x_sample"])[1]]
    meta = f(inp["meta_tokens"])
    w_in = f(inp["w_in"])[0]; b_in = f(inp["b_in"])[0]
    o_cq, o_ckv, o_kr, o_mq, o_mk, o_mv, o_mo, o_g, o_ga, o_gb = np.cumsum([0, 256, 128, 32, 512, 512, 512, 512, 16, 1024])
    rot = np.concatenate([np.arange(16, 32), np.arange(0, 16)])
    cols_a = np.concatenate([np.arange(o_ckv, o_ckv + 128), o_kr + np.arange(32), o_kr + rot,
                             np.arange(o_mk, o_mk + 512), np.arange(o_mv, o_mv + 512), np.arange(o_g, o_g + 16)])
    cols_l = np.concatenate([np.arange(o_cq, o_cq + 256), np.arange(o_mq, o_mq + 512), np.arange(o_mo, o_mo + 512)])
    cols_g = np.arange(o_ga, o_ga + 2048)
    w_uq = f(inp["w_uq"])[0]
    cu = []
    for h in range(NH):
        b = h * QK_DIM
        cu += [b + np.arange(64), b + 64 + np.arange(32), b + 64 + rot]
    w_uq_e = np.ascontiguousarray(w_uq[:, np.concatenate(cu)])
    w_ukv = f(inp["w_ukv"])[0]
    ck = np.concatenate([h * 128 + np.arange(64) for h in range(NH)])
    cv = np.concatenate([h * 128 + 64 + np.arange(64) for h in range(NH)])
    w_ukv_e = np.ascontiguousarray(w_ukv[:, np.concatenate([ck, cv])])
    conv_w = f(inp["conv_w"])[0]; conv_b = f(inp["conv_b"])[0]
    cwt = np.zeros((128, 8, 3), np.float32); cbt = np.zeros((128, 8), np.float32)
    for h in range(4):
        cwt[:, h, :] = conv_w[:, 512 + h * 128:512 + (h + 1) * 128].T
        cwt[:, 4 + h, :] = conv_w[:, h * 128:(h + 1) * 128].T
        cbt[:, h] = conv_b[512 + h * 128:512 + (h + 1) * 128]
        cbt[:, 4 + h] = conv_b[h * 128:(h + 1) * 128]
    colmaj = lambda v, n: np.ascontiguousarray(f(v).reshape(n, 128).T)
    shared = {
        "cst": _consts(),
        "vmeta": ((np.arange(128) >= META_LO) & (np.arange(128) < META_HI)).astype(np.float32)[:, None].copy(),
        "w_a": np.ascontiguousarray(w_in[:, cols_a]), "b_a": np.ascontiguousarray(b_in[cols_a])[None],
        "w_l": np.ascontiguousarray(w_in[:, cols_l]), "b_l": np.ascontiguousarray(b_in[cols_l])[None],
        "w_g": np.ascontiguousarray(w_in[:, cols_g]), "b_g": np.ascontiguousarray(b_in[cols_g])[None],
        "w_uq": w_uq_e, "w_ukv": w_ukv_e,
        "w_pa": f(inp["w_pa"])[0], "w_pb": f(inp["w_pb"])[0], "w_o": f(inp["w_o"])[0],
        "w_gu": np.ascontiguousarray(np.concatenate([f(inp["w_ffn_gate"])[0], f(inp["w_ffn_up"])[0]], 1)),
        "w_dn": f(inp["w_ffn_down"])[0],
        "g1": colmaj(inp["norm1_g"], 8), "g2": colmaj(inp["norm2_g"], 8),
        "gq": colmaj(inp["q_norm_g"], 2), "gkv": colmaj(inp["kv_norm_g"], 1), "gm": colmaj(inp["m_norm_g"], 4),
        "gfin": f(inp["final_norm_g"])[None],
        "cw": cwt.reshape(128, 24), "cb": cbt,
    }
    in_maps = []
    for c in range(NCORE):
        m = dict(shared)
        for s in range(NSEQ):
            nsup, nl = NSUP[s], NLOC[s]
            L0 = c * nl
            order = [(L0 + i) % nsup for i in range(nsup)]
            x = xs[s]
            mt = np.zeros((128, D), np.float32)
            mt[0] = meta[15] if L0 == 0 else x[L0 * 512 - 1]
            mt[META_LO:META_HI] = meta
            mt[127] = x[0]
            xr = x.reshape(nsup, 512, D)[order].reshape(nsup * 512, D)
            m["xin%d" % s] = np.concatenate([mt, xr], 0)
            m["xloc%d" % s] = np.ascontiguousarray(x[L0 * 512:(L0 + nl) * 512])
            pos = np.zeros(TK(s), np.float32)
            pos[META_LO:META_HI] = np.arange(16)
            for i, su in enumerate(order):
                pos[128 + i * 512:128 + (i + 1) * 512] = 16 + su * 512 + np.arange(512)
            ct, sn = _rope_tables(pos)
            m["cos%d" % s] = ct; m["sin%d" % s] = sn
            mk = np.zeros((1 + nsup, 4), np.float32)
            mk[0] = [1, 0, 0, 0]
            for i, su in enumerate(order):
                st_ = i + 1
                before = su < L0
                after = su >= L0 + nl
                wl = 0.0 if su == 0 else 1.0
                wr = 0.0 if su == nsup - 1 else 1.0
                mk[st_] = [float(before), float(after), wl, wr]
            m["msk%d" % s] = np.ascontiguousarray(np.broadcast_to(mk.reshape(1, -1), (128, (1 + nsup) * 4)))
        in_maps.append(m)
    return in_maps


_NC_CACHE = {}


def kernel(**inputs):
    in_maps = prep_inputs(inputs)
    if 'nc' not in _NC_CACHE:
        _NC_CACHE['nc'] = build()
    nc = _NC_CACHE['nc']
    res = run_bass_kernel_spmd(nc, in_maps, core_ids=list(range(NCORE)))
    outs = []
    yp = np.concatenate([res.results[c]["y0"] for c in range(NCORE)], 0)[None]
    ys = np.stack([np.concatenate([res.results[c]["y%d" % s] for c in range(NCORE)], 0) for s in (1, 2)], 0)
    return (np.ascontiguousarray(yp.astype(np.float32)), np.ascontiguousarray(ys.astype(np.float32)))
```
